# Optimizing a Trainium2 kernel written in Bass

```python
import math
import jax
import jax.numpy as jnp
from jax import lax
import numpy as np

D_MODEL = 4096
BATCH = 4
SEQ = 2048
DEPTH = 4

HEAD_DIM = 128
FOX_WIDTH = D_MODEL // 4
FOX_HEADS = FOX_WIDTH // HEAD_DIM
FOX_BLOCK = 128
S5_WIDTH = D_MODEL // 4
S5_GROUP_CH = 16
S5_GROUPS = S5_WIDTH // S5_GROUP_CH
S5_STATE = 64
NSA_WIDTH = D_MODEL // 2
NSA_HEADS = NSA_WIDTH // HEAD_DIM
NSA_KV_HEADS = 4
NSA_REP = NSA_HEADS // NSA_KV_HEADS
NSA_KV_WIDTH = NSA_KV_HEADS * HEAD_DIM
CMP_BLOCK = 32
CMP_STRIDE = 16
SLC_BLOCK = 64
SLC_TOPN = 16
SLC_Q_CHUNK = 32
WINDOW = 512
WIN_BLOCK = 128
FORCE_BONUS = 1e3
MIX_WIDTH = FOX_WIDTH + S5_WIDTH + NSA_WIDTH
REL_BUCKETS = 32
REL_EXACT = 16
REL_MAX_DIST = 128
D_FF = 11008
CONV_WIDTH = 3
RMS_EPS = 1e-6
NEG_INF = -1e30
IN_SPLITS = [FOX_WIDTH, FOX_WIDTH, FOX_WIDTH, FOX_HEADS, S5_WIDTH, NSA_WIDTH,
             NSA_KV_WIDTH, NSA_KV_WIDTH, NSA_KV_WIDTH, NSA_KV_WIDTH, NSA_KV_WIDTH, NSA_KV_WIDTH,
             3 * NSA_HEADS]
N_IN = sum(IN_SPLITS)

kernel_name = 'hybrid_fox_s5_nsa_sandwich'


def rmsnorm(x, g):
    xf = x.astype(jnp.float32)
    y = xf * lax.rsqrt(jnp.mean(xf * xf, axis=-1, keepdims=True) + RMS_EPS)
    return (y * g.astype(jnp.float32)).astype(x.dtype)


def t5_bucket(dist):
    n = jnp.maximum(dist, 0)
    nf = jnp.maximum(n, 1).astype(jnp.float32)
    large = REL_EXACT + (jnp.log(nf / REL_EXACT) / math.log(REL_MAX_DIST / REL_EXACT)
                         * (REL_BUCKETS - REL_EXACT)).astype(jnp.int32)
    return jnp.where(n < REL_EXACT, n, jnp.minimum(large, REL_BUCKETS - 1))


def fox_mixer(q, k, v, z_f, b_f):
    B, S = q.shape[:2]
    q = q.reshape(B, S, FOX_HEADS, HEAD_DIM)
    k = k.reshape(B, S, FOX_HEADS, HEAD_DIM)
    v = v.reshape(B, S, FOX_HEADS, HEAD_DIM)
    log_f = jax.nn.log_sigmoid(z_f.astype(jnp.float32) + b_f.astype(jnp.float32))
    c_k = jnp.transpose(lax.cumsum(log_f, axis=1), (0, 2, 1))
    nb = S // FOX_BLOCK
    qb = jnp.moveaxis(q.reshape(B, nb, FOX_BLOCK, FOX_HEADS, HEAD_DIM), 1, 0)
    cb = jnp.moveaxis(c_k.reshape(B, FOX_HEADS, nb, FOX_BLOCK), 2, 0)
    starts = jnp.arange(nb, dtype=jnp.int32) * FOX_BLOCK
    kpos = jnp.arange(S, dtype=jnp.int32)
    scale = HEAD_DIM ** -0.5

    def block(args):
        qi, ci, st = args
        s = (jnp.einsum('bqhd,bkhd->bhqk', qi, k).astype(jnp.float32) * scale
             + ci[..., None] - c_k[:, :, None, :])
        causal = kpos[None, :] <= (st + jnp.arange(FOX_BLOCK, dtype=jnp.int32))[:, None]
        p = jax.nn.softmax(jnp.where(causal, s, NEG_INF), axis=-1)
        return jnp.einsum('bhqk,bkhd->bqhd', p.astype(v.dtype), v)

    o = lax.map(block, (qb, cb, starts))
    return jnp.moveaxis(o, 0, 1).reshape(B, S, FOX_WIDTH)


def _ssm_combine(e1, e2):
    a1r, a1i, b1r, b1i = e1
    a2r, a2i, b2r, b2i = e2
    return (a2r * a1r - a2i * a1i,
            a2r * a1i + a2i * a1r,
            a2r * b1r - a2i * b1i + b2r,
            a2r * b1i + a2i * b1r + b2i)


def s5_mixer(u, a_re, a_im, log_dt, b_re, b_im, c_re, c_im, d_skip, w_glu):
    B, S = u.shape[:2]
    f32 = jnp.float32
    uf = u.astype(f32).reshape(B, S, S5_GROUPS, S5_GROUP_CH)
    lam_re = jnp.minimum(a_re.astype(f32), -1e-4)
    lam_im = a_im.astype(f32)
    dt = jnp.exp(log_dt.astype(f32))[:, None]
    mag = jnp.exp(lam_re * dt)
    ang = lam_im * dt
    lb_re, lb_im = mag * jnp.cos(ang), mag * jnp.sin(ang)
    den = lam_re * lam_re + lam_im * lam_im
    nr, ni = lb_re - 1.0, lb_im
    coef_re = (nr * lam_re + ni * lam_im) / den
    coef_im = (ni * lam_re - nr * lam_im) / den
    b_re, b_im = b_re.astype(f32), b_im.astype(f32)
    bb_re = coef_re[..., None] * b_re - coef_im[..., None] * b_im
    bb_im = coef_re[..., None] * b_im + coef_im[..., None] * b_re
    bu_re = jnp.einsum('bsgh,gph->bsgp', uf, bb_re)
    bu_im = jnp.einsum('bsgh,gph->bsgp', uf, bb_im)
    a_re_t = jnp.broadcast_to(lb_re, bu_re.shape)
    a_im_t = jnp.broadcast_to(lb_im, bu_re.shape)
    _, _, h_re, h_im = lax.associative_scan(_ssm_combine, (a_re_t, a_im_t, bu_re, bu_im), axis=1)
    y = (jnp.einsum('ghp,bsgp->bsgh', c_re.astype(f32), h_re)
         - jnp.einsum('ghp,bsgp->bsgh', c_im.astype(f32), h_im)
         + d_skip.astype(f32).reshape(S5_GROUPS, S5_GROUP_CH) * uf)
    y = jax.nn.gelu(y.reshape(B, S, S5_WIDTH))
    return y * jax.nn.sigmoid(y @ w_glu.astype(f32))


def compress(kv, pe, w):
    B, S = kv.shape[:2]
    ch = kv.reshape(B, S // CMP_STRIDE, CMP_STRIDE, NSA_KV_HEADS, HEAD_DIM)
    blk = jnp.concatenate([ch[:, :-1], ch[:, 1:]], axis=2)
    return jnp.einsum('bnlgd,lde->bnge', blk + pe[:, None, :], w)


def nsa_mixer(q, k_c, v_c, k_s, v_s, k_w, v_w, z_gate, pe_k, pe_v, w_ck, w_cv, rel_bias):
    B, S = q.shape[:2]
    G, R, Dh = NSA_KV_HEADS, NSA_REP, HEAD_DIM
    scale = Dh ** -0.5
    q = q.reshape(B, S, G, R, Dh)
    kv_shape = (B, S, G, Dh)
    k_c, v_c, k_s, v_s, k_w, v_w = [t.reshape(kv_shape) for t in (k_c, v_c, k_s, v_s, k_w, v_w)]

    kc = compress(k_c, pe_k, w_ck)
    vc = compress(v_c, pe_v, w_cv)
    nc = kc.shape[1]
    blk_start = jnp.arange(nc, dtype=jnp.int32) * CMP_STRIDE
    blk_end = blk_start + CMP_BLOCK - 1
    ns = S // SLC_BLOCK
    n_sel = min(SLC_TOPN, ns)
    sel_start = jnp.arange(ns, dtype=jnp.int32) * SLC_BLOCK
    overlap = ((blk_start[:, None] <= sel_start[None, :] + SLC_BLOCK - 1)
               & (blk_end[:, None] >= sel_start[None, :])).astype(jnp.float32)
    kb = jnp.transpose(k_s.reshape(B, ns, SLC_BLOCK, G, Dh), (0, 3, 1, 2, 4))
    vb = jnp.transpose(v_s.reshape(B, ns, SLC_BLOCK, G, Dh), (0, 3, 1, 2, 4))
    bias_gr = rel_bias.reshape(REL_BUCKETS, G, R)
    bix = jnp.arange(B)[:, None, None, None]
    gix = jnp.arange(G)[None, None, :, None]
    jb = jnp.arange(ns, dtype=jnp.int32)

    def cmp_slc_chunk(args):
        qi, st = args
        tq = st + jnp.arange(SLC_Q_CHUNK, dtype=jnp.int32)
        s = jnp.einsum('bqgrd,bngd->bqgrn', qi, kc).astype(jnp.float32) * scale
        bias_c = rel_bias[t5_bucket(tq[:, None] - blk_end[None, :])]
        bias_c = jnp.transpose(bias_c.reshape(SLC_Q_CHUNK, nc, G, R), (0, 2, 3, 1))
        valid_c = (blk_end[None, :] <= tq[:, None])[:, None, None, :]
        p = jnp.where(valid_c, jax.nn.softmax(jnp.where(valid_c, s + bias_c, NEG_INF), axis=-1), 0.0)
        o_cmp = jnp.einsum('bqgrn,bngd->bqgrd', p.astype(vc.dtype), vc)
        imp = jnp.einsum('bqgn,nj->bqgj', p.sum(axis=3), overlap)
        cur = (tq // SLC_BLOCK)[:, None]
        causal_b = jb[None, :] * SLC_BLOCK <= tq[:, None]
        forced = ((jb[None, :] == 0) | (jb[None, :] == cur) | (jb[None, :] == cur - 1)).astype(jnp.float32)
        score = jnp.where(causal_b[None, :, None, :], imp + FORCE_BONUS * forced[None, :, None, :], NEG_INF)
        _, idx = lax.top_k(score, n_sel)
        kg = kb[bix, gix, idx]
        vg = vb[bix, gix, idx]
        kpos = idx[..., None] * SLC_BLOCK + jnp.arange(SLC_BLOCK, dtype=jnp.int32)
        dist = tq[None, :, None, None, None] - kpos
        bias_s = jnp.moveaxis(bias_gr[t5_bucket(dist), gix[..., None]], -1, 3)
        s2 = jnp.einsum('bqgrd,bqgnkd->bqgrnk', qi, kg).astype(jnp.float32) * scale + bias_s
        s2 = jnp.where((dist >= 0)[:, :, :, None], s2, NEG_INF)
        p2 = jax.nn.softmax(s2.reshape(s2.shape[:4] + (n_sel * SLC_BLOCK,)), axis=-1)
        o_slc = jnp.einsum('bqgrm,bqgmd->bqgrd', p2.astype(vg.dtype),
                           vg.reshape(vg.shape[:3] + (n_sel * SLC_BLOCK, Dh)))
        return o_cmp, o_slc

    nq = S // SLC_Q_CHUNK
    q_chunks = jnp.moveaxis(q.reshape(B, nq, SLC_Q_CHUNK, G, R, Dh), 1, 0)
    o_cmp, o_slc = lax.map(cmp_slc_chunk, (q_chunks, jnp.arange(nq, dtype=jnp.int32) * SLC_Q_CHUNK))
    o_cmp = jnp.moveaxis(o_cmp, 0, 1).reshape(B, S, G, R, Dh)
    o_slc = jnp.moveaxis(o_slc, 0, 1).reshape(B, S, G, R, Dh)

    kw_pad = jnp.pad(k_w, ((0, 0), (WINDOW, 0), (0, 0), (0, 0)))
    vw_pad = jnp.pad(v_w, ((0, 0), (WINDOW, 0), (0, 0), (0, 0)))
    span = WIN_BLOCK + WINDOW
    wdist = (jnp.arange(WIN_BLOCK, dtype=jnp.int32)[:, None]
             - jnp.arange(span, dtype=jnp.int32)[None, :] + WINDOW)
    band = (wdist >= 0) & (wdist < WINDOW)
    bias_w = jnp.transpose(rel_bias[t5_bucket(wdist)].reshape(WIN_BLOCK, span, G, R), (0, 2, 3, 1))

    def win_block(args):
        qi, st = args
        kw = lax.dynamic_slice_in_dim(kw_pad, st, span, axis=1)
        vw = lax.dynamic_slice_in_dim(vw_pad, st, span, axis=1)
        s = jnp.einsum('bqgrd,bkgd->bqgrk', qi, kw).astype(jnp.float32) * scale + bias_w
        keypos = st - WINDOW + jnp.arange(span, dtype=jnp.int32)
        mask = (band & (keypos >= 0)[None, :])[:, None, None, :]
        p = jax.nn.softmax(jnp.where(mask, s, NEG_INF), axis=-1)
        return jnp.einsum('bqgrk,bkgd->bqgrd', p.astype(vw.dtype), vw)

    nw = S // WIN_BLOCK
    o_win = lax.map(win_block, (jnp.moveaxis(q.reshape(B, nw, WIN_BLOCK, G, R, Dh), 1, 0),
                                jnp.arange(nw, dtype=jnp.int32) * WIN_BLOCK))
    o_win = jnp.moveaxis(o_win, 0, 1).reshape(B, S, G, R, Dh)

    gates = jax.nn.sigmoid(z_gate.astype(jnp.float32)).reshape(B, S, G, R, 3)
    o = gates[..., 0:1] * o_cmp + gates[..., 1:2] * o_slc + gates[..., 2:3] * o_win
    return o.reshape(B, S, NSA_WIDTH)


def conv_ffn(h, w_up, conv_w, conv_b, w_down):
    S = h.shape[1]
    u = h @ w_up
    up = jnp.pad(u, ((0, 0), (CONV_WIDTH - 1, 0), (0, 0)))
    y = conv_b
    for tap in range(CONV_WIDTH):
        y = y + conv_w[tap] * up[:, tap:tap + S]
    gate, val = jnp.split(y, 2, axis=-1)
    return (jax.nn.gelu(gate, approximate=True) * val) @ w_down


def setup_inputs(seed: int = 0) -> dict:
    key = jax.random.key(seed)
    ks = jax.random.split(key, 32)

    def nrm(k, shape, scale):
        return jax.random.normal(k, shape, jnp.float32) * scale

    def gain(k, shape):
        return 1.0 + nrm(k, shape, 0.05)

    L, G, P, H = DEPTH, S5_GROUPS, S5_STATE, S5_GROUP_CH
    n_idx = jnp.arange(P, dtype=jnp.float32)
    return {
        'x': nrm(ks[0], (BATCH, SEQ, D_MODEL), 1.0),
        'w_in': nrm(ks[1], (L, D_MODEL, N_IN), D_MODEL ** -0.5),
        'b_forget': jax.random.uniform(ks[2], (L, FOX_HEADS), jnp.float32, 1.0, 4.0),
        's5_a_re': -0.5 + nrm(ks[3], (L, G, P), 0.01),
        's5_a_im': math.pi * n_idx + nrm(ks[4], (L, G, P), 0.01),
        's5_log_dt': jax.random.uniform(ks[5], (L, G), jnp.float32, math.log(1e-3), math.log(1e-1)),
        's5_b_re': nrm(ks[6], (L, G, P, H), (2 * H) ** -0.5),
        's5_b_im': nrm(ks[7], (L, G, P, H), (2 * H) ** -0.5),
        's5_c_re': nrm(ks[8], (L, G, H, P), 1.0),
        's5_c_im': nrm(ks[9], (L, G, H, P), 1.0),
        's5_d': nrm(ks[10], (L, S5_WIDTH), 0.5),
        's5_w_glu': nrm(ks[11], (L, S5_WIDTH, S5_WIDTH), S5_WIDTH ** -0.5),
        'cmp_pe_k': nrm(ks[12], (L, CMP_BLOCK, HEAD_DIM), 0.5),
        'cmp_pe_v': nrm(ks[13], (L, CMP_BLOCK, HEAD_DIM), 0.5),
        'cmp_w_k': nrm(ks[14], (L, CMP_BLOCK, HEAD_DIM, HEAD_DIM), (CMP_BLOCK * HEAD_DIM) ** -0.5),
        'cmp_w_v': nrm(ks[15], (L, CMP_BLOCK, HEAD_DIM, HEAD_DIM), (CMP_BLOCK * HEAD_DIM) ** -0.5),
        'rel_bias': nrm(ks[16], (REL_BUCKETS, NSA_HEADS), 0.5),
        'g_out_fox': gain(ks[17], (L, FOX_WIDTH)),
        'g_out_s5': gain(ks[18], (L, S5_WIDTH)),
        'g_out_nsa': gain(ks[19], (L, NSA_WIDTH)),
        'w_out': nrm(ks[20], (L, MIX_WIDTH, D_MODEL), MIX_WIDTH ** -0.5),
        'g_pre_mix': gain(ks[21], (L, D_MODEL)),
        'g_post_mix': gain(ks[22], (L, D_MODEL)),
        'g_pre_ffn': gain(ks[23], (L, D_MODEL)),
        'g_post_ffn': gain(ks[24], (L, D_MODEL)),
        'w_up': nrm(ks[25], (L, D_MODEL, 2 * D_FF), D_MODEL ** -0.5),
        'conv_w': nrm(ks[26], (L, CONV_WIDTH, 2 * D_FF), CONV_WIDTH ** -0.5),
        'conv_b': nrm(ks[27], (L, 2 * D_FF), 0.02),
        'w_down': nrm(ks[28], (L, D_FF, D_MODEL), D_FF ** -0.5),
    }


def reference(x, w_in, b_forget, s5_a_re, s5_a_im, s5_log_dt, s5_b_re, s5_b_im, s5_c_re, s5_c_im,
              s5_d, s5_w_glu, cmp_pe_k, cmp_pe_v, cmp_w_k, cmp_w_v, rel_bias, g_out_fox, g_out_s5,
              g_out_nsa, w_out, g_pre_mix, g_post_mix, g_pre_ffn, g_post_ffn, w_up, conv_w, conv_b,
              w_down):
    split_points = np.cumsum(IN_SPLITS)[:-1].tolist()
    for l in range(DEPTH):
        h = rmsnorm(x, g_pre_mix[l])
        z = h @ w_in[l]
        (q_a, k_a, v_a, z_f, u_b, q_c, k_cc, v_cc, k_cs, v_cs, k_cw, v_cw,
         z_g) = jnp.split(z, split_points, axis=-1)
        o_a = fox_mixer(q_a, k_a, v_a, z_f, b_forget[l])
        o_b = s5_mixer(u_b, s5_a_re[l], s5_a_im[l], s5_log_dt[l], s5_b_re[l], s5_b_im[l],
                       s5_c_re[l], s5_c_im[l], s5_d[l], s5_w_glu[l])
        o_c = nsa_mixer(q_c, k_cc, v_cc, k_cs, v_cs, k_cw, v_cw, z_g, cmp_pe_k[l], cmp_pe_v[l],
                        cmp_w_k[l], cmp_w_v[l], rel_bias)
        mixed = jnp.concatenate([rmsnorm(o_a.astype(x.dtype), g_out_fox[l]),
                                 rmsnorm(o_b.astype(x.dtype), g_out_s5[l]),
                                 rmsnorm(o_c.astype(x.dtype), g_out_nsa[l])], axis=-1) @ w_out[l]
        x = x + rmsnorm(mixed, g_post_mix[l])
        f = conv_ffn(rmsnorm(x, g_pre_ffn[l]), w_up[l], conv_w[l], conv_b[l], w_down[l])
        x = x + rmsnorm(f.astype(x.dtype), g_post_ffn[l])
    return x
```

```python
import math


import numpy as np
from contextlib import ExitStack
import concourse.bass as bass
import concourse.mybir as mybir
from concourse.bass_utils import run_bass_kernel_spmd

F32 = mybir.dt.float32
BF16 = mybir.dt.bfloat16
I32 = mybir.dt.int32
AF = mybir.ActivationFunctionType
ALU = mybir.AluOpType
AX = mybir.AxisListType

ENGS = ("pe", "act", "dve", "pool", "sp")
ND = 8


class Buf:
    __slots__ = ("name", "writers", "readers")

    def __init__(self, name=""):
        self.name = name
        self.writers = {}
        self.readers = {}


class Prog:
    def __init__(self, nc, stack):
        self.nc = nc
        self.stack = stack
        self.ops = {e: [] for e in ENGS}
        self.cnt = {e: 0 for e in ENGS}
        self.seen = {e: {} for e in ENGS}
        self.esem = {e: stack.enter_context(nc.semaphore("c_" + e)) for e in ENGS}
        self.dsem = {e: [stack.enter_context(nc.semaphore("d_%s%d" % (e, i))) for i in range(ND)]
                     for e in ("sp", "pool", "act")}
        self.dcnt = {e: [0] * ND for e in self.dsem}
        self.dnext = {e: 0 for e in self.dsem}
        self.semname = {}
        self.nbuf = 0

    def sb(self, name, shape, dtype):
        t = self.stack.enter_context(self.nc.sbuf_tensor(name, list(shape), dtype))
        return t

    def ps(self, name, shape, dtype=F32):
        t = self.stack.enter_context(self.nc.psum_tensor(name, list(shape), dtype))
        return t

    def dram(self, name, shape, dtype, kind="Internal"):
        return self.nc.dram_tensor(name, list(shape), dtype, kind=kind).ap()

    def _collect(self, eng, reads, writes, same_ok=False):
        need = {}

        def add(sem, val):
            if need.get(id(sem), (None, -1))[1] < val:
                need[id(sem)] = (sem, val)

        for b in reads:
            for sem, val in b.writers.values():
                add(sem, val)
        for b in writes:
            for sem, val in b.writers.values():
                add(sem, val)
            for sem, val in b.readers.values():
                add(sem, val)
        waits = []
        own = self.esem[eng]
        for key, (sem, val) in need.items():
            if same_ok and sem is own:
                continue
            if self.seen[eng].get(key, -1) >= val:
                continue
            self.seen[eng][key] = val
            waits.append((sem, val))
        return waits

    def _commit(self, ev, reads, writes):
        sem, val = ev
        for b in reads:
            b.readers[id(sem)] = (sem, val)
        for b in writes:
            b.writers = {id(sem): (sem, val)}
            b.readers = {}

    def op(self, eng, fn, reads=(), writes=()):
        waits = self._collect(eng, reads, writes, same_ok=(eng == "pe"))
        self.cnt[eng] += 1
        ev = (self.esem[eng], self.cnt[eng])
        self.ops[eng].append((fn, waits, (self.esem[eng], 1)))
        self._commit(ev, reads, writes)

    def dma(self, eng, out, in_, reads=(), writes=(), **kw):
        slot = self.dnext[eng]
        self.dnext[eng] = (slot + 1) % ND
        sem = self.dsem[eng][slot]
        waits = self._collect(eng, reads, writes)
        prev = self.dcnt[eng][slot] * 16
        if prev > 0 and self.seen[eng].get(id(sem), -1) < prev:
            self.seen[eng][id(sem)] = prev
            waits.append((sem, prev))
        self.dcnt[eng][slot] += 1
        ev = (sem, prev + 16)

        def fn(e, out=out, in_=in_, kw=kw):
            return e.dma_start(out=out, in_=in_, **kw)

        self.ops[eng].append((fn, waits, (sem, 16)))
        self._commit(ev, reads, writes)

    def finish_wait_all(self, eng, bufs):
        waits = self._collect(eng, bufs, ())
        self.ops[eng].append((None, waits, None))

    def emit(self):
        nc = self.nc
        with nc.Block() as block:
            def run(e, name):
                for fn, waits, inc in self.ops[name]:
                    for sem, val in waits:
                        e.wait_ge(sem, val)
                    if fn is None:
                        continue
                    ins = fn(e)
                    if inc is not None:
                        ins.then_inc(inc[0], inc[1])

            @block.tensor
            def _(e):
                run(e, "pe")

            @block.scalar
            def _(e):
                run(e, "act")

            @block.vector
            def _(e):
                run(e, "dve")

            @block.gpsimd
            def _(e):
                run(e, "pool")

            @block.sync
            def _(e):
                run(e, "sp")


SCALE = 128 ** -0.5
BIG = 4096.0
PAD = 2048
BROW = 4096

def t5_bucket_np(dist):
    n = np.maximum(dist, 0)
    nf = np.maximum(n, 1).astype(np.float32)
    large = 16 + (np.log(nf / np.float32(16)) / np.float32(math.log(128 / 16)) * np.float32(16)).astype(np.int32)
    return np.where(n < 16, n, np.minimum(large, 31))

def make_consts():
    c = {}
    c["ident"] = np.eye(128, dtype=np.float32)
    d = np.arange(BROW) - PAD
    bk = t5_bucket_np(d)
    oh = np.zeros((32, BROW), np.float32)
    oh[bk, np.arange(BROW)] = 1.0
    c["onehot"] = oh
    m = np.zeros((17, 128, 512), np.float32)
    s = np.arange(128)[:, None]; q = np.arange(512)[None, :]
    for k, r in enumerate(range(-4, 4)):
        dist = q - s - 128 * r
        m[k] = np.where((dist >= 0) & (dist < 512), 0.0, -BIG)
    for j in range(4):
        ok = (16 * s + 31) <= (512 * j + q)
        m[9 + j] = np.where(ok, 0.0, -BIG)
        m[9 + j][127] = -BIG
    for r in range(4):
        m[13 + r] = np.where(q - s - 128 * r >= 0, 0.0, -BIG)
    for k in range(9):
        m[k] = m[k][::-1].copy()
    for k in range(9, 13):
        m[k][:127] = m[k][:127][::-1].copy()
    c["masks"] = m
    c["jrev"] = np.eye(128, dtype=np.float32)[::-1].copy()
    j127 = np.zeros((128, 128), np.float32); j127[:127, :127] = np.eye(127, dtype=np.float32)[::-1]
    c["jrev127"] = j127
    bs = np.arange(127) * 16; be = bs + 31
    ss = np.arange(32) * 64
    ov = ((bs[:, None] <= ss[None, :] + 63) & (be[:, None] >= ss[None, :])).astype(np.float32)
    ovp = np.zeros((128, 32), np.float32); ovp[:127] = ov
    c["overlap"] = ovp
    E = np.zeros((32, 2048), np.float32)
    E[np.arange(2048) // 64, np.arange(2048)] = 1.0
    c["expand"] = E
    tq = np.arange(2048)[:, None]; jb = np.arange(32)[None, :]
    cur = tq // 64
    causal = jb * 64 <= tq
    forced = ((jb == 0) | (jb == cur) | (jb == cur - 1)).astype(np.float32)
    c["scorec"] = np.where(causal, 1000.0 * forced, -1e30).astype(np.float32)
    mB = np.zeros((128, 4, 128), np.float32)
    for gl in range(8):
        for g2 in range(2):
            for pr4 in range(4):
                if gl == 2 * pr4 + g2:
                    mB[16 * gl:16 * gl + 16, pr4, 64 * g2:64 * g2 + 64] = 1.0
    c["maskB"] = mB
    c["maskC"] = np.ascontiguousarray(mB.transpose(2, 1, 0))
    c["iota"] = np.arange(1024, dtype=np.float32)[None, :]
    return c


def barrier(P):
    evs = []
    for e in ENGS:
        if P.cnt[e] > 0:
            evs.append((P.esem[e], P.cnt[e]))
    for e in P.dsem:
        for i in range(ND):
            if P.dcnt[e][i] > 0:
                evs.append((P.dsem[e][i], P.dcnt[e][i] * 16))
    for e in ENGS:
        waits = []
        for sem, val in evs:
            if sem is P.esem[e]:
                continue
            if P.seen[e].get(id(sem), -1) >= val:
                continue
            P.seen[e][id(sem)] = val
            waits.append((sem, val))
        if waits:
            P.ops[e].append((None, waits, None))

def gemm(P, tag, actT, K, T, W, N, outT, out_dtype, TB=1024, PW=512, epi=None, Wpan=None, b_wpan=None):
    nc = P.nc
    KC = K // 128
    assert K % 128 == 0 and T % TB == 0 and TB % 512 == 0
    NTS = TB // 512
    with ExitStack() as st:
        old = P.stack; P.stack = st
        act = P.sb(tag + "_act", [128, KC, TB], BF16)
        wts = [P.sb(tag + "_w%d" % i, [128, KC, PW], BF16) for i in range(2)]
        osb = [P.sb(tag + "_o%d" % i, [128, TB], out_dtype) for i in range(2)]
        pss = [P.ps(tag + "_p%d" % i, [128, TB], F32) for i in range(2)]
        b_act = Buf(); b_w = [Buf(), Buf()]; b_o = [Buf(), Buf()]; b_p = [Buf(), Buf()]
        actv = actT.rearrange("(c p) t -> p c t", p=128)
        Wv = W.rearrange("(c p) n -> p c n", p=128) if W is not None else None
        it = 0
        npan = (N + PW - 1) // PW
        KG = 8
        for tb in range(T // TB):
            t0 = tb * TB
            for k0 in range(0, KC, KG):
                k1 = min(KC, k0 + KG)
                P.dma("sp", act[:, k0:k1, :], actv[:, k0:k1, t0:t0 + TB], writes=[b_act])
            for pn in range(npan):
                n0 = pn * PW
                pw = min(PW, N - n0)
                wb = pn % 2
                if Wpan is not None:
                    for k0 in range(0, KC, 32):
                        k1 = min(KC, k0 + 32)
                        P.dma("act", wts[wb][:, k0:k1, :pw], Wpan[pn, :, k0:k1, :pw], reads=[b_wpan], writes=[b_w[wb]])
                else:
                    for k0 in range(0, KC, KG):
                        k1 = min(KC, k0 + KG)
                        P.dma("pool", wts[wb][:, k0:k1, :pw], Wv[:, k0:k1, n0:n0 + pw], writes=[b_w[wb]])
                for c0 in range(0, pw, 128):
                    m = min(128, pw - c0)
                    pb = it % 2; it += 1
                    def mm(e, wb=wb, c0=c0, m=m, pb=pb):
                        ins = None
                        for ts in range(NTS):
                            for k in range(KC):
                                ins = e.matmul(pss[pb][:m, ts * 512:(ts + 1) * 512], lhsT=wts[wb][:, k, c0:c0 + m],
                                               rhs=act[:, k, ts * 512:(ts + 1) * 512], start=(k == 0), stop=(k == KC - 1))
                        return ins
                    P.op("pe", mm, reads=[b_act, b_w[wb]], writes=[b_p[pb]])
                    ob = pb
                    if epi is None:
                        if pb == 0:
                            P.op("act", lambda e, m=m, pb=pb, ob=ob: e.copy(out=osb[ob][:m, :], in_=pss[pb][:m, :]),
                                 reads=[b_p[pb]], writes=[b_o[ob]])
                        else:
                            P.op("dve", lambda e, m=m, pb=pb, ob=ob: e.tensor_copy(out=osb[ob][:m, :], in_=pss[pb][:m, :]),
                                 reads=[b_p[pb]], writes=[b_o[ob]])
                    else:
                        epi(P, pss[pb], b_p[pb], osb[ob], b_o[ob], n0 + c0, m, t0, TB)
                    P.dma("sp", outT[n0 + c0:n0 + c0 + m, t0:t0 + TB], osb[ob][:m, :], reads=[b_o[ob]])
        P.stack = old
    barrier(P)


def dsl(j, n=512):
    return slice(j * n, (j + 1) * n)

class AttnCtx:
    pass

def setup_bias(P, relb, cst, scr):
    nc = P.nc
    with ExitStack() as st:
        old = P.stack; P.stack = st
        rb = P.sb("sb_rb", [32, 16], F32); oh = P.sb("sb_oh", [32, BROW], F32)
        brow = P.sb("sb_brow", [16, BROW], F32)
        msk = P.sb("sb_msk", [128, 13, 512], F32)
        toe = [P.sb("sb_toe%d" % i, [128, 512], F32) for i in range(2)]
        bmo = [P.sb("sb_bmo%d" % i, [128, 512], BF16) for i in range(2)]
        ps = P.ps("sb_ps", [16, 512], F32)
        b_rb, b_oh, b_brow, b_msk, b_ps = Buf(), Buf(), Buf(), Buf(), Buf()
        b_toe = [Buf(), Buf()]; b_bmo = [Buf(), Buf()]
        P.dma("sp", rb[:, :], relb[:, :], writes=[b_rb])
        P.dma("sp", oh[:, :], cst["onehot"][:, :], writes=[b_oh])
        for k0 in range(0, 13, 4):
            k1 = min(13, k0 + 4)
            P.dma("sp", msk[:, k0:k1, :], cst["masks"][k0:k1].rearrange("k p q -> p k q"), writes=[b_msk])
        for c in range(BROW // 512):
            P.op("pe", lambda e, c=c: e.matmul(ps[:, :], lhsT=rb[:, :], rhs=oh[:, dsl(c)], start=True, stop=True),
                 reads=[b_rb, b_oh], writes=[b_ps])
            P.op("act", lambda e, c=c:
                 e.activation(out=brow[:, dsl(c)], in_=ps[:, :], func=AF.Copy, scale=1.0 / SCALE),
                 reads=[b_ps], writes=[b_brow])
        b_browD = scr["b_browS"]
        P.dma("sp", scr["browS"][:, :], brow[:, :], reads=[b_brow], writes=[b_browD])
        it = 0
        bt = scr["browS"].tensor
        for h in range(16):
            for k in range(13):
                i = it % 2; it += 1
                if k < 8:
                    off = h * BROW + PAD - 127 - 128 * (k - 4); apat = [[1, 128], [1, 512]]; np_ = 128
                elif k == 8:
                    off = h * BROW + PAD - 127 + 128; apat = [[1, 128], [1, 512]]; np_ = 128
                else:
                    off = h * BROW + PAD + 512 * (k - 9) - 2047; apat = [[16, 127], [1, 512]]; np_ = 127
                src = bass.AP(bt, off, apat)
                P.dma("sp", toe[i][:np_, :], src, reads=[b_browD], writes=[b_toe[i]], allow_slow_non_contiguous=False)
                P.op("dve", lambda e, i=i, k=k, np_=np_: e.tensor_tensor(out=bmo[i][:np_, :], in0=toe[i][:np_, :], in1=msk[:np_, k, :], op=ALU.add),
                     reads=[b_toe[i], b_msk], writes=[b_bmo[i]])
                P.dma("sp", scr["BM"][h, k, :np_, :], bmo[i][:np_, :], reads=[b_bmo[i]], writes=[scr["b_BM"]])
        P.stack = old
    barrier(P)


def attn_consts(P, cst):
    C = AttnCtx()
    C.identf = P.sb("c_identf", [128, 128], F32)
    C.identb = P.sb("c_identb", [128, 128], BF16)
    C.onesb = P.sb("c_onesb", [128, 128], BF16)
    C.onesf = P.sb("c_onesf", [128, 128], F32)
    C.ov = P.sb("c_ov", [128, 32], F32)
    C.Eb = P.sb("c_E", [32, 2048], BF16)
    C.scorec = P.sb("c_scorec", [128, 16, 32], F32)
    C.foxm = P.sb("c_foxm", [128, 4, 512], BF16)
    C.b = Buf()
    C.jrevb = P.sb("c_jrevb", [128, 128], BF16)
    C.jrev127b = P.sb("c_jrev127b", [128, 128], BF16)
    P.dma("pool", C.jrevb[:, :], cst["jrev"][:, :], writes=[C.b])
    P.dma("pool", C.jrev127b[:, :], cst["jrev127"][:, :], writes=[C.b])
    P.dma("sp", C.identf[:, :], cst["ident"][:, :], writes=[C.b])
    P.dma("pool", C.identb[:, :], cst["ident"][:, :], writes=[C.b])
    P.dma("pool", C.Eb[:, :], cst["expand"][:, :], writes=[C.b])
    P.dma("sp", C.ov[:, :], cst["overlap"][:, :], writes=[C.b])
    P.dma("sp", C.scorec[:, :, :], cst["scorec"].rearrange("(c p) j -> p c j", p=128), writes=[C.b])
    P.dma("pool", C.foxm[:, :, :], cst["masks"][13:17].rearrange("k p q -> p k q"), writes=[C.b])
    P.op("dve", lambda e: e.memset(C.onesb[:, :], 1.0), writes=[C.b])
    P.op("dve", lambda e: e.memset(C.onesf[:, :], 1.0), writes=[C.b])
    return C


class AttnRes:
    def __init__(self, P, tag):
        self.ps_s = [P.ps(tag + "_pss%d" % i, [128, 512], F32) for i in range(3)]
        self.ps_o = [P.ps(tag + "_pso%d" % i, [128, 512], F32) for i in range(2)]
        self.ps_d = [P.ps(tag + "_psd%d" % i, [128, 512], F32) for i in range(2)]
        self.ps_m = P.ps(tag + "_psm", [128, 512], F32)
        self.ps_t = self.ps_m[:, 0:256].bitcast(BF16)
        self.b_s = [Buf(), Buf(), Buf()]; self.b_o = [Buf(), Buf()]; self.b_d = [Buf(), Buf()]
        self.b_m = Buf(); self.b_t = self.b_m
        self.pt = [P.sb(tag + "_pt%d" % i, [128, 512], BF16) for i in range(4)]
        self.b_pt = [Buf() for _ in range(4)]
        self.tmp = [P.sb(tag + "_tmp%d" % i, [128, 512], F32) for i in range(2)]
        self.b_tmp = [Buf(), Buf()]
        self.rd = [P.sb(tag + "_rd%d" % i, [128, 512], F32) for i in range(2)]
        self.b_rd = [Buf(), Buf()]
        self.t1 = [P.sb(tag + "_t1%d" % i, [128, 512], F32) for i in range(2)]
        self.b_t1 = [Buf(), Buf()]
        self.si = 0; self.pi = 0; self.oi = 0; self.ti = 0


def attn_qblock(P, C, R, QT, b_q, j, tiles, epilogue):
    oi = R.oi % 2; R.oi += 1
    n = len(tiles)
    pend = []

    def emit_s(t):
        si = R.si % 3; R.si += 1
        m = t["m"]
        ex = t.get("extra", [])
        def f(e, t=t, si=si, m=m, ex=ex):
            ins = e.matmul(R.ps_s[si][:m, :], lhsT=t["KT"], rhs=QT[:, dsl(j)], start=True, stop=(len(ex) == 0))
            for q, (l, r, _) in enumerate(ex):
                ins = e.matmul(R.ps_s[si][:m, :], lhsT=l, rhs=r, start=False, stop=(q == len(ex) - 1))
            return ins
        rd = [b_q, t["bK"]] + [b for (_, _, bs) in ex for b in bs]
        P.op("pe", f, reads=rd, writes=[R.b_s[si]])
        pi = R.pi % 4; R.pi += 1
        bias = t.get("bias")
        if t.get("fox") is not None:
            cbc, bcbc = t["fox"]
            ti = R.ti % 2; R.ti += 1
            P.op("dve", lambda e, si=si, ti=ti, m=m, cbc=cbc: e.scalar_tensor_tensor(
                out=R.tmp[ti][:m, :], in0=R.ps_s[si][:m, :], scalar=SCALE, in1=cbc[:m, dsl(j)], op0=ALU.mult, op1=ALU.subtract),
                reads=[R.b_s[si], bcbc], writes=[R.b_tmp[ti]])
            P.op("act", lambda e, ti=ti, pi=pi, m=m, bias=bias: e.activation(
                out=R.pt[pi][:m, :], in_=R.tmp[ti][:m, :], func=AF.Exp, bias=bias, scale=1.0),
                reads=[R.b_tmp[ti], t["bbias"]], writes=[R.b_pt[pi]])
        else:
            kw = {}
            rds = [R.b_s[si]]
            if bias is not None:
                kw["bias"] = bias; rds.append(t["bbias"])
            P.op("act", lambda e, si=si, pi=pi, m=m, kw=kw: e.activation(
                out=R.pt[pi][:m, :], in_=R.ps_s[si][:m, :], func=AF.Exp, scale=SCALE, **kw),
                reads=rds, writes=[R.b_pt[pi]])
        return pi

    def emit_pv(t, pi, first, last):
        m = t["m"]
        def f(e, t=t, pi=pi, m=m):
            e.matmul(R.ps_o[oi][:, :], lhsT=t["V"], rhs=R.pt[pi][:m, :], start=first, stop=last)
            return e.matmul(R.ps_d[oi][:, :], lhsT=C.onesb[:m, :], rhs=R.pt[pi][:m, :], start=first, stop=last)
        P.op("pe", f, reads=[t["bV"], R.b_pt[pi], C.b], writes=[R.b_o[oi], R.b_d[oi]])

    pis = []
    LA = 2
    done = 0
    for q, t in enumerate(tiles):
        pis.append(emit_s(t))
        if q >= LA:
            emit_pv(tiles[done], pis[done], done == 0, done == n - 1)
            done += 1
    while done < n:
        emit_pv(tiles[done], pis[done], done == 0, done == n - 1)
        done += 1
    epilogue(oi, pis)


def load_rows_bf16(P, dst, zT, r0, buf):
    P.dma("pool", dst[:, :], zT[r0:r0 + 128, :], writes=[buf])


def make_V(P, C, R, VT, b_vt, V, b_v, nchunks=16):
    for g4 in range(0, nchunks, 4):
        def f(e, g4=g4):
            ins = None
            for q in range(4):
                ins = e.transpose(out=R.ps_t[:, q * 128:(q + 1) * 128], in_=VT[:, (g4 + q) * 128:(g4 + q + 1) * 128], identity=C.identb[:, :])
            return ins
        P.op("pe", f, reads=[b_vt, C.b], writes=[R.b_t])
        P.op("dve", lambda e, g4=g4: e.tensor_copy(out=V[:, g4:g4 + 4, :], in_=R.ps_t[:, :].rearrange("p (q d) -> p q d", q=4)),
             reads=[R.b_t], writes=[b_v])


def fox_phase(P, C, tag, zT, mixT, b_mix, bfor, scr):
    nc = P.nc
    with ExitStack() as st:
        old = P.stack; P.stack = st
        R = AttnRes(P, tag)
        zf = P.sb(tag + "_zf", [8, 2048], F32); ee = P.sb(tag + "_ee", [8, 2048], F32)
        ll = P.sb(tag + "_ll", [8, 2048], F32); cp = P.sb(tag + "_cp", [8, 2048], F32)
        on8 = P.sb(tag + "_on8", [8, 2048], F32)
        bneg = P.sb(tag + "_bn", [8, 1], F32); bf = P.sb(tag + "_bf", [8, 1], F32)
        ccol = P.sb(tag + "_ccol", [128, 8, 16], F32)
        cbc = [P.sb(tag + "_cbc%d" % i, [128, 2048], F32) for i in range(2)]
        QT = [P.sb(tag + "_q%d" % i, [128, 2048], BF16) for i in range(2)]
        KT = [P.sb(tag + "_k%d" % i, [128, 2048], BF16) for i in range(2)]
        VT = [P.sb(tag + "_vt%d" % i, [128, 2048], BF16) for i in range(2)]
        V = [P.sb(tag + "_v%d" % i, [128, 16, 128], BF16) for i in range(2)]
        ob = [P.sb(tag + "_ob%d" % i, [128, 512], F32) for i in range(2)]
        b_ob = [Buf(), Buf()]
        b_zf, b_e, b_l, b_cp, b_on, b_bn, b_bf, b_ccol = [Buf() for _ in range(8)]
        b_cbc = [Buf(), Buf()]; b_q = [Buf(), Buf()]; b_k = [Buf(), Buf()]; b_vt = [Buf(), Buf()]; b_v = [Buf(), Buf()]
        b_cD = scr["b_cD"]
        P.dma("sp", zf[:, :], zT[3072:3080, :], writes=[b_zf])
        P.dma("sp", bf[:, :], bfor.rearrange("(h o) -> h o", o=1), writes=[b_bf])
        P.op("dve", lambda e: e.memset(on8[:, :], 1.0), writes=[b_on])
        P.op("act", lambda e: e.mul(out=bneg[:, :], in_=bf[:, :], mul=-1.0), reads=[b_bf], writes=[b_bn])
        P.op("act", lambda e: e.activation(out=ee[:, :], in_=zf[:, :], func=AF.Exp, bias=bneg[:, :], scale=-1.0),
             reads=[b_zf, b_bn], writes=[b_e])
        P.op("act", lambda e: e.activation(out=ll[:, :], in_=ee[:, :], func=AF.Ln, bias=1.0, scale=1.0),
             reads=[b_e], writes=[b_l])
        P.op("dve", lambda e: e.tensor_tensor_scan(out=cp[:, :], data0=on8[:, :], data1=ll[:, :], initial=0.0, op0=ALU.mult, op1=ALU.add),
             reads=[b_on, b_l], writes=[b_cp])
        P.dma("sp", scr["cD"][:, :], cp[:, :], reads=[b_cp], writes=[b_cD])
        P.dma("sp", ccol[:, :, :], scr["cD"].rearrange("h (c p) -> p h c", p=128), reads=[b_cD], writes=[b_ccol], allow_slow_non_contiguous=True)
        oc = 0
        for h in range(8):
            i = h % 2
            P.dma("sp", cbc[i][:, :], scr["cD"][h:h + 1, :].partition_broadcast(128), reads=[b_cD], writes=[b_cbc[i]])
            load_rows_bf16(P, QT[i], zT, 128 * h, b_q[i])
            load_rows_bf16(P, KT[i], zT, 1024 + 128 * h, b_k[i])
            load_rows_bf16(P, VT[i], zT, 2048 + 128 * h, b_vt[i])
            make_V(P, C, R, VT[i], b_vt[i], V[i], b_v[i])
            for j in range(4):
                tiles = []
                for kc in range(4 * j + 4):
                    t = dict(KT=KT[i][:, kc * 128:(kc + 1) * 128], bK=b_k[i], V=V[i][:, kc, :], bV=b_v[i], m=128,
                             fox=(cbc[i], b_cbc[i]), bias=ccol[:, h, kc:kc + 1], bbias=b_ccol)
                    r = kc - 4 * j
                    if r >= 0:
                        t["extra"] = [(C.identb[:, :], C.foxm[:, r, :], [C.b])]
                    tiles.append(t)
                def epi(oi, pis, h=h, j=j):
                    nonlocal oc
                    o = oc % 2; oc += 1
                    P.op("dve", lambda e, oi=oi, o=o: e.reciprocal(out=R.rd[o][:, :], in_=R.ps_d[oi][:, :]), reads=[R.b_d[oi]], writes=[R.b_rd[o]])
                    P.op("dve", lambda e, oi=oi, o=o: e.tensor_tensor(out=ob[o][:, :], in0=R.ps_o[oi][:, :], in1=R.rd[o][:, :], op=ALU.mult),
                         reads=[R.b_o[oi], R.b_rd[o]], writes=[b_ob[o]])
                    P.dma("sp", mixT[128 * h:128 * h + 128, dsl(j)], ob[o][:, :], reads=[b_ob[o]], writes=[b_mix])
                attn_qblock(P, C, R, QT[i], b_q[i], j, tiles, epi)
        P.stack = old
    barrier(P)


def nsa_phase(P, C, tag, zT, mixT, b_mix, pek, pev, wk, wv, relb, scr):
    Q0, KC0, VC0, KS0, VS0, KW0, VW0, ZG0 = 4104, 6152, 6664, 7176, 7688, 8200, 8712, 9224
    with ExitStack() as st:
        old = P.stack; P.stack = st
        zg = P.sb(tag + "_zg", [48, 2048], F32); gs = P.sb(tag + "_gs", [48, 2048], F32)
        b_zg, b_gs = Buf(), Buf()
        P.dma("sp", zg[:, :], zT[ZG0:ZG0 + 48, :], writes=[b_zg])
        P.op("act", lambda e: e.activation(out=gs[:, :], in_=zg[:, :], func=AF.Sigmoid), reads=[b_zg], writes=[b_gs])
        P.dma("sp", scr["gD"][:, :], gs[:, :], reads=[b_gs], writes=[scr["b_gD"]])
        P.stack = old
    barrier(P)
    with ExitStack() as st:
        old = P.stack; P.stack = st
        R = AttnRes(P, tag)
        cbias = P.sb(tag + "_cbias", [128, 16], F32); b_cb = Buf()
        P.dma("sp", cbias[:, :], relb[31:32, :].partition_broadcast(128), writes=[b_cb])
        wck = P.sb(tag + "_wck", [128, 32, 128], BF16); wcv = P.sb(tag + "_wcv", [128, 32, 128], BF16)
        peT = P.sb(tag + "_peT", [128, 2, 32], F32)
        b_wc, b_pe = Buf(), Buf()
        for l0 in range(0, 32, 8):
            P.dma("pool", wck[:, l0:l0 + 8, :], wk[l0:l0 + 8].rearrange("l d e -> d l e"), writes=[b_wc])
            P.dma("pool", wcv[:, l0:l0 + 8, :], wv[l0:l0 + 8].rearrange("l d e -> d l e"), writes=[b_wc])
        P.dma("sp", peT[:, 0, :], pek.rearrange("l d -> d l"), writes=[b_pe], allow_slow_non_contiguous=True)
        P.dma("sp", peT[:, 1, :], pev.rearrange("l d -> d l"), writes=[b_pe], allow_slow_non_contiguous=True)
        names = ["kc", "vc", "ks", "vs", "kw", "vw"]
        row0 = dict(kc=KC0, vc=VC0, ks=KS0, vs=VS0, kw=KW0, vw=VW0)
        XT = {n: P.sb(tag + "_x" + n, [128, 2048], BF16) for n in names}
        b_x = {n: Buf() for n in names}
        XA = {n: P.sb(tag + "_a" + n, [128, 2048], BF16) for n in ("kc", "vc")}
        XB = {n: P.sb(tag + "_b" + n, [128, 2048], BF16) for n in ("kc", "vc")}
        b_xa = {n: Buf() for n in XA}
        Vs = P.sb(tag + "_Vs", [128, 16, 128], BF16); Vw = P.sb(tag + "_Vw", [128, 16, 128], BF16)
        b_Vs, b_Vw = Buf(), Buf()
        kcT = P.sb(tag + "_kcT", [128, 128], BF16); vcm = P.sb(tag + "_vcm", [128, 128], BF16)
        b_kcT, b_vcm = Buf(), Buf()
        QT = [P.sb(tag + "_q%d" % r, [128, 2048], BF16) for r in range(4)]
        b_q = [Buf() for _ in range(4)]
        Pn = [P.sb(tag + "_pn%d" % r, [128, 2048], F32) for r in range(4)]
        b_pn = [Buf() for _ in range(4)]
        acc = [P.sb(tag + "_acc%d" % r, [128, 2048], F32) for r in range(4)]
        b_acc = [Buf() for _ in range(4)]
        XF = {"kc": acc[0], "vc": acc[1]}
        b_xf = {"kc": b_acc[0], "vc": b_acc[1]}
        _bm = P.sb(tag + "_bm", [128, 13, 512], BF16); _bbm = Buf()
        BMh = [_bm, _bm]
        b_bm = [_bbm, _bbm]
        gbc = [P.sb(tag + "_gbc%d" % i, [128, 2048], F32) for i in range(2)]
        b_gbc = [Buf(), Buf()]
        selmT = P.sb(tag + "_selmT", [32, 2048], BF16); b_selmT = Buf()
        sc = P.sb(tag + "_sc", [128, 32], F32); sc2 = P.sb(tag + "_sc2", [128, 32], F32)
        m1 = P.sb(tag + "_m1", [128, 8], F32); m2 = P.sb(tag + "_m2", [128, 8], F32)
        selm = P.sb(tag + "_selm", [128, 32], F32)
        b_sc, b_sc2, b_m1, b_m2, b_selm = [Buf() for _ in range(5)]
        gi = [0]; bmi = [0]

        def branch_epilogue(r, hh, br, j, first):
            def epi(oi, pis):
                o = R.oi % 2
                P.op("dve", lambda e, oi=oi, o=o: e.tensor_scalar_max(out=R.rd[o][:, :], in0=R.ps_d[oi][:, :], scalar1=1e-30),
                     reads=[R.b_d[oi]], writes=[R.b_rd[o]])
                P.op("dve", lambda e, o=o: e.reciprocal(out=R.rd[o][:, :], in_=R.rd[o][:, :]), reads=[R.b_rd[o]], writes=[R.b_rd[o]])
                if br == 0:
                    pi = pis[0]
                    P.op("dve", lambda e, o=o, pi=pi: e.tensor_tensor(out=Pn[r][:127, dsl(j)], in0=R.pt[pi][:127, :], in1=R.rd[o][:127, :], op=ALU.mult),
                         reads=[R.b_pt[pi], R.b_rd[o]], writes=[b_pn[r]])
                P.op("dve", lambda e, oi=oi, o=o: e.tensor_tensor(out=R.t1[o][:, :], in0=R.ps_o[oi][:, :], in1=R.rd[o][:, :], op=ALU.mult),
                     reads=[R.b_o[oi], R.b_rd[o]], writes=[R.b_t1[o]])
                g = gcur[0]
                if first:
                    P.op("dve", lambda e, o=o, g=g: e.tensor_tensor(out=acc[r][:, dsl(j)], in0=R.t1[o][:, :], in1=gbc[g][:, dsl(j)], op=ALU.mult),
                         reads=[R.b_t1[o], b_gbc[g]], writes=[b_acc[r]])
                else:
                    P.op("dve", lambda e, o=o, g=g: e.tensor_tensor(out=R.t1[o][:, :], in0=R.t1[o][:, :], in1=gbc[g][:, dsl(j)], op=ALU.mult),
                         reads=[R.b_t1[o], b_gbc[g]], writes=[R.b_t1[o]])
                    P.op("dve", lambda e, o=o: e.tensor_tensor(out=acc[r][:, dsl(j)], in0=acc[r][:, dsl(j)], in1=R.t1[o][:, :], op=ALU.add),
                         reads=[R.b_t1[o], b_acc[r]], writes=[b_acc[r]])
            return epi

        gcur = [0]

        def load_gate(hh, br):
            g = gi[0] % 2; gi[0] += 1
            gcur[0] = g
            row = hh * 3 + br
            P.dma("sp", gbc[g][:, :], scr["gD"][row:row + 1, :].partition_broadcast(128), reads=[scr["b_gD"]], writes=[b_gbc[g]])

        def load_bm(hh):
            i = bmi[0] % 2; bmi[0] += 1
            P.dma("sp", BMh[i][:, 0:9, :], scr["BM"][hh, 0:9].rearrange("k p q -> p k q"), reads=[scr["b_BM"]], writes=[b_bm[i]])
            P.dma("sp", BMh[i][:127, 9:13, :], scr["BM"][hh, 9:13, :127].rearrange("k p q -> p k q"), reads=[scr["b_BM"]], writes=[b_bm[i]])
            return i

        for g in range(4):
            for n in names:
                load_rows_bf16(P, XT[n], zT, row0[n] + 128 * g, b_x[n])
            for n in ("kc", "vc"):
                P.dma("sp", XF[n][:, :], zT[row0[n] + 128 * g: row0[n] + 128 * g + 128, :], writes=[b_xf[n]])
            for r in range(4):
                load_rows_bf16(P, QT[r], zT, Q0 + 128 * (4 * g + r), b_q[r])
            make_V(P, C, R, XT["vs"], b_x["vs"], Vs, b_Vs)
            make_V(P, C, R, XT["vw"], b_x["vw"], Vw, b_Vw)
            for qi, n in enumerate(("kc", "vc")):
                P.op("dve", lambda e, n=n, qi=qi: e.tensor_tensor(
                    out=XA[n][:, :].rearrange("p (c l) -> p c l", l=16), in0=XF[n][:, :].rearrange("p (c l) -> p c l", l=16),
                    in1=peT[:, qi, 0:16].unsqueeze(1).to_broadcast([128, 128, 16]), op=ALU.add),
                    reads=[b_xf[n], b_pe], writes=[b_xa[n]])
                P.op("dve", lambda e, n=n, qi=qi: e.tensor_tensor(
                    out=XB[n][:, :].rearrange("p (c l) -> p c l", l=16), in0=XF[n][:, :].rearrange("p (c l) -> p c l", l=16),
                    in1=peT[:, qi, 16:32].unsqueeze(1).to_broadcast([128, 128, 16]), op=ALU.add),
                    reads=[b_xf[n], b_pe], writes=[b_xa[n]])
            def f_kc(e):
                ins = None
                for l in range(32):
                    X = XA["kc"] if l < 16 else XB["kc"]
                    ins = e.matmul(R.ps_m[:, 0:127], lhsT=wck[:, l, :], rhs=X[:, l:l + 16 * 126 + 1:16], start=(l == 0), stop=(l == 31))
                return ins
            P.op("pe", f_kc, reads=[b_wc, b_xa["kc"]], writes=[R.b_m])
            P.op("dve", lambda e: e.memset(kcT[:, :], 0.0), writes=[b_kcT])
            P.op("dve", lambda e: e.tensor_copy(out=kcT[:, 0:127], in_=R.ps_m[:, 0:127]), reads=[R.b_m], writes=[b_kcT])
            def f_vc(e):
                ins = None
                for l in range(32):
                    X = XA["vc"] if l < 16 else XB["vc"]
                    ins = e.matmul(R.ps_m[:127, 0:128], lhsT=X[:, l:l + 16 * 126 + 1:16], rhs=wcv[:, l, :], start=(l == 0), stop=(l == 31))
                return ins
            P.op("pe", f_vc, reads=[b_wc, b_xa["vc"]], writes=[R.b_m])
            P.op("dve", lambda e: e.memset(vcm[:, :], 0.0), writes=[b_vcm])
            P.op("dve", lambda e: e.tensor_copy(out=vcm[:127, :], in_=R.ps_m[:127, 0:128]), reads=[R.b_m], writes=[b_vcm])
            bmsel = {}
            for r in range(4):
                hh = 4 * g + r
                bi = load_bm(hh); bmsel[r] = bi
                load_gate(hh, 0)
                for j in range(4):
                    t = dict(KT=kcT[:, 0:127], bK=b_kcT, V=vcm[:127, :], bV=b_vcm, m=127,
                             extra=[(C.jrev127b[:127, :127], BMh[bi][:127, 9 + j, :], [C.b, b_bm[bi]])])
                    attn_qblock(P, C, R, QT[r], b_q[r], j, [t], branch_epilogue(r, hh, 0, j, True))
                if r % 2 == 1 and r < 3:
                    pass
            for tc in range(16):
                def f_imp(e, tc=tc):
                    ins = None
                    for r in range(4):
                        ins = e.matmul(R.ps_m[:, 0:32], lhsT=Pn[r][:127, tc * 128:(tc + 1) * 128], rhs=C.ov[:127, :], start=(r == 0), stop=(r == 3))
                    return ins
                P.op("pe", f_imp, reads=b_pn + [C.b], writes=[R.b_m])
                P.op("dve", lambda e, tc=tc: e.tensor_tensor(out=sc[:, :], in0=R.ps_m[:, 0:32], in1=C.scorec[:, tc, :], op=ALU.add),
                     reads=[R.b_m, C.b], writes=[b_sc])
                P.op("dve", lambda e: e.max(out=m1[:, :], in_=sc[:, :]), reads=[b_sc], writes=[b_m1])
                P.op("dve", lambda e: e.match_replace(out=sc2[:, :], in_to_replace=m1[:, :], in_values=sc[:, :], imm_value=-3.0e38),
                     reads=[b_sc, b_m1], writes=[b_sc2])
                P.op("dve", lambda e: e.max(out=m2[:, :], in_=sc2[:, :]), reads=[b_sc2], writes=[b_m2])
                P.op("dve", lambda e: e.tensor_scalar(out=selm[:, :], in0=sc[:, :], scalar1=m2[:, 7:8], scalar2=None, op0=ALU.is_ge),
                     reads=[b_sc, b_m2], writes=[b_selm])
                P.op("dve", lambda e: e.tensor_scalar(out=selm[:, :], in0=selm[:, :], scalar1=1.0, scalar2=BIG, op0=ALU.subtract, op1=ALU.mult),
                     reads=[b_selm], writes=[b_selm])
                P.op("pe", lambda e: e.transpose(out=R.ps_m[:32, 128:256], in_=selm[:, :], identity=C.identf[:, :]),
                     reads=[b_selm, C.b], writes=[R.b_m])
                P.op("dve", lambda e, tc=tc: e.tensor_copy(out=selmT[:, tc * 128:(tc + 1) * 128], in_=R.ps_m[:32, 128:256]),
                     reads=[R.b_m], writes=[b_selmT])
            for r in range(4):
                hh = 4 * g + r
                bi = load_bm(hh)
                load_gate(hh, 1)
                for j in range(4):
                    tiles = []
                    for kc in range(4 * j + 4):
                        rr = kc - 4 * j
                        ex = [(C.Eb[:, kc * 128:(kc + 1) * 128], selmT[:, dsl(j)], [C.b, b_selmT])]
                        t = dict(KT=XT["ks"][:, kc * 128:(kc + 1) * 128], bK=b_x["ks"], V=Vs[:, kc, :], bV=b_Vs, m=128)
                        if rr >= -1:
                            kidx = 8 if rr == -1 else rr + 4
                            ex.append((C.jrevb[:, :], BMh[bi][:, kidx, :], [C.b, b_bm[bi]]))
                        else:
                            t["bias"] = cbias[:, hh:hh + 1]; t["bbias"] = b_cb
                        t["extra"] = ex
                        tiles.append(t)
                    attn_qblock(P, C, R, QT[r], b_q[r], j, tiles, branch_epilogue(r, hh, 1, j, False))
                load_gate(hh, 2)
                for j in range(4):
                    tiles = []
                    for kc in range(max(0, 4 * j - 4), 4 * j + 4):
                        rr = kc - 4 * j
                        t = dict(KT=XT["kw"][:, kc * 128:(kc + 1) * 128], bK=b_x["kw"], V=Vw[:, kc, :], bV=b_Vw, m=128,
                                 extra=[(C.jrevb[:, :], BMh[bi][:, rr + 4, :], [C.b, b_bm[bi]])])
                        tiles.append(t)
                    attn_qblock(P, C, R, QT[r], b_q[r], j, tiles, branch_epilogue(r, hh, 2, j, False))
                for j in range(4):
                    P.dma("sp", mixT[2048 + 128 * hh: 2048 + 128 * hh + 128, dsl(j)], acc[r][:, dsl(j)], reads=[b_acc[r]], writes=[b_mix])
        P.stack = old
    barrier(P)


TWO_PI = 2.0 * math.pi
GC = 1.5957691216057308

def s5_phase(P, tag, zT, mixT, b_mix, prm, cst, scr):
    U0 = 3080
    L = 512
    with ExitStack() as st:
        old = P.stack; P.stack = st
        cnt = [0]
        def T(shape, dt=F32):
            cnt[0] += 1
            return P.sb("%s_t%d" % (tag, cnt[0]), shape, dt)
        def dve(fn, reads, writes):
            P.op("dve", fn, reads=reads, writes=writes)
        def pool(fn, reads, writes):
            P.op("pool", fn, reads=reads, writes=writes)
        def act(fn, reads, writes):
            P.op("act", fn, reads=reads, writes=writes)

        def trig(y, by, shape, eng_name="dve"):
            op = dve if eng_name == "dve" else pool
            ki = T(shape, I32); kf = T(shape); f = T(shape); m = T(shape); fc = T(shape)
            s = T(shape); c = T(shape)
            b = Buf()
            def sl(t):
                return t[tuple(slice(None) for _ in shape)]
            op(lambda e: e.tensor_copy(out=sl(ki), in_=y), [by], [b])
            op(lambda e: e.tensor_copy(out=sl(kf), in_=sl(ki)), [b], [b])
            op(lambda e: e.tensor_tensor(out=sl(f), in0=y, in1=sl(kf), op=ALU.subtract), [by, b], [b])
            def wrap(t):
                op(lambda e: e.tensor_single_scalar(out=sl(m), in_=sl(t), scalar=0.5, op=ALU.is_gt), [b], [b])
                op(lambda e: e.tensor_tensor(out=sl(t), in0=sl(t), in1=sl(m), op=ALU.subtract), [b], [b])
                op(lambda e: e.tensor_single_scalar(out=sl(m), in_=sl(t), scalar=-0.5, op=ALU.is_lt), [b], [b])
                op(lambda e: e.tensor_tensor(out=sl(t), in0=sl(t), in1=sl(m), op=ALU.add), [b], [b])
            wrap(f)
            op(lambda e: e.tensor_scalar_add(out=sl(fc), in0=sl(f), scalar1=0.25), [b], [b])
            wrap(fc)
            act(lambda e: e.activation(out=sl(s), in_=sl(f), func=AF.Sin, scale=TWO_PI), [b], [b])
            act(lambda e: e.activation(out=sl(c), in_=sl(fc), func=AF.Sin, scale=TWO_PI), [b], [b])
            return s, c, b

        lr = T([128, 32]); nlr = T([128, 32]); angn = T([128, 32]); lb_re = T([128, 32]); lb_im = T([128, 32])
        LB = [T([128, 32, 128], BF16) for _ in range(2)]
        LC = [T([128, 32, 128], BF16) for _ in range(2)]
        ps_tr = [P.ps("%s_ptr%d" % (tag, i), [128, 512], F32) for i in range(2)]
        st2 = ExitStack(); st2.__enter__(); P.stack = st2
        A_re = T([128, 32]); A_im = T([128, 32]); ldt = T([128, 32])
        bp = Buf()
        P.dma("sp", A_re[:, :], prm["a_re"].rearrange("(pr g2) p -> g2 p pr", g2=2)[0], writes=[bp], allow_slow_non_contiguous=True) if False else None
        for g2 in range(2):
            P.dma("sp", A_re[64 * g2:64 * g2 + 64, :], prm["a_re"].rearrange("(pr g2) p -> g2 p pr", g2=2)[g2], writes=[bp], allow_slow_non_contiguous=True)
            P.dma("sp", A_im[64 * g2:64 * g2 + 64, :], prm["a_im"].rearrange("(pr g2) p -> g2 p pr", g2=2)[g2], writes=[bp], allow_slow_non_contiguous=True)
            P.dma("sp", ldt[64 * g2:64 * g2 + 64, :], prm["log_dt"].rearrange("(pr g2) -> g2 pr", g2=2)[g2:g2 + 1, :].partition_broadcast(64), writes=[bp], allow_slow_non_contiguous=True)
        lam_re = T([128, 32]); dtt = T([128, 32]); mag = T([128, 32])
        dve(lambda e: e.tensor_scalar_min(out=lam_re[:, :], in0=A_re[:, :], scalar1=-1e-4), [bp], [bp])
        act(lambda e: e.activation(out=dtt[:, :], in_=ldt[:, :], func=AF.Exp), [bp], [bp])
        dve(lambda e: e.tensor_tensor(out=lr[:, :], in0=lam_re[:, :], in1=dtt[:, :], op=ALU.mult), [bp], [bp])
        dve(lambda e: e.tensor_scalar_mul(out=nlr[:, :], in0=lr[:, :], scalar1=-1.0), [bp], [bp])
        dve(lambda e: e.scalar_tensor_tensor(out=angn[:, :], in0=A_im[:, :], scalar=1.0 / TWO_PI, in1=dtt[:, :], op0=ALU.mult, op1=ALU.mult), [bp], [bp])
        act(lambda e: e.activation(out=mag[:, :], in_=lr[:, :], func=AF.Exp), [bp], [bp])
        s0, c0, bt0 = trig(angn[:, :], bp, [128, 32])
        den = T([128, 32]); nr = T([128, 32]); t1 = T([128, 32]); t2 = T([128, 32])
        cf_re = T([128, 32]); cf_im = T([128, 32])
        dve(lambda e: e.tensor_tensor(out=lb_re[:, :], in0=mag[:, :], in1=c0[:, :], op=ALU.mult), [bp, bt0], [bp])
        dve(lambda e: e.tensor_tensor(out=lb_im[:, :], in0=mag[:, :], in1=s0[:, :], op=ALU.mult), [bp, bt0], [bp])
        dve(lambda e: e.tensor_tensor(out=den[:, :], in0=lam_re[:, :], in1=lam_re[:, :], op=ALU.mult), [bp], [bp])
        dve(lambda e: e.tensor_tensor(out=t1[:, :], in0=A_im[:, :], in1=A_im[:, :], op=ALU.mult), [bp], [bp])
        dve(lambda e: e.tensor_tensor(out=den[:, :], in0=den[:, :], in1=t1[:, :], op=ALU.add), [bp], [bp])
        dve(lambda e: e.reciprocal(out=den[:, :], in_=den[:, :]), [bp], [bp])
        dve(lambda e: e.tensor_scalar_add(out=nr[:, :], in0=lb_re[:, :], scalar1=-1.0), [bp], [bp])
        dve(lambda e: e.tensor_tensor(out=t1[:, :], in0=nr[:, :], in1=lam_re[:, :], op=ALU.mult), [bp], [bp])
        dve(lambda e: e.tensor_tensor(out=t2[:, :], in0=lb_im[:, :], in1=A_im[:, :], op=ALU.mult), [bp], [bp])
        dve(lambda e: e.tensor_tensor(out=t1[:, :], in0=t1[:, :], in1=t2[:, :], op=ALU.add), [bp], [bp])
        dve(lambda e: e.tensor_tensor(out=cf_re[:, :], in0=t1[:, :], in1=den[:, :], op=ALU.mult), [bp], [bp])
        dve(lambda e: e.tensor_tensor(out=t1[:, :], in0=lb_im[:, :], in1=lam_re[:, :], op=ALU.mult), [bp], [bp])
        dve(lambda e: e.tensor_tensor(out=t2[:, :], in0=nr[:, :], in1=A_im[:, :], op=ALU.mult), [bp], [bp])
        dve(lambda e: e.tensor_tensor(out=t1[:, :], in0=t1[:, :], in1=t2[:, :], op=ALU.subtract), [bp], [bp])
        dve(lambda e: e.tensor_tensor(out=cf_im[:, :], in0=t1[:, :], in1=den[:, :], op=ALU.mult), [bp], [bp])
        Bre = T([128, 32, 16]); Bim = T([128, 32, 16]); BBre = T([128, 32, 16]); BBim = T([128, 32, 16]); tb = T([128, 32, 16])
        for g2 in range(2):
            P.dma("sp", Bre[64 * g2:64 * g2 + 64, :, :], prm["b_re"].rearrange("(pr g2) p h -> g2 p pr h", g2=2)[g2], writes=[bp])
            P.dma("sp", Bim[64 * g2:64 * g2 + 64, :, :], prm["b_im"].rearrange("(pr g2) p h -> g2 p pr h", g2=2)[g2], writes=[bp])
        def bc(t):
            return t[:, :].unsqueeze(2).to_broadcast([128, 32, 16])
        dve(lambda e: e.tensor_tensor(out=BBre[:, :, :], in0=Bre[:, :, :], in1=bc(cf_re), op=ALU.mult), [bp], [bp])
        dve(lambda e: e.tensor_tensor(out=tb[:, :, :], in0=Bim[:, :, :], in1=bc(cf_im), op=ALU.mult), [bp], [bp])
        dve(lambda e: e.tensor_tensor(out=BBre[:, :, :], in0=BBre[:, :, :], in1=tb[:, :, :], op=ALU.subtract), [bp], [bp])
        dve(lambda e: e.tensor_tensor(out=BBim[:, :, :], in0=Bim[:, :, :], in1=bc(cf_re), op=ALU.mult), [bp], [bp])
        dve(lambda e: e.tensor_tensor(out=tb[:, :, :], in0=Bre[:, :, :], in1=bc(cf_im), op=ALU.mult), [bp], [bp])
        dve(lambda e: e.tensor_tensor(out=BBim[:, :, :], in0=BBim[:, :, :], in1=tb[:, :, :], op=ALU.add), [bp], [bp])
        bL = Buf()
        mB = T([128, 4, 128]); mC = T([128, 4, 128]); idf = T([128, 128]); bm_ = Buf()
        P.dma("sp", mB[:, :, :], cst["maskB"][:, :, :], writes=[bm_])
        P.dma("sp", mC[:, :, :], cst["maskC"][:, :, :], writes=[bm_])
        P.dma("sp", idf[:, :], cst["ident"][:, :], writes=[bm_])
        BBrep = [T([128, 32, 8, 16]) for _ in range(2)]
        Cw = [T([128, 8, 128]) for _ in range(2)]
        bCw = Buf()
        b_ptr = [Buf(), Buf()]
        for q, BBq in enumerate((BBre, BBim)):
            dve(lambda e, q=q, BBq=BBq: e.tensor_copy(out=BBrep[q][:, :, :, :], in_=BBq[:, :, :].unsqueeze(2).to_broadcast([128, 32, 8, 16])), [bp], [bp])
            csrc = prm["c_re"] if q == 0 else prm["c_im"]
            for half in range(2):
                P.dma("sp", Cw[q][:, :, 64 * half:64 * half + 64], csrc.rearrange("(cc gl) ho p -> (gl ho) cc p", gl=8), writes=[bCw])
        it = 0
        for q in range(2):
            for cc in range(8):
                i = it % 2; it += 1
                def f_t(e, q=q, cc=cc, i=i):
                    ins = None
                    for k in range(4):
                        ins = e.transpose(out=ps_tr[i][:, 128 * k:128 * k + 128], in_=BBrep[q][:, 4 * cc + k, :, :].rearrange("p a b -> p (a b)"), identity=idf[:, :])
                    return ins
                P.op("pe", f_t, reads=[bp, bm_], writes=[b_ptr[i]])
                dve(lambda e, q=q, cc=cc, i=i: e.tensor_tensor(out=LB[q][:, 4 * cc:4 * cc + 4, :], in0=ps_tr[i][:, :].rearrange("p (k m) -> p k m", k=4), in1=mB[:, :, :], op=ALU.mult),
                    [b_ptr[i], bm_], [bL])
                i = it % 2; it += 1
                P.op("pe", lambda e, q=q, cc=cc, i=i: e.transpose(out=ps_tr[i][:, 0:128], in_=Cw[q][:, cc, :], identity=idf[:, :]), reads=[bCw, bm_], writes=[b_ptr[i]])
                sgn = 1.0 if q == 0 else -1.0
                dve(lambda e, q=q, cc=cc, i=i, sgn=sgn: e.scalar_tensor_tensor(out=LC[q][:, 4 * cc:4 * cc + 4, :], in0=mC[:, :, :], scalar=sgn,
                                                                           in1=ps_tr[i][:, 0:128].unsqueeze(1).to_broadcast([128, 4, 128]), op0=ALU.mult, op1=ALU.mult),
                    [b_ptr[i], bm_], [bL])
        st2.__exit__(None, None, None); P.stack = st
        barrier(P)
        ubf = T([128, 2, 2048], BF16); bu2 = [Buf(), Buf()]
        dsk = T([128, 8]); bd = Buf()
        P.dma("sp", dsk[:, :], prm["d"].rearrange("(c p) -> p c", p=128), writes=[bd], allow_slow_non_contiguous=True)
        iot = T([128, L]); bi = Buf()
        P.dma("sp", iot[:, :], cst["iota"][0:1, 0:L].partition_broadcast(128), writes=[bi])
        wg = T([128, 8, 1024], BF16); bw = Buf()
        for k in range(8):
            P.dma("pool", wg[:, k, :], prm["w_glu"][128 * k:128 * k + 128, :], writes=[bw])
        ygb = T([128, 8, 2048], BF16); bygb = Buf()
        ps_bu = [[P.ps("%s_pbu%d%d" % (tag, 0, q), [128, 512], F32) for q in range(2)], ps_tr]
        b_pbu = [Buf(), Buf()]
        b_pbu[1] = b_ptr[0]; b_ptr[1] = b_ptr[0]
        ps_y = [P.ps("%s_py%d" % (tag, i), [128, 512], F32) for i in range(4)]
        b_py = [Buf() for _ in range(4)]
        yt = T([128, L]); mp = T([128, L]); mn = T([128, L])
        Lp = [T([128, L]), T([128, L])]; Lm = [T([128, L]), T([128, L])]
        b_tab = Buf()
        tki = T([128, L], I32); tkf = T([128, L]); tf = T([128, L]); tm = T([128, L]); tfc = T([128, L]); ts_ = T([128, L]); tc_ = T([128, L])
        gre = [T([128, L]) for _ in range(2)]; gim = [T([128, L]) for _ in range(2)]
        Gre = [T([128, L]) for _ in range(2)]; Gim = [T([128, L]) for _ in range(2)]
        tmpa = [T([128, L]) for _ in range(2)]; tmpb = [T([128, L]) for _ in range(2)]
        hre = [T([128, L], BF16) for _ in range(2)]; him = [T([128, L], BF16) for _ in range(2)]
        hlast = [T([128, 2]) for _ in range(2)]
        b_g = [Buf(), Buf()]; b_G = [Buf(), Buf()]; b_h = [Buf(), Buf()]; b_tmp = [Buf(), Buf()]; b_hl = [Buf(), Buf()]
        ones = T([128, L]); b1 = Buf()
        dve(lambda e: e.memset(ones[:, :], 1.0), [], [b1])
        sc4 = T([128, 4]); b_sc = Buf()
        yf = T([128, 2048]); x2 = T([128, 2048]); b_yf = Buf()
        uf = T([128, 2048]); b_uf = Buf()
        blk = 0
        for pr in range(32):
            cc = pr // 4
            bu_ = bu2[cc % 2]
            if pr % 4 == 0:
                P.dma("pool", ubf[:, cc % 2, :], zT[U0 + 128 * cc:U0 + 128 * cc + 128, :], writes=[bu_])
            dve(lambda e, pr=pr: e.tensor_scalar_mul(out=yt[:, :], in0=iot[:, :], scalar1=angn[:, pr:pr + 1]), [bi, bp, b_tab], [b_tab])
            dve(lambda e: e.tensor_copy(out=tki[:, :], in_=yt[:, :]), [b_tab], [b_tab])
            dve(lambda e: e.tensor_copy(out=tkf[:, :], in_=tki[:, :]), [b_tab], [b_tab])
            dve(lambda e: e.tensor_tensor(out=tf[:, :], in0=yt[:, :], in1=tkf[:, :], op=ALU.subtract), [b_tab], [b_tab])
            def wrap(t):
                dve(lambda e: e.tensor_single_scalar(out=tm[:, :], in_=t[:, :], scalar=0.5, op=ALU.is_gt), [b_tab], [b_tab])
                dve(lambda e: e.tensor_tensor(out=t[:, :], in0=t[:, :], in1=tm[:, :], op=ALU.subtract), [b_tab], [b_tab])
                dve(lambda e: e.tensor_single_scalar(out=tm[:, :], in_=t[:, :], scalar=-0.5, op=ALU.is_lt), [b_tab], [b_tab])
                dve(lambda e: e.tensor_tensor(out=t[:, :], in0=t[:, :], in1=tm[:, :], op=ALU.add), [b_tab], [b_tab])
            wrap(tf)
            dve(lambda e: e.tensor_scalar_add(out=tfc[:, :], in0=tf[:, :], scalar1=0.25), [b_tab], [b_tab])
            wrap(tfc)
            act(lambda e: e.activation(out=ts_[:, :], in_=tf[:, :], func=AF.Sin, scale=TWO_PI), [b_tab], [b_tab])
            act(lambda e: e.activation(out=tc_[:, :], in_=tfc[:, :], func=AF.Sin, scale=TWO_PI), [b_tab], [b_tab])
            act(lambda e, pr=pr: e.activation(out=mp[:, :], in_=iot[:, :], func=AF.Exp, scale=lr[:, pr:pr + 1]), [bi, bp, b_tab], [b_tab])
            act(lambda e, pr=pr: e.activation(out=mn[:, :], in_=iot[:, :], func=AF.Exp, scale=nlr[:, pr:pr + 1]), [bi, bp, b_tab], [b_tab])
            dve(lambda e: e.tensor_tensor(out=Lp[0][:, :], in0=mp[:, :], in1=tc_[:, :], op=ALU.mult), [b_tab], [b_tab])
            dve(lambda e: e.tensor_tensor(out=Lp[1][:, :], in0=mp[:, :], in1=ts_[:, :], op=ALU.mult), [b_tab], [b_tab])
            dve(lambda e: e.tensor_tensor(out=Lm[0][:, :], in0=mn[:, :], in1=tc_[:, :], op=ALU.mult), [b_tab], [b_tab])
            dve(lambda e: e.scalar_tensor_tensor(out=Lm[1][:, :], in0=mn[:, :], scalar=-1.0, in1=ts_[:, :], op0=ALU.mult, op1=ALU.mult), [b_tab], [b_tab])
            for c in range(4):
                i = blk % 2; blk += 1
                def f_bu(e, pr=pr, cc=cc, c=c, i=i):
                    e.matmul(ps_bu[i][0][:, :], lhsT=LB[0][:, pr, :], rhs=ubf[:, cc % 2, c * L:(c + 1) * L], start=True, stop=True)
                    return e.matmul(ps_bu[i][1][:, :], lhsT=LB[1][:, pr, :], rhs=ubf[:, cc % 2, c * L:(c + 1) * L], start=True, stop=True)
                P.op("pe", f_bu, reads=[bL, bu_], writes=[b_pbu[i]])
                dve(lambda e, i=i: e.tensor_tensor(out=gre[i][:, :], in0=ps_bu[i][0][:, :], in1=Lm[0][:, :], op=ALU.mult), [b_pbu[i], b_tab], [b_g[i]])
                dve(lambda e, i=i: e.tensor_tensor(out=tmpa[i][:, :], in0=ps_bu[i][1][:, :], in1=Lm[1][:, :], op=ALU.mult), [b_pbu[i], b_tab], [b_tmp[i]])
                pool(lambda e, i=i: e.tensor_tensor(out=gre[i][:, :], in0=gre[i][:, :], in1=tmpa[i][:, :], op=ALU.subtract), [b_g[i], b_tmp[i]], [b_g[i]])
                dve(lambda e, i=i: e.tensor_tensor(out=gim[i][:, :], in0=ps_bu[i][1][:, :], in1=Lm[0][:, :], op=ALU.mult), [b_pbu[i], b_tab], [b_g[i]])
                dve(lambda e, i=i: e.tensor_tensor(out=tmpb[i][:, :], in0=ps_bu[i][0][:, :], in1=Lm[1][:, :], op=ALU.mult), [b_pbu[i], b_tab], [b_tmp[i]])
                pool(lambda e, i=i: e.tensor_tensor(out=gim[i][:, :], in0=gim[i][:, :], in1=tmpb[i][:, :], op=ALU.add), [b_g[i], b_tmp[i]], [b_g[i]])
                dve(lambda e, i=i: e.tensor_tensor_scan(out=Gre[i][:, :], data0=ones[:, :], data1=gre[i][:, :], initial=0.0, op0=ALU.mult, op1=ALU.add), [b1, b_g[i]], [b_G[i]])
                dve(lambda e, i=i: e.tensor_tensor_scan(out=Gim[i][:, :], data0=ones[:, :], data1=gim[i][:, :], initial=0.0, op0=ALU.mult, op1=ALU.add), [b1, b_g[i]], [b_G[i]])
                if c > 0:
                    pv = 1 - i
                    dve(lambda e, pr=pr, pv=pv: e.tensor_tensor(out=sc4[:, 0:1], in0=hlast[pv][:, 0:1], in1=lb_re[:, pr:pr + 1], op=ALU.mult), [b_hl[pv], bp, b_sc], [b_sc])
                    dve(lambda e, pr=pr, pv=pv: e.tensor_tensor(out=sc4[:, 1:2], in0=hlast[pv][:, 1:2], in1=lb_im[:, pr:pr + 1], op=ALU.mult), [b_hl[pv], bp, b_sc], [b_sc])
                    dve(lambda e: e.tensor_tensor(out=sc4[:, 0:1], in0=sc4[:, 0:1], in1=sc4[:, 1:2], op=ALU.subtract), [b_sc], [b_sc])
                    dve(lambda e, pr=pr, pv=pv: e.tensor_tensor(out=sc4[:, 2:3], in0=hlast[pv][:, 1:2], in1=lb_re[:, pr:pr + 1], op=ALU.mult), [b_hl[pv], bp, b_sc], [b_sc])
                    dve(lambda e, pr=pr, pv=pv: e.tensor_tensor(out=sc4[:, 3:4], in0=hlast[pv][:, 0:1], in1=lb_im[:, pr:pr + 1], op=ALU.mult), [b_hl[pv], bp, b_sc], [b_sc])
                    dve(lambda e: e.tensor_tensor(out=sc4[:, 2:3], in0=sc4[:, 2:3], in1=sc4[:, 3:4], op=ALU.add), [b_sc], [b_sc])
                    dve(lambda e, i=i: e.tensor_scalar_add(out=Gre[i][:, :], in0=Gre[i][:, :], scalar1=sc4[:, 0:1]), [b_sc, b_G[i]], [b_G[i]])
                    dve(lambda e, i=i: e.tensor_scalar_add(out=Gim[i][:, :], in0=Gim[i][:, :], scalar1=sc4[:, 2:3]), [b_sc, b_G[i]], [b_G[i]])
                pool(lambda e, i=i: e.tensor_tensor(out=tmpa[i][:, :], in0=Gre[i][:, :], in1=Lp[0][:, :], op=ALU.mult), [b_G[i], b_tab, b_tmp[i]], [b_tmp[i]])
                pool(lambda e, i=i: e.tensor_tensor(out=tmpb[i][:, :], in0=Gim[i][:, :], in1=Lp[1][:, :], op=ALU.mult), [b_G[i], b_tab, b_tmp[i]], [b_tmp[i]])
                pool(lambda e, i=i: e.tensor_tensor(out=gre[i][:, :], in0=tmpa[i][:, :], in1=tmpb[i][:, :], op=ALU.subtract), [b_tmp[i], b_g[i]], [b_g[i]])
                pool(lambda e, i=i: e.tensor_tensor(out=tmpa[i][:, :], in0=Gim[i][:, :], in1=Lp[0][:, :], op=ALU.mult), [b_G[i], b_tab, b_tmp[i]], [b_tmp[i]])
                pool(lambda e, i=i: e.tensor_tensor(out=tmpb[i][:, :], in0=Gre[i][:, :], in1=Lp[1][:, :], op=ALU.mult), [b_G[i], b_tab, b_tmp[i]], [b_tmp[i]])
                pool(lambda e, i=i: e.tensor_tensor(out=gim[i][:, :], in0=tmpa[i][:, :], in1=tmpb[i][:, :], op=ALU.add), [b_tmp[i], b_g[i]], [b_g[i]])
                pool(lambda e, i=i: e.tensor_copy(out=hlast[i][:, 0:1], in_=gre[i][:, L - 1:L]), [b_g[i], b_hl[i]], [b_hl[i]])
                pool(lambda e, i=i: e.tensor_copy(out=hlast[i][:, 1:2], in_=gim[i][:, L - 1:L]), [b_g[i], b_hl[i]], [b_hl[i]])
                act(lambda e, i=i: e.copy(out=hre[i][:, :], in_=gre[i][:, :]), [b_g[i]], [b_h[i]])
                act(lambda e, i=i: e.copy(out=him[i][:, :], in_=gim[i][:, :]), [b_g[i]], [b_h[i]])
                def f_c(e, pr=pr, c=c, i=i):
                    e.matmul(ps_y[c][:, :], lhsT=LC[0][:, pr, :], rhs=hre[i][:, :], start=(pr % 4 == 0), stop=False)
                    return e.matmul(ps_y[c][:, :], lhsT=LC[1][:, pr, :], rhs=him[i][:, :], start=False, stop=(pr % 4 == 3))
                P.op("pe", f_c, reads=[bL, b_h[i]], writes=[b_py[c]])
            if pr % 4 == 3:
                P.dma("sp", uf[:, :], zT[U0 + 128 * cc:U0 + 128 * cc + 128, :], writes=[b_uf])
                for c in range(4):
                    dve(lambda e, c=c, cc=cc: e.scalar_tensor_tensor(out=yf[:, c * L:(c + 1) * L], in0=uf[:, c * L:(c + 1) * L], scalar=dsk[:, cc:cc + 1],
                                                                      in1=ps_y[c][:, :], op0=ALU.mult, op1=ALU.add), [b_uf, bd, b_py[c]], [b_yf])
                act(lambda e: e.activation(out=x2[:, :], in_=yf[:, :], func=AF.Square), [b_yf], [b_yf])
                dve(lambda e: e.tensor_scalar(out=x2[:, :], in0=x2[:, :], scalar1=0.044715, scalar2=1.0, op0=ALU.mult, op1=ALU.add), [b_yf], [b_yf])
                dve(lambda e: e.tensor_tensor(out=x2[:, :], in0=x2[:, :], in1=yf[:, :], op=ALU.mult), [b_yf], [b_yf])
                act(lambda e: e.activation(out=x2[:, :], in_=x2[:, :], func=AF.Sigmoid, scale=GC), [b_yf], [b_yf])
                dve(lambda e: e.tensor_tensor(out=yf[:, :], in0=yf[:, :], in1=x2[:, :], op=ALU.mult), [b_yf], [b_yf])
                act(lambda e, cc=cc: e.copy(out=ygb[:, cc, :], in_=yf[:, :]), [b_yf], [bygb])
                P.dma("sp", scr["ygD"][128 * cc:128 * cc + 128, :], yf[:, :], reads=[b_yf], writes=[scr["b_ygD"]])
        for n in range(8):
            for c in range(4):
                def f_g(e, n=n, c=c):
                    ins = None
                    for k in range(8):
                        ins = e.matmul(ps_y[c][:, :], lhsT=wg[:, k, 128 * n:128 * n + 128], rhs=ygb[:, k, c * L:(c + 1) * L], start=(k == 0), stop=(k == 7))
                    return ins
                P.op("pe", f_g, reads=[bw, bygb], writes=[b_py[c]])
            P.dma("sp", uf[:, :], scr["ygD"][128 * n:128 * n + 128, :], reads=[scr["b_ygD"]], writes=[b_uf])
            for c in range(4):
                act(lambda e, c=c: e.activation(out=x2[:, c * L:(c + 1) * L], in_=ps_y[c][:, :], func=AF.Sigmoid), [b_py[c], b_yf], [b_yf])
            dve(lambda e: e.tensor_tensor(out=yf[:, :], in0=uf[:, :], in1=x2[:, :], op=ALU.mult), [b_uf, b_yf], [b_yf])
            P.dma("sp", mixT[1024 + 128 * n:1024 + 128 * n + 128, :], yf[:, :], reads=[b_yf], writes=[b_mix])
        P.stack = old
    barrier(P)


def transpose_in(P, tag, x, xT, b_xT, identf, b_c):
    with ExitStack() as st:
        old = P.stack; P.stack = st
        xs = [P.sb(tag + "_xs%d" % i, [128, 4096], F32) for i in range(2)]
        stg = [P.sb(tag + "_st%d" % i, [128, 32, 128], F32) for i in range(2)]
        ps = [P.ps(tag + "_ps%d" % i, [128, 512], F32) for i in range(2)]
        b_xs = [Buf(), Buf()]; b_st = [Buf(), Buf()]; b_ps = [Buf(), Buf()]
        it = 0
        xTv = xT.rearrange("(fc p) t -> p fc t", p=128)
        for tt in range(16):
            i = tt % 2
            P.dma("sp", xs[i][:, :], x[tt * 128:(tt + 1) * 128, :], writes=[b_xs[i]])
            for f4 in range(8):
                pi = it % 2; it += 1
                def f(e, i=i, f4=f4, pi=pi):
                    ins = None
                    for k in range(4):
                        fc = 4 * f4 + k
                        ins = e.transpose(out=ps[pi][:, 128 * k:128 * k + 128], in_=xs[i][:, fc * 128:(fc + 1) * 128], identity=identf[:, :])
                    return ins
                P.op("pe", f, reads=[b_xs[i], b_c], writes=[b_ps[pi]])
                eng = "dve" if pi == 0 else "act"
                if eng == "dve":
                    P.op("dve", lambda e, i=i, f4=f4, pi=pi: e.tensor_copy(out=stg[i][:, 4 * f4:4 * f4 + 4, :], in_=ps[pi][:, :].rearrange("p (k t) -> p k t", k=4)),
                         reads=[b_ps[pi]], writes=[b_st[i]])
                else:
                    P.op("act", lambda e, i=i, f4=f4, pi=pi: e.copy(out=stg[i][:, 4 * f4:4 * f4 + 4, :], in_=ps[pi][:, :].rearrange("p (k t) -> p k t", k=4)),
                         reads=[b_ps[pi]], writes=[b_st[i]])
            for f8 in range(0, 32, 8):
                P.dma("sp", xTv[:, f8:f8 + 8, tt * 128:(tt + 1) * 128], stg[i][:, f8:f8 + 8, :], reads=[b_st[i]], writes=[b_xT])
        P.stack = old
    barrier(P)


def transpose_out(P, tag, xT, b_xT, y, b_y, identf, b_c):
    with ExitStack() as st:
        old = P.stack; P.stack = st
        xs = [P.sb(tag + "_xs%d" % i, [128, 32, 128], F32) for i in range(2)]
        stg = [P.sb(tag + "_st%d" % i, [128, 4096], F32) for i in range(2)]
        ps = [P.ps(tag + "_ps%d" % i, [128, 512], F32) for i in range(2)]
        b_xs = [Buf(), Buf()]; b_st = [Buf(), Buf()]; b_ps = [Buf(), Buf()]
        it = 0
        xTv = xT.rearrange("(fc p) t -> p fc t", p=128)
        for tt in range(16):
            i = tt % 2
            for f8 in range(0, 32, 8):
                P.dma("sp", xs[i][:, f8:f8 + 8, :], xTv[:, f8:f8 + 8, tt * 128:(tt + 1) * 128], reads=[b_xT], writes=[b_xs[i]])
            for f4 in range(8):
                pi = it % 2; it += 1
                def f(e, i=i, f4=f4, pi=pi):
                    ins = None
                    for k in range(4):
                        ins = e.transpose(out=ps[pi][:, 128 * k:128 * k + 128], in_=xs[i][:, 4 * f4 + k, :], identity=identf[:, :])
                    return ins
                P.op("pe", f, reads=[b_xs[i], b_c], writes=[b_ps[pi]])
                if pi == 0:
                    P.op("dve", lambda e, i=i, f4=f4, pi=pi: e.tensor_copy(out=stg[i][:, 512 * f4:512 * f4 + 512], in_=ps[pi][:, :]), reads=[b_ps[pi]], writes=[b_st[i]])
                else:
                    P.op("act", lambda e, i=i, f4=f4, pi=pi: e.copy(out=stg[i][:, 512 * f4:512 * f4 + 512], in_=ps[pi][:, :]), reads=[b_ps[pi]], writes=[b_st[i]])
            P.dma("sp", y[tt * 128:(tt + 1) * 128, :], stg[i][:, :], reads=[b_st[i]], writes=[b_y])
        P.stack = old
    barrier(P)


def norm_phase(P, tag, srcT, r0, F, gain, mode, dstT, d0, b_src, b_dst, onesf, b_c, eps=1e-6):
    C = F // 128
    TB = 1024
    with ExitStack() as st:
        old = P.stack; P.stack = st
        X = P.sb(tag + "_X", [128, C, TB], F32); b_X = Buf()
        g = P.sb(tag + "_g", [128, C], F32); b_g = Buf()
        sq = [P.sb(tag + "_sq%d" % i, [128, TB], F32) for i in range(2)]; b_sq = [Buf(), Buf()]
        rs = P.sb(tag + "_rs", [128, TB], F32); b_rs = Buf()
        ps = [P.ps(tag + "_ps%d" % i, [128, 512], F32) for i in range(2)]; b_ps = Buf()
        o16 = [P.sb(tag + "_o%d" % i, [128, TB], BF16 if mode == "bf16" else F32) for i in range(2)]; b_o = [Buf(), Buf()]
        xr = [P.sb(tag + "_xr%d" % i, [128, TB], F32) for i in range(2)]; b_xr = [Buf(), Buf()]
        P.dma("sp", g[:, :], gain.rearrange("(c p) -> p c", p=128), writes=[b_g], allow_slow_non_contiguous=True)
        sv = srcT[r0:r0 + F, :].rearrange("(c p) t -> p c t", p=128)
        dv = dstT[d0:d0 + F, :].rearrange("(c p) t -> p c t", p=128)
        for tb in range(2048 // TB):
            t0 = tb * TB
            for c0 in range(0, C, 4):
                c1 = min(C, c0 + 4)
                P.dma("sp", X[:, c0:c1, :], sv[:, c0:c1, t0:t0 + TB], reads=[b_src], writes=[b_X])
            for c in range(C):
                i = c % 2
                P.op("act", lambda e, c=c, i=i: e.activation(out=sq[i][:, :], in_=X[:, c, :], func=AF.Square), reads=[b_X], writes=[b_sq[i]])
                def f(e, c=c, i=i):
                    e.matmul(ps[0][:, :], lhsT=onesf[:, :], rhs=sq[i][:, 0:512], start=(c == 0), stop=(c == C - 1))
                    return e.matmul(ps[1][:, :], lhsT=onesf[:, :], rhs=sq[i][:, 512:1024], start=(c == 0), stop=(c == C - 1))
                P.op("pe", f, reads=[b_sq[i], b_c], writes=[b_ps])
            for h in range(2):
                P.op("act", lambda e, h=h: e.activation(out=rs[:, 512 * h:512 * h + 512], in_=ps[h][:, :], func=AF.Sqrt, scale=1.0 / F, bias=eps),
                     reads=[b_ps], writes=[b_rs])
            P.op("dve", lambda e: e.reciprocal(out=rs[:, :], in_=rs[:, :]), reads=[b_rs], writes=[b_rs])
            for c in range(C):
                i = c % 2
                if mode == "bf16":
                    P.op("dve", lambda e, c=c, i=i: e.scalar_tensor_tensor(out=o16[i][:, :], in0=X[:, c, :], scalar=g[:, c:c + 1], in1=rs[:, :], op0=ALU.mult, op1=ALU.mult),
                         reads=[b_X, b_g, b_rs], writes=[b_o[i]])
                    P.dma("sp", dv[:, c, t0:t0 + TB], o16[i][:, :], reads=[b_o[i]], writes=[b_dst])
                else:
                    P.dma("sp", xr[i][:, :], dv[:, c, t0:t0 + TB], reads=[b_dst], writes=[b_xr[i]])
                    P.op("dve", lambda e, c=c, i=i: e.scalar_tensor_tensor(out=o16[i][:, :], in0=X[:, c, :], scalar=g[:, c:c + 1], in1=rs[:, :], op0=ALU.mult, op1=ALU.mult),
                         reads=[b_X, b_g, b_rs], writes=[b_o[i]])
                    P.op("dve", lambda e, i=i: e.tensor_tensor(out=o16[i][:, :], in0=o16[i][:, :], in1=xr[i][:, :], op=ALU.add),
                         reads=[b_o[i], b_xr[i]], writes=[b_o[i]])
                    P.dma("sp", dv[:, c, t0:t0 + TB], o16[i][:, :], reads=[b_o[i]], writes=[b_dst])
        P.stack = old
    barrier(P)


def conv_phase(P, tag, uT, b_u, actT, b_act, conv_w, conv_b, DF=11008, jobs=None):
    NJ = DF // 128
    with ExitStack() as st:
        old = P.stack; P.stack = st
        cw = P.sb(tag + "_cw", [128, 3, 2 * NJ], F32); cb = P.sb(tag + "_cb", [128, 2 * NJ], F32); b_cw = Buf()
        for k in range(3):
            P.dma("sp", cw[:, k, :], conv_w[k].rearrange("(c p) -> p c", p=128), writes=[b_cw], allow_slow_non_contiguous=True)
        P.dma("sp", cb[:, :], conv_b.rearrange("(c p) -> p c", p=128), writes=[b_cw], allow_slow_non_contiguous=True)
        ug = [P.sb(tag + "_ug%d" % i, [128, 2050], F32) for i in range(2)]
        uv = [P.sb(tag + "_uv%d" % i, [128, 2050], F32) for i in range(2)]
        b_ug = [Buf(), Buf()]; b_uv = [Buf(), Buf()]
        yg = P.sb(tag + "_yg", [128, 2048], F32); yv = P.sb(tag + "_yv", [128, 2048], F32); s2 = P.sb(tag + "_s2", [128, 2048], F32)
        b_yg, b_yv, b_s2 = Buf(), Buf(), Buf()
        tv = P.sb(tag + "_tv", [128, 2048], F32); b_tv = Buf()
        ao = [P.sb(tag + "_ao%d" % i, [128, 2048], BF16) for i in range(2)]; b_ao = [Buf(), Buf()]
        for i in range(2):
            P.op("dve", lambda e, i=i: e.memset(ug[i][:, 0:2], 0.0), writes=[b_ug[i]])
            P.op("dve", lambda e, i=i: e.memset(uv[i][:, 0:2], 0.0), writes=[b_uv[i]])
        for j in range(NJ):
            i = j % 2
            P.dma("sp", ug[i][:, 2:2050], uT[128 * j:128 * j + 128, :], reads=[b_u], writes=[b_ug[i]])
            P.dma("sp", uv[i][:, 2:2050], uT[DF + 128 * j:DF + 128 * j + 128, :], reads=[b_u], writes=[b_uv[i]])
            for (u, bu, y, by, cj) in ((ug[i], b_ug[i], yg, b_yg, j), (uv[i], b_uv[i], yv, b_yv, NJ + j)):
                P.op("act", lambda e, u=u, y=y, cj=cj: e.activation(out=y[:, :], in_=u[:, 2:2050], func=AF.Identity, scale=cw[:, 2, cj:cj + 1], bias=cb[:, cj:cj + 1]),
                     reads=[bu, b_cw], writes=[by])
            P.op("dve", lambda e, u=ug[i], cj=j: e.scalar_tensor_tensor(out=yg[:, :], in0=u[:, 1:2049], scalar=cw[:, 1, cj:cj + 1], in1=yg[:, :], op0=ALU.mult, op1=ALU.add),
                 reads=[b_ug[i], b_cw, b_yg], writes=[b_yg])
            P.op("dve", lambda e, u=ug[i], cj=j: e.scalar_tensor_tensor(out=yg[:, :], in0=u[:, 0:2048], scalar=cw[:, 0, cj:cj + 1], in1=yg[:, :], op0=ALU.mult, op1=ALU.add),
                 reads=[b_ug[i], b_cw, b_yg], writes=[b_yg])
            for tap in (1, 0):
                P.op("pool", lambda e, u=uv[i], cj=NJ + j, tap=tap: e.tensor_scalar(out=tv[:, :], in0=u[:, tap:tap + 2048], scalar1=cw[:, tap, cj:cj + 1], scalar2=0.0, op0=ALU.mult, op1=ALU.add),
                     reads=[b_uv[i], b_cw, b_tv], writes=[b_tv])
                P.op("pool", lambda e: e.tensor_tensor(out=yv[:, :], in0=yv[:, :], in1=tv[:, :], op=ALU.add), reads=[b_tv, b_yv], writes=[b_yv])
            if jobs:
                for _ in range(3):
                    if jobs:
                        jobs.pop(0)()
            P.op("act", lambda e: e.activation(out=s2[:, :], in_=yg[:, :], func=AF.Square), reads=[b_yg], writes=[b_s2])
            P.op("dve", lambda e: e.tensor_scalar(out=s2[:, :], in0=s2[:, :], scalar1=0.044715, scalar2=1.0, op0=ALU.mult, op1=ALU.add), reads=[b_s2], writes=[b_s2])
            P.op("dve", lambda e: e.tensor_tensor(out=s2[:, :], in0=s2[:, :], in1=yg[:, :], op=ALU.mult), reads=[b_s2, b_yg], writes=[b_s2])
            P.op("act", lambda e: e.activation(out=s2[:, :], in_=s2[:, :], func=AF.Sigmoid, scale=GC), reads=[b_s2], writes=[b_s2])
            P.op("dve", lambda e: e.tensor_tensor(out=yg[:, :], in0=yg[:, :], in1=s2[:, :], op=ALU.mult), reads=[b_s2, b_yg], writes=[b_yg])
            P.op("dve", lambda e, i=i: e.tensor_tensor(out=ao[i][:, :], in0=yg[:, :], in1=yv[:, :], op=ALU.mult), reads=[b_yg, b_yv], writes=[b_ao[i]])
            P.dma("sp", actT[128 * j:128 * j + 128, :], ao[i][:, :], reads=[b_ao[i]], writes=[b_act])
        while jobs:
            jobs.pop(0)()
        P.stack = old
    barrier(P)

NCORES = 4
DEPTH = 4
_CACHE = {}


def build_program(depth=DEPTH, stop_after=None):
    nc = bass.Bass("TRN2", target_bir_lowering=False)
    st = ExitStack()
    P = Prog(nc, st)
    cst_np = make_consts()
    ext = lambda name, shape: P.dram(name, shape, F32, kind="ExternalInput")
    x = ext("x", [2048, 4096])
    w_in = [ext("w_in%d" % l, [4096, 9272]) for l in range(depth)]
    w_out = [ext("w_out%d" % l, [4096, 4096]) for l in range(depth)]
    w_up = [ext("w_up%d" % l, [4096, 22016]) for l in range(depth)]
    w_down = [ext("w_down%d" % l, [11008, 4096]) for l in range(depth)]
    small = {}
    for name, shape in (("b_forget", [4, 8]), ("s5_a_re", [4, 64, 64]), ("s5_a_im", [4, 64, 64]), ("s5_log_dt", [4, 64]),
                        ("s5_b_re", [4, 64, 64, 16]), ("s5_b_im", [4, 64, 64, 16]), ("s5_c_re", [4, 64, 16, 64]), ("s5_c_im", [4, 64, 16, 64]),
                        ("s5_d", [4, 1024]), ("s5_w_glu", [4, 1024, 1024]), ("cmp_pe_k", [4, 32, 128]), ("cmp_pe_v", [4, 32, 128]),
                        ("cmp_w_k", [4, 32, 128, 128]), ("cmp_w_v", [4, 32, 128, 128]), ("rel_bias", [32, 16]),
                        ("g_out_fox", [4, 1024]), ("g_out_s5", [4, 1024]), ("g_out_nsa", [4, 2048]), ("g_pre_mix", [4, 4096]),
                        ("g_post_mix", [4, 4096]), ("g_pre_ffn", [4, 4096]), ("g_post_ffn", [4, 4096]), ("conv_w", [4, 3, 22016]), ("conv_b", [4, 22016])):
        small[name] = ext(name, shape)
    cst = {k: ext("k_" + k, list(v.shape)) for k, v in cst_np.items()}
    y = P.dram("y", [2048, 4096], F32, kind="ExternalOutput")
    xT = P.dram("s_xT", [4096, 2048], F32); hT = P.dram("s_hT", [4096, 2048], BF16)
    zT = P.dram("s_zT", [9272, 2048], F32); mixT = P.dram("s_mixT", [4096, 2048], F32)
    moT = P.dram("s_moT", [4096, 2048], F32); uT = P.dram("s_uT", [22016, 2048], F32)
    actT = P.dram("s_actT", [11008, 2048], BF16)
    wdbP = P.dram("s_wdbP", [16, 128, 86, 256], BF16); b_wdb = Buf()
    scr = dict(browS=P.dram("s_browS", [16, BROW], F32), b_browS=Buf(), BM=P.dram("s_BM", [16, 13, 128, 512], BF16), b_BM=Buf(),
               cD=P.dram("s_cD", [8, 2048], F32), b_cD=Buf(), gD=P.dram("s_gD", [48, 2048], F32), b_gD=Buf(),
               bbD=P.dram("s_bbD", [2, 64, 64, 16], F32), b_bbD=Buf(), ygD=P.dram("s_ygD", [1024, 2048], F32), b_ygD=Buf())
    b_xT, b_hT, b_zT, b_mix, b_mo, b_u, b_act, b_y = [Buf() for _ in range(8)]
    C = attn_consts(P, cst)
    setup_bias(P, small["rel_bias"], cst, scr)
    transpose_in(P, "ti", x, xT, b_xT, C.identf, C.b)
    for l in range(depth):
        L = "L%d" % l
        norm_phase(P, L + "n1", xT, 0, 4096, small["g_pre_mix"][l], "bf16", hT, 0, b_xT, b_hT, C.onesf, C.b)
        gemm(P, L + "gi", hT, 4096, 2048, w_in[l], 9272, zT, F32, TB=1024, PW=512)
        fox_phase(P, C, L + "fx", zT, mixT, b_mix, small["b_forget"][l], scr)
        prm = dict(a_re=small["s5_a_re"][l], a_im=small["s5_a_im"][l], log_dt=small["s5_log_dt"][l], b_re=small["s5_b_re"][l],
                   b_im=small["s5_b_im"][l], c_re=small["s5_c_re"][l], c_im=small["s5_c_im"][l], d=small["s5_d"][l], w_glu=small["s5_w_glu"][l])
        s5_phase(P, L + "s5", zT, mixT, b_mix, prm, cst, scr)
        nsa_phase(P, C, L + "ns", zT, mixT, b_mix, small["cmp_pe_k"][l], small["cmp_pe_v"][l], small["cmp_w_k"][l], small["cmp_w_v"][l],
                  small["rel_bias"], scr)
        norm_phase(P, L + "na", mixT, 0, 1024, small["g_out_fox"][l], "bf16", hT, 0, b_mix, b_hT, C.onesf, C.b)
        norm_phase(P, L + "nb", mixT, 1024, 1024, small["g_out_s5"][l], "bf16", hT, 1024, b_mix, b_hT, C.onesf, C.b)
        norm_phase(P, L + "nc", mixT, 2048, 2048, small["g_out_nsa"][l], "bf16", hT, 2048, b_mix, b_hT, C.onesf, C.b)
        gemm(P, L + "go", hT, 4096, 2048, w_out[l], 4096, moT, F32, TB=1024, PW=512)
        norm_phase(P, L + "n2", moT, 0, 4096, small["g_post_mix"][l], "resid", xT, 0, b_mo, b_xT, C.onesf, C.b)
        norm_phase(P, L + "n3", xT, 0, 4096, small["g_pre_ffn"][l], "bf16", hT, 0, b_xT, b_hT, C.onesf, C.b)
        gemm(P, L + "gu", hT, 4096, 2048, w_up[l], 22016, uT, F32, TB=1024, PW=512)
        jobs = []
        wdv = w_down[l].rearrange("(c p) n -> p c n", p=128)
        for pn in range(16):
            for k0 in range(0, 86, 8):
                k1 = min(86, k0 + 8)
                jobs.append(lambda pn=pn, k0=k0, k1=k1, wdv=wdv: P.dma("pool", wdbP[pn, :, k0:k1, :], wdv[:, k0:k1, 256 * pn:256 * pn + 256], writes=[b_wdb]))
        conv_phase(P, L + "cv", uT, b_u, actT, b_act, small["conv_w"][l], small["conv_b"][l], jobs=jobs)
        gemm(P, L + "gd", actT, 11008, 2048, None, 4096, moT, F32, TB=512, PW=256, Wpan=wdbP, b_wpan=b_wdb)
        norm_phase(P, L + "n4", moT, 0, 4096, small["g_post_ffn"][l], "resid", xT, 0, b_mo, b_xT, C.onesf, C.b)
    transpose_out(P, "to", xT, b_xT, y, b_y, C.identf, C.b)
    P.finish_wait_all("sp", [b_y])
    P.emit()
    return nc, cst_np, st


def kernel(**inputs):
    if "prog" not in _CACHE:
        _CACHE["prog"] = build_program()
    nc, cst_np, _st = _CACHE["prog"]
    f32 = lambda a: np.ascontiguousarray(np.asarray(a, dtype=np.float32))
    shared = {}
    for l in range(DEPTH):
        shared["w_in%d" % l] = f32(inputs["w_in"][l])
        shared["w_out%d" % l] = f32(inputs["w_out"][l])
        shared["w_up%d" % l] = f32(inputs["w_up"][l])
        shared["w_down%d" % l] = f32(inputs["w_down"][l])
    for name in ("b_forget", "s5_a_re", "s5_a_im", "s5_log_dt", "s5_b_re", "s5_b_im", "s5_c_re", "s5_c_im", "s5_d", "s5_w_glu",
                 "cmp_pe_k", "cmp_pe_v", "cmp_w_k", "cmp_w_v", "rel_bias", "g_out_fox", "g_out_s5", "g_out_nsa", "g_pre_mix",
                 "g_post_mix", "g_pre_ffn", "g_post_ffn", "conv_w", "conv_b"):
        shared[name] = f32(inputs[name])
    for k, v in cst_np.items():
        shared["k_" + k] = v
    xs = np.asarray(inputs["x"], dtype=np.float32)
    in_maps = []
    for b in range(NCORES):
        m = dict(shared)
        m["x"] = np.ascontiguousarray(xs[b])
        in_maps.append(m)
    res = run_bass_kernel_spmd(nc, in_maps, core_ids=list(range(NCORES)))
    return np.stack([np.asarray(res.results[b]["y"], dtype=np.float32) for b in range(NCORES)], axis=0)
```

```python
import math


import numpy as np
from contextlib import ExitStack
import concourse.bass as bass
import concourse.mybir as mybir
from concourse.bass_utils import run_bass_kernel_spmd

F32 = mybir.dt.float32
BF16 = mybir.dt.bfloat16
I32 = mybir.dt.int32
AF = mybir.ActivationFunctionType
ALU = mybir.AluOpType
AX = mybir.AxisListType

ENGS = ("pe", "act", "dve", "pool", "sp")
ND = 8


class Buf:
    __slots__ = ("name", "writers", "readers")

    def __init__(self, name=""):
        self.name = name
        self.writers = {}
        self.readers = {}


class Prog:
    def __init__(self, nc, stack):
        self.nc = nc
        self.stack = stack
        self.ops = {e: [] for e in ENGS}
        self.cnt = {e: 0 for e in ENGS}
        self.seen = {e: {} for e in ENGS}
        self.esem = {e: stack.enter_context(nc.semaphore("c_" + e)) for e in ENGS}
        self.dsem = {e: [stack.enter_context(nc.semaphore("d_%s%d" % (e, i))) for i in range(ND)]
                     for e in ("sp", "pool", "act")}
        self.dcnt = {e: [0] * ND for e in self.dsem}
        self.dnext = {e: 0 for e in self.dsem}
        self.semname = {}
        self.nbuf = 0

    def sb(self, name, shape, dtype):
        t = self.stack.enter_context(self.nc.sbuf_tensor(name, list(shape), dtype))
        return t

    def ps(self, name, shape, dtype=F32):
        t = self.stack.enter_context(self.nc.psum_tensor(name, list(shape), dtype))
        return t

    def dram(self, name, shape, dtype, kind="Internal"):
        return self.nc.dram_tensor(name, list(shape), dtype, kind=kind).ap()

    def _collect(self, eng, reads, writes, same_ok=False):
        need = {}

        def add(sem, val):
            if need.get(id(sem), (None, -1))[1] < val:
                need[id(sem)] = (sem, val)

        for b in reads:
            for sem, val in b.writers.values():
                add(sem, val)
        for b in writes:
            for sem, val in b.writers.values():
                add(sem, val)
            for sem, val in b.readers.values():
                add(sem, val)
        waits = []
        own = self.esem[eng]
        for key, (sem, val) in need.items():
            if same_ok and sem is own:
                continue
            if self.seen[eng].get(key, -1) >= val:
                continue
            self.seen[eng][key] = val
            waits.append((sem, val))
        return waits

    def _commit(self, ev, reads, writes):
        sem, val = ev
        for b in reads:
            b.readers[id(sem)] = (sem, val)
        for b in writes:
            b.writers = {id(sem): (sem, val)}
            b.readers = {}

    def op(self, eng, fn, reads=(), writes=()):
        waits = self._collect(eng, reads, writes, same_ok=(eng == "pe"))
        self.cnt[eng] += 1
        ev = (self.esem[eng], self.cnt[eng])
        self.ops[eng].append((fn, waits, (self.esem[eng], 1)))
        self._commit(ev, reads, writes)

    def dma(self, eng, out, in_, reads=(), writes=(), **kw):
        slot = self.dnext[eng]
        self.dnext[eng] = (slot + 1) % ND
        sem = self.dsem[eng][slot]
        waits = self._collect(eng, reads, writes)
        prev = self.dcnt[eng][slot] * 16
        if prev > 0 and self.seen[eng].get(id(sem), -1) < prev:
            self.seen[eng][id(sem)] = prev
            waits.append((sem, prev))
        self.dcnt[eng][slot] += 1
        ev = (sem, prev + 16)

        def fn(e, out=out, in_=in_, kw=kw):
            return e.dma_start(out=out, in_=in_, **kw)

        self.ops[eng].append((fn, waits, (sem, 16)))
        self._commit(ev, reads, writes)

    def finish_wait_all(self, eng, bufs):
        waits = self._collect(eng, bufs, ())
        self.ops[eng].append((None, waits, None))

    def emit(self):
        nc = self.nc
        with nc.Block() as block:
            def run(e, name):
                for fn, waits, inc in self.ops[name]:
                    for sem, val in waits:
                        e.wait_ge(sem, val)
                    if fn is None:
                        continue
                    ins = fn(e)
                    if inc is not None:
                        ins.then_inc(inc[0], inc[1])

            @block.tensor
            def _(e):
                run(e, "pe")

            @block.scalar
            def _(e):
                run(e, "act")

            @block.vector
            def _(e):
                run(e, "dve")

            @block.gpsimd
            def _(e):
                run(e, "pool")

            @block.sync
            def _(e):
                run(e, "sp")


SCALE = 128 ** -0.5
BIG = 4096.0
PAD = 2048
BROW = 4096

def t5_bucket_np(dist):
    n = np.maximum(dist, 0)
    nf = np.maximum(n, 1).astype(np.float32)
    large = 16 + (np.log(nf / np.float32(16)) / np.float32(math.log(128 / 16)) * np.float32(16)).astype(np.int32)
    return np.where(n < 16, n, np.minimum(large, 31))

def make_consts():
    c = {}
    c["ident"] = np.eye(128, dtype=np.float32)
    d = np.arange(BROW) - PAD
    bk = t5_bucket_np(d)
    oh = np.zeros((32, BROW), np.float32)
    oh[bk, np.arange(BROW)] = 1.0
    c["onehot"] = oh
    m = np.zeros((17, 128, 512), np.float32)
    s = np.arange(128)[:, None]; q = np.arange(512)[None, :]
    for k, r in enumerate(range(-4, 4)):
        dist = q - s - 128 * r
        m[k] = np.where((dist >= 0) & (dist < 512), 0.0, -BIG)
    for j in range(4):
        ok = (16 * s + 31) <= (512 * j + q)
        m[9 + j] = np.where(ok, 0.0, -BIG)
        m[9 + j][127] = -BIG
    for r in range(4):
        m[13 + r] = np.where(q - s - 128 * r >= 0, 0.0, -BIG)
    for k in range(9):
        m[k] = m[k][::-1].copy()
    for k in range(9, 13):
        m[k][:127] = m[k][:127][::-1].copy()
    c["masks"] = m
    c["jrev"] = np.eye(128, dtype=np.float32)[::-1].copy()
    j127 = np.zeros((128, 128), np.float32); j127[:127, :127] = np.eye(127, dtype=np.float32)[::-1]
    c["jrev127"] = j127
    bs = np.arange(127) * 16; be = bs + 31
    ss = np.arange(32) * 64
    ov = ((bs[:, None] <= ss[None, :] + 63) & (be[:, None] >= ss[None, :])).astype(np.float32)
    ovp = np.zeros((128, 32), np.float32); ovp[:127] = ov
    c["overlap"] = ovp
    E = np.zeros((32, 2048), np.float32)
    E[np.arange(2048) // 64, np.arange(2048)] = 1.0
    c["expand"] = E
    tq = np.arange(2048)[:, None]; jb = np.arange(32)[None, :]
    cur = tq // 64
    causal = jb * 64 <= tq
    forced = ((jb == 0) | (jb == cur) | (jb == cur - 1)).astype(np.float32)
    c["scorec"] = np.where(causal, 1000.0 * forced, -1e30).astype(np.float32)
    mB = np.zeros((128, 4, 128), np.float32)
    for gl in range(8):
        for g2 in range(2):
            for pr4 in range(4):
                if gl == 2 * pr4 + g2:
                    mB[16 * gl:16 * gl + 16, pr4, 64 * g2:64 * g2 + 64] = 1.0
    c["maskB"] = mB
    c["maskC"] = np.ascontiguousarray(mB.transpose(2, 1, 0))
    c["iota"] = np.arange(1024, dtype=np.float32)[None, :]
    return c


def barrier(P):
    evs = []
    for e in ENGS:
        if P.cnt[e] > 0:
            evs.append((P.esem[e], P.cnt[e]))
    for e in P.dsem:
        for i in range(ND):
            if P.dcnt[e][i] > 0:
                evs.append((P.dsem[e][i], P.dcnt[e][i] * 16))
    for e in ENGS:
        waits = []
        for sem, val in evs:
            if sem is P.esem[e]:
                continue
            if P.seen[e].get(id(sem), -1) >= val:
                continue
            P.seen[e][id(sem)] = val
            waits.append((sem, val))
        if waits:
            P.ops[e].append((None, waits, None))

def gemm(P, tag, actT, K, T, W, N, outT, out_dtype, TB=1024, PW=512, epi=None, Wpan=None, b_wpan=None):
    nc = P.nc
    KC = K // 128
    assert K % 128 == 0 and T % TB == 0 and TB % 512 == 0
    NTS = TB // 512
    with ExitStack() as st:
        old = P.stack; P.stack = st
        act = P.sb(tag + "_act", [128, KC, TB], BF16)
        wts = [P.sb(tag + "_w%d" % i, [128, KC, PW], BF16) for i in range(2)]
        osb = [P.sb(tag + "_o%d" % i, [128, TB], out_dtype) for i in range(2)]
        pss = [P.ps(tag + "_p%d" % i, [128, TB], F32) for i in range(2)]
        b_act = Buf(); b_w = [Buf(), Buf()]; b_o = [Buf(), Buf()]; b_p = [Buf(), Buf()]
        actv = actT.rearrange("(c p) t -> p c t", p=128)
        Wv = W.rearrange("(c p) n -> p c n", p=128) if W is not None else None
        it = 0
        npan = (N + PW - 1) // PW
        KG = 8
        for tb in range(T // TB):
            t0 = tb * TB
            for k0 in range(0, KC, KG):
                k1 = min(KC, k0 + KG)
                P.dma("sp", act[:, k0:k1, :], actv[:, k0:k1, t0:t0 + TB], writes=[b_act])
            for pn in range(npan):
                n0 = pn * PW
                pw = min(PW, N - n0)
                wb = pn % 2
                if Wpan is not None:
                    for k0 in range(0, KC, 32):
                        k1 = min(KC, k0 + 32)
                        P.dma("act", wts[wb][:, k0:k1, :pw], Wpan[pn, :, k0:k1, :pw], reads=[b_wpan], writes=[b_w[wb]])
                else:
                    for k0 in range(0, KC, KG):
                        k1 = min(KC, k0 + KG)
                        P.dma("pool", wts[wb][:, k0:k1, :pw], Wv[:, k0:k1, n0:n0 + pw], writes=[b_w[wb]])
                for c0 in range(0, pw, 128):
                    m = min(128, pw - c0)
                    pb = it % 2; it += 1
                    def mm(e, wb=wb, c0=c0, m=m, pb=pb):
                        ins = None
                        for ts in range(NTS):
                            for k in range(KC):
                                ins = e.matmul(pss[pb][:m, ts * 512:(ts + 1) * 512], lhsT=wts[wb][:, k, c0:c0 + m],
                                               rhs=act[:, k, ts * 512:(ts + 1) * 512], start=(k == 0), stop=(k == KC - 1))
                        return ins
                    P.op("pe", mm, reads=[b_act, b_w[wb]], writes=[b_p[pb]])
                    ob = pb
                    if epi is None:
                        if pb == 0:
                            P.op("act", lambda e, m=m, pb=pb, ob=ob: e.copy(out=osb[ob][:m, :], in_=pss[pb][:m, :]),
                                 reads=[b_p[pb]], writes=[b_o[ob]])
                        else:
                            P.op("dve", lambda e, m=m, pb=pb, ob=ob: e.tensor_copy(out=osb[ob][:m, :], in_=pss[pb][:m, :]),
                                 reads=[b_p[pb]], writes=[b_o[ob]])
                    else:
                        epi(P, pss[pb], b_p[pb], osb[ob], b_o[ob], n0 + c0, m, t0, TB)
                    P.dma("sp", outT[n0 + c0:n0 + c0 + m, t0:t0 + TB], osb[ob][:m, :], reads=[b_o[ob]])
        P.stack = old
    barrier(P)


def dsl(j, n=512):
    return slice(j * n, (j + 1) * n)

class AttnCtx:
    pass

def setup_bias(P, relb, cst, scr):
    nc = P.nc
    with ExitStack() as st:
        old = P.stack; P.stack = st
        rb = P.sb("sb_rb", [32, 16], F32); oh = P.sb("sb_oh", [32, BROW], F32)
        brow = P.sb("sb_brow", [16, BROW], F32)
        msk = P.sb("sb_msk", [128, 13, 512], F32)
        toe = [P.sb("sb_toe%d" % i, [128, 512], F32) for i in range(2)]
        bmo = [P.sb("sb_bmo%d" % i, [128, 512], BF16) for i in range(2)]
        ps = P.ps("sb_ps", [16, 512], F32)
        b_rb, b_oh, b_brow, b_msk, b_ps = Buf(), Buf(), Buf(), Buf(), Buf()
        b_toe = [Buf(), Buf()]; b_bmo = [Buf(), Buf()]
        P.dma("sp", rb[:, :], relb[:, :], writes=[b_rb])
        P.dma("sp", oh[:, :], cst["onehot"][:, :], writes=[b_oh])
        for k0 in range(0, 13, 4):
            k1 = min(13, k0 + 4)
            P.dma("sp", msk[:, k0:k1, :], cst["masks"][k0:k1].rearrange("k p q -> p k q"), writes=[b_msk])
        for c in range(BROW // 512):
            P.op("pe", lambda e, c=c: e.matmul(ps[:, :], lhsT=rb[:, :], rhs=oh[:, dsl(c)], start=True, stop=True),
                 reads=[b_rb, b_oh], writes=[b_ps])
            P.op("act", lambda e, c=c:
                 e.activation(out=brow[:, dsl(c)], in_=ps[:, :], func=AF.Copy, scale=1.0 / SCALE),
                 reads=[b_ps], writes=[b_brow])
        b_browD = scr["b_browS"]
        P.dma("sp", scr["browS"][:, :], brow[:, :], reads=[b_brow], writes=[b_browD])
        it = 0
        bt = scr["browS"].tensor
        for h in range(16):
            for k in range(13):
                i = it % 2; it += 1
                if k < 8:
                    off = h * BROW + PAD - 127 - 128 * (k - 4); apat = [[1, 128], [1, 512]]; np_ = 128
                elif k == 8:
                    off = h * BROW + PAD - 127 + 128; apat = [[1, 128], [1, 512]]; np_ = 128
                else:
                    off = h * BROW + PAD + 512 * (k - 9) - 2047; apat = [[16, 127], [1, 512]]; np_ = 127
                src = bass.AP(bt, off, apat)
                P.dma("sp", toe[i][:np_, :], src, reads=[b_browD], writes=[b_toe[i]], allow_slow_non_contiguous=False)
                P.op("dve", lambda e, i=i, k=k, np_=np_: e.tensor_tensor(out=bmo[i][:np_, :], in0=toe[i][:np_, :], in1=msk[:np_, k, :], op=ALU.add),
                     reads=[b_toe[i], b_msk], writes=[b_bmo[i]])
                P.dma("sp", scr["BM"][h, k, :np_, :], bmo[i][:np_, :], reads=[b_bmo[i]], writes=[scr["b_BM"]])
        P.stack = old
    barrier(P)


def attn_consts(P, cst):
    C = AttnCtx()
    C.identf = P.sb("c_identf", [128, 128], F32)
    C.identb = P.sb("c_identb", [128, 128], BF16)
    C.onesb = P.sb("c_onesb", [128, 128], BF16)
    C.onesf = P.sb("c_onesf", [128, 128], F32)
    C.ov = P.sb("c_ov", [128, 32], F32)
    C.Eb = P.sb("c_E", [32, 2048], BF16)
    C.scorec = P.sb("c_scorec", [128, 16, 32], F32)
    C.foxm = P.sb("c_foxm", [128, 4, 512], BF16)
    C.b = Buf()
    C.jrevb = P.sb("c_jrevb", [128, 128], BF16)
    C.jrev127b = P.sb("c_jrev127b", [128, 128], BF16)
    P.dma("pool", C.jrevb[:, :], cst["jrev"][:, :], writes=[C.b])
    P.dma("pool", C.jrev127b[:, :], cst["jrev127"][:, :], writes=[C.b])
    P.dma("sp", C.identf[:, :], cst["ident"][:, :], writes=[C.b])
    P.dma("pool", C.identb[:, :], cst["ident"][:, :], writes=[C.b])
    P.dma("pool", C.Eb[:, :], cst["expand"][:, :], writes=[C.b])
    P.dma("sp", C.ov[:, :], cst["overlap"][:, :], writes=[C.b])
    P.dma("sp", C.scorec[:, :, :], cst["scorec"].rearrange("(c p) j -> p c j", p=128), writes=[C.b])
    P.dma("pool", C.foxm[:, :, :], cst["masks"][13:17].rearrange("k p q -> p k q"), writes=[C.b])
    P.op("dve", lambda e: e.memset(C.onesb[:, :], 1.0), writes=[C.b])
    P.op("dve", lambda e: e.memset(C.onesf[:, :], 1.0), writes=[C.b])
    return C


class AttnRes:
    def __init__(self, P, tag):
        self.ps_s = [P.ps(tag + "_pss%d" % i, [128, 512], F32) for i in range(3)]
        self.ps_o = [P.ps(tag + "_pso%d" % i, [128, 512], F32) for i in range(2)]
        self.ps_d = [P.ps(tag + "_psd%d" % i, [128, 512], F32) for i in range(2)]
        self.ps_m = P.ps(tag + "_psm", [128, 512], F32)
        self.ps_t = self.ps_m[:, 0:256].bitcast(BF16)
        self.b_s = [Buf(), Buf(), Buf()]; self.b_o = [Buf(), Buf()]; self.b_d = [Buf(), Buf()]
        self.b_m = Buf(); self.b_t = self.b_m
        self.pt = [P.sb(tag + "_pt%d" % i, [128, 512], BF16) for i in range(4)]
        self.b_pt = [Buf() for _ in range(4)]
        self.tmp = [P.sb(tag + "_tmp%d" % i, [128, 512], F32) for i in range(2)]
        self.b_tmp = [Buf(), Buf()]
        self.rd = [P.sb(tag + "_rd%d" % i, [128, 512], F32) for i in range(2)]
        self.b_rd = [Buf(), Buf()]
        self.t1 = [P.sb(tag + "_t1%d" % i, [128, 512], F32) for i in range(2)]
        self.b_t1 = [Buf(), Buf()]
        self.si = 0; self.pi = 0; self.oi = 0; self.ti = 0


def attn_qblock(P, C, R, QT, b_q, j, tiles, epilogue):
    oi = R.oi % 2; R.oi += 1
    n = len(tiles)
    pend = []

    def emit_s(t):
        si = R.si % 3; R.si += 1
        m = t["m"]
        ex = t.get("extra", [])
        def f(e, t=t, si=si, m=m, ex=ex):
            ins = e.matmul(R.ps_s[si][:m, :], lhsT=t["KT"], rhs=QT[:, dsl(j)], start=True, stop=(len(ex) == 0))
            for q, (l, r, _) in enumerate(ex):
                ins = e.matmul(R.ps_s[si][:m, :], lhsT=l, rhs=r, start=False, stop=(q == len(ex) - 1))
            return ins
        rd = [b_q, t["bK"]] + [b for (_, _, bs) in ex for b in bs]
        P.op("pe", f, reads=rd, writes=[R.b_s[si]])
        pi = R.pi % 4; R.pi += 1
        bias = t.get("bias")
        if t.get("fox") is not None:
            cbc, bcbc = t["fox"]
            ti = R.ti % 2; R.ti += 1
            P.op("dve", lambda e, si=si, ti=ti, m=m, cbc=cbc: e.scalar_tensor_tensor(
                out=R.tmp[ti][:m, :], in0=R.ps_s[si][:m, :], scalar=SCALE, in1=cbc[:m, dsl(j)], op0=ALU.mult, op1=ALU.subtract),
                reads=[R.b_s[si], bcbc], writes=[R.b_tmp[ti]])
            P.op("act", lambda e, ti=ti, pi=pi, m=m, bias=bias: e.activation(
                out=R.pt[pi][:m, :], in_=R.tmp[ti][:m, :], func=AF.Exp, bias=bias, scale=1.0),
                reads=[R.b_tmp[ti], t["bbias"]], writes=[R.b_pt[pi]])
        else:
            kw = {}
            rds = [R.b_s[si]]
            if bias is not None:
                kw["bias"] = bias; rds.append(t["bbias"])
            P.op("act", lambda e, si=si, pi=pi, m=m, kw=kw: e.activation(
                out=R.pt[pi][:m, :], in_=R.ps_s[si][:m, :], func=AF.Exp, scale=SCALE, **kw),
                reads=rds, writes=[R.b_pt[pi]])
        return pi

    def emit_pv(t, pi, first, last):
        m = t["m"]
        def f(e, t=t, pi=pi, m=m):
            e.matmul(R.ps_o[oi][:, :], lhsT=t["V"], rhs=R.pt[pi][:m, :], start=first, stop=last)
            return e.matmul(R.ps_d[oi][:, :], lhsT=C.onesb[:m, :], rhs=R.pt[pi][:m, :], start=first, stop=last)
        P.op("pe", f, reads=[t["bV"], R.b_pt[pi], C.b], writes=[R.b_o[oi], R.b_d[oi]])

    pis = []
    LA = 2
    done = 0
    for q, t in enumerate(tiles):
        pis.append(emit_s(t))
        if q >= LA:
            emit_pv(tiles[done], pis[done], done == 0, done == n - 1)
            done += 1
    while done < n:
        emit_pv(tiles[done], pis[done], done == 0, done == n - 1)
        done += 1
    epilogue(oi, pis)


def load_rows_bf16(P, dst, zT, r0, buf):
    P.dma("pool", dst[:, :], zT[r0:r0 + 128, :], writes=[buf])


def make_V(P, C, R, VT, b_vt, V, b_v, nchunks=16):
    for g4 in range(0, nchunks, 4):
        def f(e, g4=g4):
            ins = None
            for q in range(4):
                ins = e.transpose(out=R.ps_t[:, q * 128:(q + 1) * 128], in_=VT[:, (g4 + q) * 128:(g4 + q + 1) * 128], identity=C.identb[:, :])
            return ins
        P.op("pe", f, reads=[b_vt, C.b], writes=[R.b_t])
        P.op("dve", lambda e, g4=g4: e.tensor_copy(out=V[:, g4:g4 + 4, :], in_=R.ps_t[:, :].rearrange("p (q d) -> p q d", q=4)),
             reads=[R.b_t], writes=[b_v])


def fox_phase(P, C, tag, zT, mixT, b_mix, bfor, scr):
    nc = P.nc
    with ExitStack() as st:
        old = P.stack; P.stack = st
        R = AttnRes(P, tag)
        zf = P.sb(tag + "_zf", [8, 2048], F32); ee = P.sb(tag + "_ee", [8, 2048], F32)
        ll = P.sb(tag + "_ll", [8, 2048], F32); cp = P.sb(tag + "_cp", [8, 2048], F32)
        on8 = P.sb(tag + "_on8", [8, 2048], F32)
        bneg = P.sb(tag + "_bn", [8, 1], F32); bf = P.sb(tag + "_bf", [8, 1], F32)
        ccol = P.sb(tag + "_ccol", [128, 8, 16], F32)
        cbc = [P.sb(tag + "_cbc%d" % i, [128, 2048], F32) for i in range(2)]
        QT = [P.sb(tag + "_q%d" % i, [128, 2048], BF16) for i in range(2)]
        KT = [P.sb(tag + "_k%d" % i, [128, 2048], BF16) for i in range(2)]
        VT = [P.sb(tag + "_vt%d" % i, [128, 2048], BF16) for i in range(2)]
        V = [P.sb(tag + "_v%d" % i, [128, 16, 128], BF16) for i in range(2)]
        ob = [P.sb(tag + "_ob%d" % i, [128, 512], F32) for i in range(2)]
        b_ob = [Buf(), Buf()]
        b_zf, b_e, b_l, b_cp, b_on, b_bn, b_bf, b_ccol = [Buf() for _ in range(8)]
        b_cbc = [Buf(), Buf()]; b_q = [Buf(), Buf()]; b_k = [Buf(), Buf()]; b_vt = [Buf(), Buf()]; b_v = [Buf(), Buf()]
        b_cD = scr["b_cD"]
        P.dma("sp", zf[:, :], zT[3072:3080, :], writes=[b_zf])
        P.dma("sp", bf[:, :], bfor.rearrange("(h o) -> h o", o=1), writes=[b_bf])
        P.op("dve", lambda e: e.memset(on8[:, :], 1.0), writes=[b_on])
        P.op("act", lambda e: e.mul(out=bneg[:, :], in_=bf[:, :], mul=-1.0), reads=[b_bf], writes=[b_bn])
        P.op("act", lambda e: e.activation(out=ee[:, :], in_=zf[:, :], func=AF.Exp, bias=bneg[:, :], scale=-1.0),
             reads=[b_zf, b_bn], writes=[b_e])
        P.op("act", lambda e: e.activation(out=ll[:, :], in_=ee[:, :], func=AF.Ln, bias=1.0, scale=1.0),
             reads=[b_e], writes=[b_l])
        P.op("dve", lambda e: e.tensor_tensor_scan(out=cp[:, :], data0=on8[:, :], data1=ll[:, :], initial=0.0, op0=ALU.mult, op1=ALU.add),
             reads=[b_on, b_l], writes=[b_cp])
        P.dma("sp", scr["cD"][:, :], cp[:, :], reads=[b_cp], writes=[b_cD])
        P.dma("sp", ccol[:, :, :], scr["cD"].rearrange("h (c p) -> p h c", p=128), reads=[b_cD], writes=[b_ccol], allow_slow_non_contiguous=True)
        oc = 0
        for h in range(8):
            i = h % 2
            P.dma("sp", cbc[i][:, :], scr["cD"][h:h + 1, :].partition_broadcast(128), reads=[b_cD], writes=[b_cbc[i]])
            load_rows_bf16(P, QT[i], zT, 128 * h, b_q[i])
            load_rows_bf16(P, KT[i], zT, 1024 + 128 * h, b_k[i])
            load_rows_bf16(P, VT[i], zT, 2048 + 128 * h, b_vt[i])
            make_V(P, C, R, VT[i], b_vt[i], V[i], b_v[i])
            for j in range(4):
                tiles = []
                for kc in range(4 * j + 4):
                    t = dict(KT=KT[i][:, kc * 128:(kc + 1) * 128], bK=b_k[i], V=V[i][:, kc, :], bV=b_v[i], m=128,
                             fox=(cbc[i], b_cbc[i]), bias=ccol[:, h, kc:kc + 1], bbias=b_ccol)
                    r = kc - 4 * j
                    if r >= 0:
                        t["extra"] = [(C.identb[:, :], C.foxm[:, r, :], [C.b])]
                    tiles.append(t)
                def epi(oi, pis, h=h, j=j):
                    nonlocal oc
                    o = oc % 2; oc += 1
                    P.op("dve", lambda e, oi=oi, o=o: e.reciprocal(out=R.rd[o][:, :], in_=R.ps_d[oi][:, :]), reads=[R.b_d[oi]], writes=[R.b_rd[o]])
                    P.op("dve", lambda e, oi=oi, o=o: e.tensor_tensor(out=ob[o][:, :], in0=R.ps_o[oi][:, :], in1=R.rd[o][:, :], op=ALU.mult),
                         reads=[R.b_o[oi], R.b_rd[o]], writes=[b_ob[o]])
                    P.dma("sp", mixT[128 * h:128 * h + 128, dsl(j)], ob[o][:, :], reads=[b_ob[o]], writes=[b_mix])
                attn_qblock(P, C, R, QT[i], b_q[i], j, tiles, epi)
        P.stack = old
    barrier(P)


def nsa_phase(P, C, tag, zT, mixT, b_mix, pek, pev, wk, wv, relb, scr):
    Q0, KC0, VC0, KS0, VS0, KW0, VW0, ZG0 = 4104, 6152, 6664, 7176, 7688, 8200, 8712, 9224
    with ExitStack() as st:
        old = P.stack; P.stack = st
        zg = P.sb(tag + "_zg", [48, 2048], F32); gs = P.sb(tag + "_gs", [48, 2048], F32)
        b_zg, b_gs = Buf(), Buf()
        P.dma("sp", zg[:, :], zT[ZG0:ZG0 + 48, :], writes=[b_zg])
        P.op("act", lambda e: e.activation(out=gs[:, :], in_=zg[:, :], func=AF.Sigmoid), reads=[b_zg], writes=[b_gs])
        P.dma("sp", scr["gD"][:, :], gs[:, :], reads=[b_gs], writes=[scr["b_gD"]])
        P.stack = old
    barrier(P)
    with ExitStack() as st:
        old = P.stack; P.stack = st
        R = AttnRes(P, tag)
        cbias = P.sb(tag + "_cbias", [128, 16], F32); b_cb = Buf()
        P.dma("sp", cbias[:, :], relb[31:32, :].partition_broadcast(128), writes=[b_cb])
        wck = P.sb(tag + "_wck", [128, 32, 128], BF16); wcv = P.sb(tag + "_wcv", [128, 32, 128], BF16)
        peT = P.sb(tag + "_peT", [128, 2, 32], F32)
        b_wc, b_pe = Buf(), Buf()
        for l0 in range(0, 32, 8):
            P.dma("pool", wck[:, l0:l0 + 8, :], wk[l0:l0 + 8].rearrange("l d e -> d l e"), writes=[b_wc])
            P.dma("pool", wcv[:, l0:l0 + 8, :], wv[l0:l0 + 8].rearrange("l d e -> d l e"), writes=[b_wc])
        P.dma("sp", peT[:, 0, :], pek.rearrange("l d -> d l"), writes=[b_pe], allow_slow_non_contiguous=True)
        P.dma("sp", peT[:, 1, :], pev.rearrange("l d -> d l"), writes=[b_pe], allow_slow_non_contiguous=True)
        names = ["kc", "vc", "ks", "vs", "kw", "vw"]
        row0 = dict(kc=KC0, vc=VC0, ks=KS0, vs=VS0, kw=KW0, vw=VW0)
        XT = {n: P.sb(tag + "_x" + n, [128, 2048], BF16) for n in names}
        b_x = {n: Buf() for n in names}
        XA = {n: P.sb(tag + "_a" + n, [128, 2048], BF16) for n in ("kc", "vc")}
        XB = {n: P.sb(tag + "_b" + n, [128, 2048], BF16) for n in ("kc", "vc")}
        b_xa = {n: Buf() for n in XA}
        Vs = P.sb(tag + "_Vs", [128, 16, 128], BF16); Vw = P.sb(tag + "_Vw", [128, 16, 128], BF16)
        b_Vs, b_Vw = Buf(), Buf()
        kcT = P.sb(tag + "_kcT", [128, 128], BF16); vcm = P.sb(tag + "_vcm", [128, 128], BF16)
        b_kcT, b_vcm = Buf(), Buf()
        QT = [P.sb(tag + "_q%d" % r, [128, 2048], BF16) for r in range(4)]
        b_q = [Buf() for _ in range(4)]
        Pn = [P.sb(tag + "_pn%d" % r, [128, 2048], F32) for r in range(4)]
        b_pn = [Buf() for _ in range(4)]
        acc = [P.sb(tag + "_acc%d" % r, [128, 2048], F32) for r in range(4)]
        b_acc = [Buf() for _ in range(4)]
        XF = {"kc": acc[0], "vc": acc[1]}
        b_xf = {"kc": b_acc[0], "vc": b_acc[1]}
        _bm = P.sb(tag + "_bm", [128, 13, 512], BF16); _bbm = Buf()
        BMh = [_bm, _bm]
        b_bm = [_bbm, _bbm]
        gbc = [P.sb(tag + "_gbc%d" % i, [128, 2048], F32) for i in range(2)]
        b_gbc = [Buf(), Buf()]
        selmT = P.sb(tag + "_selmT", [32, 2048], BF16); b_selmT = Buf()
        sc = P.sb(tag + "_sc", [128, 32], F32); sc2 = P.sb(tag + "_sc2", [128, 32], F32)
        m1 = P.sb(tag + "_m1", [128, 8], F32); m2 = P.sb(tag + "_m2", [128, 8], F32)
        selm = P.sb(tag + "_selm", [128, 32], F32)
        b_sc, b_sc2, b_m1, b_m2, b_selm = [Buf() for _ in range(5)]
        gi = [0]; bmi = [0]

        def branch_epilogue(r, hh, br, j, first):
            def epi(oi, pis):
                o = R.oi % 2
                P.op("dve", lambda e, oi=oi, o=o: e.tensor_scalar_max(out=R.rd[o][:, :], in0=R.ps_d[oi][:, :], scalar1=1e-30),
                     reads=[R.b_d[oi]], writes=[R.b_rd[o]])
                P.op("dve", lambda e, o=o: e.reciprocal(out=R.rd[o][:, :], in_=R.rd[o][:, :]), reads=[R.b_rd[o]], writes=[R.b_rd[o]])
                if br == 0:
                    pi = pis[0]
                    P.op("dve", lambda e, o=o, pi=pi: e.tensor_tensor(out=Pn[r][:127, dsl(j)], in0=R.pt[pi][:127, :], in1=R.rd[o][:127, :], op=ALU.mult),
                         reads=[R.b_pt[pi], R.b_rd[o]], writes=[b_pn[r]])
                P.op("dve", lambda e, oi=oi, o=o: e.tensor_tensor(out=R.t1[o][:, :], in0=R.ps_o[oi][:, :], in1=R.rd[o][:, :], op=ALU.mult),
                     reads=[R.b_o[oi], R.b_rd[o]], writes=[R.b_t1[o]])
                g = gcur[0]
                if first:
                    P.op("dve", lambda e, o=o, g=g: e.tensor_tensor(out=acc[r][:, dsl(j)], in0=R.t1[o][:, :], in1=gbc[g][:, dsl(j)], op=ALU.mult),
                         reads=[R.b_t1[o], b_gbc[g]], writes=[b_acc[r]])
                else:
                    P.op("dve", lambda e, o=o, g=g: e.tensor_tensor(out=R.t1[o][:, :], in0=R.t1[o][:, :], in1=gbc[g][:, dsl(j)], op=ALU.mult),
                         reads=[R.b_t1[o], b_gbc[g]], writes=[R.b_t1[o]])
                    P.op("dve", lambda e, o=o: e.tensor_tensor(out=acc[r][:, dsl(j)], in0=acc[r][:, dsl(j)], in1=R.t1[o][:, :], op=ALU.add),
                         reads=[R.b_t1[o], b_acc[r]], writes=[b_acc[r]])
            return epi

        gcur = [0]

        def load_gate(hh, br):
            g = gi[0] % 2; gi[0] += 1
            gcur[0] = g
            row = hh * 3 + br
            P.dma("sp", gbc[g][:, :], scr["gD"][row:row + 1, :].partition_broadcast(128), reads=[scr["b_gD"]], writes=[b_gbc[g]])

        def load_bm(hh):
            i = bmi[0] % 2; bmi[0] += 1
            P.dma("sp", BMh[i][:, 0:9, :], scr["BM"][hh, 0:9].rearrange("k p q -> p k q"), reads=[scr["b_BM"]], writes=[b_bm[i]])
            P.dma("sp", BMh[i][:127, 9:13, :], scr["BM"][hh, 9:13, :127].rearrange("k p q -> p k q"), reads=[scr["b_BM"]], writes=[b_bm[i]])
            return i

        for g in range(4):
            for n in names:
                load_rows_bf16(P, XT[n], zT, row0[n] + 128 * g, b_x[n])
            for n in ("kc", "vc"):
                P.dma("sp", XF[n][:, :], zT[row0[n] + 128 * g: row0[n] + 128 * g + 128, :], writes=[b_xf[n]])
            for r in range(4):
                load_rows_bf16(P, QT[r], zT, Q0 + 128 * (4 * g + r), b_q[r])
            make_V(P, C, R, XT["vs"], b_x["vs"], Vs, b_Vs)
            make_V(P, C, R, XT["vw"], b_x["vw"], Vw, b_Vw)
            for qi, n in enumerate(("kc", "vc")):
                P.op("dve", lambda e, n=n, qi=qi: e.tensor_tensor(
                    out=XA[n][:, :].rearrange("p (c l) -> p c l", l=16), in0=XF[n][:, :].rearrange("p (c l) -> p c l", l=16),
                    in1=peT[:, qi, 0:16].unsqueeze(1).to_broadcast([128, 128, 16]), op=ALU.add),
                    reads=[b_xf[n], b_pe], writes=[b_xa[n]])
                P.op("dve", lambda e, n=n, qi=qi: e.tensor_tensor(
                    out=XB[n][:, :].rearrange("p (c l) -> p c l", l=16), in0=XF[n][:, :].rearrange("p (c l) -> p c l", l=16),
                    in1=peT[:, qi, 16:32].unsqueeze(1).to_broadcast([128, 128, 16]), op=ALU.add),
                    reads=[b_xf[n], b_pe], writes=[b_xa[n]])
            def f_kc(e):
                ins = None
                for l in range(32):
                    X = XA["kc"] if l < 16 else XB["kc"]
                    ins = e.matmul(R.ps_m[:, 0:127], lhsT=wck[:, l, :], rhs=X[:, l:l + 16 * 126 + 1:16], start=(l == 0), stop=(l == 31))
                return ins
            P.op("pe", f_kc, reads=[b_wc, b_xa["kc"]], writes=[R.b_m])
            P.op("dve", lambda e: e.memset(kcT[:, :], 0.0), writes=[b_kcT])
            P.op("dve", lambda e: e.tensor_copy(out=kcT[:, 0:127], in_=R.ps_m[:, 0:127]), reads=[R.b_m], writes=[b_kcT])
            def f_vc(e):
                ins = None
                for l in range(32):
                    X = XA["vc"] if l < 16 else XB["vc"]
                    ins = e.matmul(R.ps_m[:127, 0:128], lhsT=X[:, l:l + 16 * 126 + 1:16], rhs=wcv[:, l, :], start=(l == 0), stop=(l == 31))
                return ins
            P.op("pe", f_vc, reads=[b_wc, b_xa["vc"]], writes=[R.b_m])
            P.op("dve", lambda e: e.memset(vcm[:, :], 0.0), writes=[b_vcm])
            P.op("dve", lambda e: e.tensor_copy(out=vcm[:127, :], in_=R.ps_m[:127, 0:128]), reads=[R.b_m], writes=[b_vcm])
            bmsel = {}
            for r in range(4):
                hh = 4 * g + r
                bi = load_bm(hh); bmsel[r] = bi
                load_gate(hh, 0)
                for j in range(4):
                    t = dict(KT=kcT[:, 0:127], bK=b_kcT, V=vcm[:127, :], bV=b_vcm, m=127,
                             extra=[(C.jrev127b[:127, :127], BMh[bi][:127, 9 + j, :], [C.b, b_bm[bi]])])
                    attn_qblock(P, C, R, QT[r], b_q[r], j, [t], branch_epilogue(r, hh, 0, j, True))
                if r % 2 == 1 and r < 3:
                    pass
            for tc in range(16):
                def f_imp(e, tc=tc):
                    ins = None
                    for r in range(4):
                        ins = e.matmul(R.ps_m[:, 0:32], lhsT=Pn[r][:127, tc * 128:(tc + 1) * 128], rhs=C.ov[:127, :], start=(r == 0), stop=(r == 3))
                    return ins
                P.op("pe", f_imp, reads=b_pn + [C.b], writes=[R.b_m])
                P.op("dve", lambda e, tc=tc: e.tensor_tensor(out=sc[:, :], in0=R.ps_m[:, 0:32], in1=C.scorec[:, tc, :], op=ALU.add),
                     reads=[R.b_m, C.b], writes=[b_sc])
                P.op("dve", lambda e: e.max(out=m1[:, :], in_=sc[:, :]), reads=[b_sc], writes=[b_m1])
                P.op("dve", lambda e: e.match_replace(out=sc2[:, :], in_to_replace=m1[:, :], in_values=sc[:, :], imm_value=-3.0e38),
                     reads=[b_sc, b_m1], writes=[b_sc2])
                P.op("dve", lambda e: e.max(out=m2[:, :], in_=sc2[:, :]), reads=[b_sc2], writes=[b_m2])
                P.op("dve", lambda e: e.tensor_scalar(out=selm[:, :], in0=sc[:, :], scalar1=m2[:, 7:8], scalar2=None, op0=ALU.is_ge),
                     reads=[b_sc, b_m2], writes=[b_selm])
                P.op("dve", lambda e: e.tensor_scalar(out=selm[:, :], in0=selm[:, :], scalar1=1.0, scalar2=BIG, op0=ALU.subtract, op1=ALU.mult),
                     reads=[b_selm], writes=[b_selm])
                P.op("pe", lambda e: e.transpose(out=R.ps_m[:32, 128:256], in_=selm[:, :], identity=C.identf[:, :]),
                     reads=[b_selm, C.b], writes=[R.b_m])
                P.op("dve", lambda e, tc=tc: e.tensor_copy(out=selmT[:, tc * 128:(tc + 1) * 128], in_=R.ps_m[:32, 128:256]),
                     reads=[R.b_m], writes=[b_selmT])
            for r in range(4):
                hh = 4 * g + r
                bi = load_bm(hh)
                load_gate(hh, 1)
                for j in range(4):
                    tiles = []
                    for kc in range(4 * j + 4):
                        rr = kc - 4 * j
                        ex = [(C.Eb[:, kc * 128:(kc + 1) * 128], selmT[:, dsl(j)], [C.b, b_selmT])]
                        t = dict(KT=XT["ks"][:, kc * 128:(kc + 1) * 128], bK=b_x["ks"], V=Vs[:, kc, :], bV=b_Vs, m=128)
                        if rr >= -1:
                            kidx = 8 if rr == -1 else rr + 4
                            ex.append((C.jrevb[:, :], BMh[bi][:, kidx, :], [C.b, b_bm[bi]]))
                        else:
                            t["bias"] = cbias[:, hh:hh + 1]; t["bbias"] = b_cb
                        t["extra"] = ex
                        tiles.append(t)
                    attn_qblock(P, C, R, QT[r], b_q[r], j, tiles, branch_epilogue(r, hh, 1, j, False))
                load_gate(hh, 2)
                for j in range(4):
                    tiles = []
                    for kc in range(max(0, 4 * j - 4), 4 * j + 4):
                        rr = kc - 4 * j
                        t = dict(KT=XT["kw"][:, kc * 128:(kc + 1) * 128], bK=b_x["kw"], V=Vw[:, kc, :], bV=b_Vw, m=128,
                                 extra=[(C.jrevb[:, :], BMh[bi][:, rr + 4, :], [C.b, b_bm[bi]])])
                        tiles.append(t)
                    attn_qblock(P, C, R, QT[r], b_q[r], j, tiles, branch_epilogue(r, hh, 2, j, False))
                for j in range(4):
                    P.dma("sp", mixT[2048 + 128 * hh: 2048 + 128 * hh + 128, dsl(j)], acc[r][:, dsl(j)], reads=[b_acc[r]], writes=[b_mix])
        P.stack = old
    barrier(P)


TWO_PI = 2.0 * math.pi
GC = 1.5957691216057308

def s5_phase(P, tag, zT, mixT, b_mix, prm, cst, scr):
    U0 = 3080
    L = 512
    with ExitStack() as st:
        old = P.stack; P.stack = st
        cnt = [0]
        def T(shape, dt=F32):
            cnt[0] += 1
            return P.sb("%s_t%d" % (tag, cnt[0]), shape, dt)
        def dve(fn, reads, writes):
            P.op("dve", fn, reads=reads, writes=writes)
        def pool(fn, reads, writes):
            P.op("pool", fn, reads=reads, writes=writes)
        def act(fn, reads, writes):
            P.op("act", fn, reads=reads, writes=writes)

        def trig(y, by, shape, eng_name="dve"):
            op = dve if eng_name == "dve" else pool
            ki = T(shape, I32); kf = T(shape); f = T(shape); m = T(shape); fc = T(shape)
            s = T(shape); c = T(shape)
            b = Buf()
            def sl(t):
                return t[tuple(slice(None) for _ in shape)]
            op(lambda e: e.tensor_copy(out=sl(ki), in_=y), [by], [b])
            op(lambda e: e.tensor_copy(out=sl(kf), in_=sl(ki)), [b], [b])
            op(lambda e: e.tensor_tensor(out=sl(f), in0=y, in1=sl(kf), op=ALU.subtract), [by, b], [b])
            def wrap(t):
                op(lambda e: e.tensor_single_scalar(out=sl(m), in_=sl(t), scalar=0.5, op=ALU.is_gt), [b], [b])
                op(lambda e: e.tensor_tensor(out=sl(t), in0=sl(t), in1=sl(m), op=ALU.subtract), [b], [b])
                op(lambda e: e.tensor_single_scalar(out=sl(m), in_=sl(t), scalar=-0.5, op=ALU.is_lt), [b], [b])
                op(lambda e: e.tensor_tensor(out=sl(t), in0=sl(t), in1=sl(m), op=ALU.add), [b], [b])
            wrap(f)
            op(lambda e: e.tensor_scalar_add(out=sl(fc), in0=sl(f), scalar1=0.25), [b], [b])
            wrap(fc)
            act(lambda e: e.activation(out=sl(s), in_=sl(f), func=AF.Sin, scale=TWO_PI), [b], [b])
            act(lambda e: e.activation(out=sl(c), in_=sl(fc), func=AF.Sin, scale=TWO_PI), [b], [b])
            return s, c, b

        lr = T([128, 32]); nlr = T([128, 32]); angn = T([128, 32]); lb_re = T([128, 32]); lb_im = T([128, 32])
        LB = [T([128, 32, 128], BF16) for _ in range(2)]
        LC = [T([128, 32, 128], BF16) for _ in range(2)]
        ps_tr = [P.ps("%s_ptr%d" % (tag, i), [128, 512], F32) for i in range(2)]
        st2 = ExitStack(); st2.__enter__(); P.stack = st2
        A_re = T([128, 32]); A_im = T([128, 32]); ldt = T([128, 32])
        bp = Buf()
        P.dma("sp", A_re[:, :], prm["a_re"].rearrange("(pr g2) p -> g2 p pr", g2=2)[0], writes=[bp], allow_slow_non_contiguous=True) if False else None
        for g2 in range(2):
            P.dma("sp", A_re[64 * g2:64 * g2 + 64, :], prm["a_re"].rearrange("(pr g2) p -> g2 p pr", g2=2)[g2], writes=[bp], allow_slow_non_contiguous=True)
            P.dma("sp", A_im[64 * g2:64 * g2 + 64, :], prm["a_im"].rearrange("(pr g2) p -> g2 p pr", g2=2)[g2], writes=[bp], allow_slow_non_contiguous=True)
            P.dma("sp", ldt[64 * g2:64 * g2 + 64, :], prm["log_dt"].rearrange("(pr g2) -> g2 pr", g2=2)[g2:g2 + 1, :].partition_broadcast(64), writes=[bp], allow_slow_non_contiguous=True)
        lam_re = T([128, 32]); dtt = T([128, 32]); mag = T([128, 32])
        dve(lambda e: e.tensor_scalar_min(out=lam_re[:, :], in0=A_re[:, :], scalar1=-1e-4), [bp], [bp])
        act(lambda e: e.activation(out=dtt[:, :], in_=ldt[:, :], func=AF.Exp), [bp], [bp])
        dve(lambda e: e.tensor_tensor(out=lr[:, :], in0=lam_re[:, :], in1=dtt[:, :], op=ALU.mult), [bp], [bp])
        dve(lambda e: e.tensor_scalar_mul(out=nlr[:, :], in0=lr[:, :], scalar1=-1.0), [bp], [bp])
        dve(lambda e: e.scalar_tensor_tensor(out=angn[:, :], in0=A_im[:, :], scalar=1.0 / TWO_PI, in1=dtt[:, :], op0=ALU.mult, op1=ALU.mult), [bp], [bp])
        act(lambda e: e.activation(out=mag[:, :], in_=lr[:, :], func=AF.Exp), [bp], [bp])
        s0, c0, bt0 = trig(angn[:, :], bp, [128, 32])
        den = T([128, 32]); nr = T([128, 32]); t1 = T([128, 32]); t2 = T([128, 32])
        cf_re = T([128, 32]); cf_im = T([128, 32])
        dve(lambda e: e.tensor_tensor(out=lb_re[:, :], in0=mag[:, :], in1=c0[:, :], op=ALU.mult), [bp, bt0], [bp])
        dve(lambda e: e.tensor_tensor(out=lb_im[:, :], in0=mag[:, :], in1=s0[:, :], op=ALU.mult), [bp, bt0], [bp])
        dve(lambda e: e.tensor_tensor(out=den[:, :], in0=lam_re[:, :], in1=lam_re[:, :], op=ALU.mult), [bp], [bp])
        dve(lambda e: e.tensor_tensor(out=t1[:, :], in0=A_im[:, :], in1=A_im[:, :], op=ALU.mult), [bp], [bp])
        dve(lambda e: e.tensor_tensor(out=den[:, :], in0=den[:, :], in1=t1[:, :], op=ALU.add), [bp], [bp])
        dve(lambda e: e.reciprocal(out=den[:, :], in_=den[:, :]), [bp], [bp])
        dve(lambda e: e.tensor_scalar_add(out=nr[:, :], in0=lb_re[:, :], scalar1=-1.0), [bp], [bp])
        dve(lambda e: e.tensor_tensor(out=t1[:, :], in0=nr[:, :], in1=lam_re[:, :], op=ALU.mult), [bp], [bp])
        dve(lambda e: e.tensor_tensor(out=t2[:, :], in0=lb_im[:, :], in1=A_im[:, :], op=ALU.mult), [bp], [bp])
        dve(lambda e: e.tensor_tensor(out=t1[:, :], in0=t1[:, :], in1=t2[:, :], op=ALU.add), [bp], [bp])
        dve(lambda e: e.tensor_tensor(out=cf_re[:, :], in0=t1[:, :], in1=den[:, :], op=ALU.mult), [bp], [bp])
        dve(lambda e: e.tensor_tensor(out=t1[:, :], in0=lb_im[:, :], in1=lam_re[:, :], op=ALU.mult), [bp], [bp])
        dve(lambda e: e.tensor_tensor(out=t2[:, :], in0=nr[:, :], in1=A_im[:, :], op=ALU.mult), [bp], [bp])
        dve(lambda e: e.tensor_tensor(out=t1[:, :], in0=t1[:, :], in1=t2[:, :], op=ALU.subtract), [bp], [bp])
        dve(lambda e: e.tensor_tensor(out=cf_im[:, :], in0=t1[:, :], in1=den[:, :], op=ALU.mult), [bp], [bp])
        Bre = T([128, 32, 16]); Bim = T([128, 32, 16]); BBre = T([128, 32, 16]); BBim = T([128, 32, 16]); tb = T([128, 32, 16])
        for g2 in range(2):
            P.dma("sp", Bre[64 * g2:64 * g2 + 64, :, :], prm["b_re"].rearrange("(pr g2) p h -> g2 p pr h", g2=2)[g2], writes=[bp])
            P.dma("sp", Bim[64 * g2:64 * g2 + 64, :, :], prm["b_im"].rearrange("(pr g2) p h -> g2 p pr h", g2=2)[g2], writes=[bp])
        def bc(t):
            return t[:, :].unsqueeze(2).to_broadcast([128, 32, 16])
        dve(lambda e: e.tensor_tensor(out=BBre[:, :, :], in0=Bre[:, :, :], in1=bc(cf_re), op=ALU.mult), [bp], [bp])
        dve(lambda e: e.tensor_tensor(out=tb[:, :, :], in0=Bim[:, :, :], in1=bc(cf_im), op=ALU.mult), [bp], [bp])
        dve(lambda e: e.tensor_tensor(out=BBre[:, :, :], in0=BBre[:, :, :], in1=tb[:, :, :], op=ALU.subtract), [bp], [bp])
        dve(lambda e: e.tensor_tensor(out=BBim[:, :, :], in0=Bim[:, :, :], in1=bc(cf_re), op=ALU.mult), [bp], [bp])
        dve(lambda e: e.tensor_tensor(out=tb[:, :, :], in0=Bre[:, :, :], in1=bc(cf_im), op=ALU.mult), [bp], [bp])
        dve(lambda e: e.tensor_tensor(out=BBim[:, :, :], in0=BBim[:, :, :], in1=tb[:, :, :], op=ALU.add), [bp], [bp])
        bL = Buf()
        mB = T([128, 4, 128]); mC = T([128, 4, 128]); idf = T([128, 128]); bm_ = Buf()
        P.dma("sp", mB[:, :, :], cst["maskB"][:, :, :], writes=[bm_])
        P.dma("sp", mC[:, :, :], cst["maskC"][:, :, :], writes=[bm_])
        P.dma("sp", idf[:, :], cst["ident"][:, :], writes=[bm_])
        BBrep = [T([128, 32, 8, 16]) for _ in range(2)]
        Cw = [T([128, 8, 128]) for _ in range(2)]
        bCw = Buf()
        b_ptr = [Buf(), Buf()]
        for q, BBq in enumerate((BBre, BBim)):
            dve(lambda e, q=q, BBq=BBq: e.tensor_copy(out=BBrep[q][:, :, :, :], in_=BBq[:, :, :].unsqueeze(2).to_broadcast([128, 32, 8, 16])), [bp], [bp])
            csrc = prm["c_re"] if q == 0 else prm["c_im"]
            for half in range(2):
                P.dma("sp", Cw[q][:, :, 64 * half:64 * half + 64], csrc.rearrange("(cc gl) ho p -> (gl ho) cc p", gl=8), writes=[bCw])
        it = 0
        for q in range(2):
            for cc in range(8):
                i = it % 2; it += 1
                def f_t(e, q=q, cc=cc, i=i):
                    ins = None
                    for k in range(4):
                        ins = e.transpose(out=ps_tr[i][:, 128 * k:128 * k + 128], in_=BBrep[q][:, 4 * cc + k, :, :].rearrange("p a b -> p (a b)"), identity=idf[:, :])
                    return ins
                P.op("pe", f_t, reads=[bp, bm_], writes=[b_ptr[i]])
                dve(lambda e, q=q, cc=cc, i=i: e.tensor_tensor(out=LB[q][:, 4 * cc:4 * cc + 4, :], in0=ps_tr[i][:, :].rearrange("p (k m) -> p k m", k=4), in1=mB[:, :, :], op=ALU.mult),
                    [b_ptr[i], bm_], [bL])
                i = it % 2; it += 1
                P.op("pe", lambda e, q=q, cc=cc, i=i: e.transpose(out=ps_tr[i][:, 0:128], in_=Cw[q][:, cc, :], identity=idf[:, :]), reads=[bCw, bm_], writes=[b_ptr[i]])
                sgn = 1.0 if q == 0 else -1.0
                dve(lambda e, q=q, cc=cc, i=i, sgn=sgn: e.scalar_tensor_tensor(out=LC[q][:, 4 * cc:4 * cc + 4, :], in0=mC[:, :, :], scalar=sgn,
                                                                           in1=ps_tr[i][:, 0:128].unsqueeze(1).to_broadcast([128, 4, 128]), op0=ALU.mult, op1=ALU.mult),
                    [b_ptr[i], bm_], [bL])
        st2.__exit__(None, None, None); P.stack = st
        barrier(P)
        ubf = T([128, 2, 2048], BF16); bu2 = [Buf(), Buf()]
        dsk = T([128, 8]); bd = Buf()
        P.dma("sp", dsk[:, :], prm["d"].rearrange("(c p) -> p c", p=128), writes=[bd], allow_slow_non_contiguous=True)
        iot = T([128, L]); bi = Buf()
        P.dma("sp", iot[:, :], cst["iota"][0:1, 0:L].partition_broadcast(128), writes=[bi])
        wg = T([128, 8, 1024], BF16); bw = Buf()
        for k in range(8):
            P.dma("pool", wg[:, k, :], prm["w_glu"][128 * k:128 * k + 128, :], writes=[bw])
        ygb = T([128, 8, 2048], BF16); bygb = Buf()
        ps_bu = [[P.ps("%s_pbu%d%d" % (tag, 0, q), [128, 512], F32) for q in range(2)], ps_tr]
        b_pbu = [Buf(), Buf()]
        b_pbu[1] = b_ptr[0]; b_ptr[1] = b_ptr[0]
        ps_y = [P.ps("%s_py%d" % (tag, i), [128, 512], F32) for i in range(4)]
        b_py = [Buf() for _ in range(4)]
        yt = T([128, L]); mp = T([128, L]); mn = T([128, L])
        Lp = [T([128, L]), T([128, L])]; Lm = [T([128, L]), T([128, L])]
        b_tab = Buf()
        tki = T([128, L], I32); tkf = T([128, L]); tf = T([128, L]); tm = T([128, L]); tfc = T([128, L]); ts_ = T([128, L]); tc_ = T([128, L])
        gre = [T([128, L]) for _ in range(2)]; gim = [T([128, L]) for _ in range(2)]
        Gre = [T([128, L]) for _ in range(2)]; Gim = [T([128, L]) for _ in range(2)]
        tmpa = [T([128, L]) for _ in range(2)]; tmpb = [T([128, L]) for _ in range(2)]
        hre = [T([128, L], BF16) for _ in range(2)]; him = [T([128, L], BF16) for _ in range(2)]
        hlast = [T([128, 2]) for _ in range(2)]
        b_g = [Buf(), Buf()]; b_G = [Buf(), Buf()]; b_h = [Buf(), Buf()]; b_tmp = [Buf(), Buf()]; b_hl = [Buf(), Buf()]
        ones = T([128, L]); b1 = Buf()
        dve(lambda e: e.memset(ones[:, :], 1.0), [], [b1])
        sc4 = T([128, 4]); b_sc = Buf()
        yf = T([128, 2048]); x2 = T([128, 2048]); b_yf = Buf()
        uf = T([128, 2048]); b_uf = Buf()
        blk = 0
        for pr in range(32):
            cc = pr // 4
            bu_ = bu2[cc % 2]
            if pr % 4 == 0:
                P.dma("pool", ubf[:, cc % 2, :], zT[U0 + 128 * cc:U0 + 128 * cc + 128, :], writes=[bu_])
            dve(lambda e, pr=pr: e.tensor_scalar_mul(out=yt[:, :], in0=iot[:, :], scalar1=angn[:, pr:pr + 1]), [bi, bp, b_tab], [b_tab])
            dve(lambda e: e.tensor_copy(out=tki[:, :], in_=yt[:, :]), [b_tab], [b_tab])
            dve(lambda e: e.tensor_copy(out=tkf[:, :], in_=tki[:, :]), [b_tab], [b_tab])
            dve(lambda e: e.tensor_tensor(out=tf[:, :], in0=yt[:, :], in1=tkf[:, :], op=ALU.subtract), [b_tab], [b_tab])
            def wrap(t):
                dve(lambda e: e.tensor_single_scalar(out=tm[:, :], in_=t[:, :], scalar=0.5, op=ALU.is_gt), [b_tab], [b_tab])
                dve(lambda e: e.tensor_tensor(out=t[:, :], in0=t[:, :], in1=tm[:, :], op=ALU.subtract), [b_tab], [b_tab])
                dve(lambda e: e.tensor_single_scalar(out=tm[:, :], in_=t[:, :], scalar=-0.5, op=ALU.is_lt), [b_tab], [b_tab])
                dve(lambda e: e.tensor_tensor(out=t[:, :], in0=t[:, :], in1=tm[:, :], op=ALU.add), [b_tab], [b_tab])
            wrap(tf)
            dve(lambda e: e.tensor_scalar_add(out=tfc[:, :], in0=tf[:, :], scalar1=0.25), [b_tab], [b_tab])
            wrap(tfc)
            act(lambda e: e.activation(out=ts_[:, :], in_=tf[:, :], func=AF.Sin, scale=TWO_PI), [b_tab], [b_tab])
            act(lambda e: e.activation(out=tc_[:, :], in_=tfc[:, :], func=AF.Sin, scale=TWO_PI), [b_tab], [b_tab])
            act(lambda e, pr=pr: e.activation(out=mp[:, :], in_=iot[:, :], func=AF.Exp, scale=lr[:, pr:pr + 1]), [bi, bp, b_tab], [b_tab])
            act(lambda e, pr=pr: e.activation(out=mn[:, :], in_=iot[:, :], func=AF.Exp, scale=nlr[:, pr:pr + 1]), [bi, bp, b_tab], [b_tab])
            dve(lambda e: e.tensor_tensor(out=Lp[0][:, :], in0=mp[:, :], in1=tc_[:, :], op=ALU.mult), [b_tab], [b_tab])
            dve(lambda e: e.tensor_tensor(out=Lp[1][:, :], in0=mp[:, :], in1=ts_[:, :], op=ALU.mult), [b_tab], [b_tab])
            dve(lambda e: e.tensor_tensor(out=Lm[0][:, :], in0=mn[:, :], in1=tc_[:, :], op=ALU.mult), [b_tab], [b_tab])
            dve(lambda e: e.scalar_tensor_tensor(out=Lm[1][:, :], in0=mn[:, :], scalar=-1.0, in1=ts_[:, :], op0=ALU.mult, op1=ALU.mult), [b_tab], [b_tab])
            for c in range(4):
                i = blk % 2; blk += 1
                def f_bu(e, pr=pr, cc=cc, c=c, i=i):
                    e.matmul(ps_bu[i][0][:, :], lhsT=LB[0][:, pr, :], rhs=ubf[:, cc % 2, c * L:(c + 1) * L], start=True, stop=True)
                    return e.matmul(ps_bu[i][1][:, :], lhsT=LB[1][:, pr, :], rhs=ubf[:, cc % 2, c * L:(c + 1) * L], start=True, stop=True)
                P.op("pe", f_bu, reads=[bL, bu_], writes=[b_pbu[i]])
                dve(lambda e, i=i: e.tensor_tensor(out=gre[i][:, :], in0=ps_bu[i][0][:, :], in1=Lm[0][:, :], op=ALU.mult), [b_pbu[i], b_tab], [b_g[i]])
                dve(lambda e, i=i: e.tensor_tensor(out=tmpa[i][:, :], in0=ps_bu[i][1][:, :], in1=Lm[1][:, :], op=ALU.mult), [b_pbu[i], b_tab], [b_tmp[i]])
                pool(lambda e, i=i: e.tensor_tensor(out=gre[i][:, :], in0=gre[i][:, :], in1=tmpa[i][:, :], op=ALU.subtract), [b_g[i], b_tmp[i]], [b_g[i]])
                dve(lambda e, i=i: e.tensor_tensor(out=gim[i][:, :], in0=ps_bu[i][1][:, :], in1=Lm[0][:, :], op=ALU.mult), [b_pbu[i], b_tab], [b_g[i]])
                dve(lambda e, i=i: e.tensor_tensor(out=tmpb[i][:, :], in0=ps_bu[i][0][:, :], in1=Lm[1][:, :], op=ALU.mult), [b_pbu[i], b_tab], [b_tmp[i]])
                pool(lambda e, i=i: e.tensor_tensor(out=gim[i][:, :], in0=gim[i][:, :], in1=tmpb[i][:, :], op=ALU.add), [b_g[i], b_tmp[i]], [b_g[i]])
                dve(lambda e, i=i: e.tensor_tensor_scan(out=Gre[i][:, :], data0=ones[:, :], data1=gre[i][:, :], initial=0.0, op0=ALU.mult, op1=ALU.add), [b1, b_g[i]], [b_G[i]])
                dve(lambda e, i=i: e.tensor_tensor_scan(out=Gim[i][:, :], data0=ones[:, :], data1=gim[i][:, :], initial=0.0, op0=ALU.mult, op1=ALU.add), [b1, b_g[i]], [b_G[i]])
                if c > 0:
                    pv = 1 - i
                    dve(lambda e, pr=pr, pv=pv: e.tensor_tensor(out=sc4[:, 0:1], in0=hlast[pv][:, 0:1], in1=lb_re[:, pr:pr + 1], op=ALU.mult), [b_hl[pv], bp, b_sc], [b_sc])
                    dve(lambda e, pr=pr, pv=pv: e.tensor_tensor(out=sc4[:, 1:2], in0=hlast[pv][:, 1:2], in1=lb_im[:, pr:pr + 1], op=ALU.mult), [b_hl[pv], bp, b_sc], [b_sc])
                    dve(lambda e: e.tensor_tensor(out=sc4[:, 0:1], in0=sc4[:, 0:1], in1=sc4[:, 1:2], op=ALU.subtract), [b_sc], [b_sc])
                    dve(lambda e, pr=pr, pv=pv: e.tensor_tensor(out=sc4[:, 2:3], in0=hlast[pv][:, 1:2], in1=lb_re[:, pr:pr + 1], op=ALU.mult), [b_hl[pv], bp, b_sc], [b_sc])
                    dve(lambda e, pr=pr, pv=pv: e.tensor_tensor(out=sc4[:, 3:4], in0=hlast[pv][:, 0:1], in1=lb_im[:, pr:pr + 1], op=ALU.mult), [b_hl[pv], bp, b_sc], [b_sc])
                    dve(lambda e: e.tensor_tensor(out=sc4[:, 2:3], in0=sc4[:, 2:3], in1=sc4[:, 3:4], op=ALU.add), [b_sc], [b_sc])
                    dve(lambda e, i=i: e.tensor_scalar_add(out=Gre[i][:, :], in0=Gre[i][:, :], scalar1=sc4[:, 0:1]), [b_sc, b_G[i]], [b_G[i]])
                    dve(lambda e, i=i: e.tensor_scalar_add(out=Gim[i][:, :], in0=Gim[i][:, :], scalar1=sc4[:, 2:3]), [b_sc, b_G[i]], [b_G[i]])
                pool(lambda e, i=i: e.tensor_tensor(out=tmpa[i][:, :], in0=Gre[i][:, :], in1=Lp[0][:, :], op=ALU.mult), [b_G[i], b_tab, b_tmp[i]], [b_tmp[i]])
                pool(lambda e, i=i: e.tensor_tensor(out=tmpb[i][:, :], in0=Gim[i][:, :], in1=Lp[1][:, :], op=ALU.mult), [b_G[i], b_tab, b_tmp[i]], [b_tmp[i]])
                pool(lambda e, i=i: e.tensor_tensor(out=gre[i][:, :], in0=tmpa[i][:, :], in1=tmpb[i][:, :], op=ALU.subtract), [b_tmp[i], b_g[i]], [b_g[i]])
                pool(lambda e, i=i: e.tensor_tensor(out=tmpa[i][:, :], in0=Gim[i][:, :], in1=Lp[0][:, :], op=ALU.mult), [b_G[i], b_tab, b_tmp[i]], [b_tmp[i]])
                pool(lambda e, i=i: e.tensor_tensor(out=tmpb[i][:, :], in0=Gre[i][:, :], in1=Lp[1][:, :], op=ALU.mult), [b_G[i], b_tab, b_tmp[i]], [b_tmp[i]])
                pool(lambda e, i=i: e.tensor_tensor(out=gim[i][:, :], in0=tmpa[i][:, :], in1=tmpb[i][:, :], op=ALU.add), [b_tmp[i], b_g[i]], [b_g[i]])
                pool(lambda e, i=i: e.tensor_copy(out=hlast[i][:, 0:1], in_=gre[i][:, L - 1:L]), [b_g[i], b_hl[i]], [b_hl[i]])
                pool(lambda e, i=i: e.tensor_copy(out=hlast[i][:, 1:2], in_=gim[i][:, L - 1:L]), [b_g[i], b_hl[i]], [b_hl[i]])
                act(lambda e, i=i: e.copy(out=hre[i][:, :], in_=gre[i][:, :]), [b_g[i]], [b_h[i]])
                act(lambda e, i=i: e.copy(out=him[i][:, :], in_=gim[i][:, :]), [b_g[i]], [b_h[i]])
                def f_c(e, pr=pr, c=c, i=i):
                    e.matmul(ps_y[c][:, :], lhsT=LC[0][:, pr, :], rhs=hre[i][:, :], start=(pr % 4 == 0), stop=False)
                    return e.matmul(ps_y[c][:, :], lhsT=LC[1][:, pr, :], rhs=him[i][:, :], start=False, stop=(pr % 4 == 3))
                P.op("pe", f_c, reads=[bL, b_h[i]], writes=[b_py[c]])
            if pr % 4 == 3:
                P.dma("sp", uf[:, :], zT[U0 + 128 * cc:U0 + 128 * cc + 128, :], writes=[b_uf])
                for c in range(4):
                    dve(lambda e, c=c, cc=cc: e.scalar_tensor_tensor(out=yf[:, c * L:(c + 1) * L], in0=uf[:, c * L:(c + 1) * L], scalar=dsk[:, cc:cc + 1],
                                                                      in1=ps_y[c][:, :], op0=ALU.mult, op1=ALU.add), [b_uf, bd, b_py[c]], [b_yf])
                act(lambda e: e.activation(out=x2[:, :], in_=yf[:, :], func=AF.Square), [b_yf], [b_yf])
                dve(lambda e: e.tensor_scalar(out=x2[:, :], in0=x2[:, :], scalar1=0.044715, scalar2=1.0, op0=ALU.mult, op1=ALU.add), [b_yf], [b_yf])
                dve(lambda e: e.tensor_tensor(out=x2[:, :], in0=x2[:, :], in1=yf[:, :], op=ALU.mult), [b_yf], [b_yf])
                act(lambda e: e.activation(out=x2[:, :], in_=x2[:, :], func=AF.Sigmoid, scale=GC), [b_yf], [b_yf])
                dve(lambda e: e.tensor_tensor(out=yf[:, :], in0=yf[:, :], in1=x2[:, :], op=ALU.mult), [b_yf], [b_yf])
                act(lambda e, cc=cc: e.copy(out=ygb[:, cc, :], in_=yf[:, :]), [b_yf], [bygb])
                P.dma("sp", scr["ygD"][128 * cc:128 * cc + 128, :], yf[:, :], reads=[b_yf], writes=[scr["b_ygD"]])
        for n in range(8):
            for c in range(4):
                def f_g(e, n=n, c=c):
                    ins = None
                    for k in range(8):
                        ins = e.matmul(ps_y[c][:, :], lhsT=wg[:, k, 128 * n:128 * n + 128], rhs=ygb[:, k, c * L:(c + 1) * L], start=(k == 0), stop=(k == 7))
                    return ins
                P.op("pe", f_g, reads=[bw, bygb], writes=[b_py[c]])
            P.dma("sp", uf[:, :], scr["ygD"][128 * n:128 * n + 128, :], reads=[scr["b_ygD"]], writes=[b_uf])
            for c in range(4):
                act(lambda e, c=c: e.activation(out=x2[:, c * L:(c + 1) * L], in_=ps_y[c][:, :], func=AF.Sigmoid), [b_py[c], b_yf], [b_yf])
            dve(lambda e: e.tensor_tensor(out=yf[:, :], in0=uf[:, :], in1=x2[:, :], op=ALU.mult), [b_uf, b_yf], [b_yf])
            P.dma("sp", mixT[1024 + 128 * n:1024 + 128 * n + 128, :], yf[:, :], reads=[b_yf], writes=[b_mix])
        P.stack = old
    barrier(P)


def transpose_in(P, tag, x, xT, b_xT, identf, b_c):
    with ExitStack() as st:
        old = P.stack; P.stack = st
        xs = [P.sb(tag + "_xs%d" % i, [128, 4096], F32) for i in range(2)]
        stg = [P.sb(tag + "_st%d" % i, [128, 32, 128], F32) for i in range(2)]
        ps = [P.ps(tag + "_ps%d" % i, [128, 512], F32) for i in range(2)]
        b_xs = [Buf(), Buf()]; b_st = [Buf(), Buf()]; b_ps = [Buf(), Buf()]
        it = 0
        xTv = xT.rearrange("(fc p) t -> p fc t", p=128)
        for tt in range(16):
            i = tt % 2
            P.dma("sp", xs[i][:, :], x[tt * 128:(tt + 1) * 128, :], writes=[b_xs[i]])
            for f4 in range(8):
                pi = it % 2; it += 1
                def f(e, i=i, f4=f4, pi=pi):
                    ins = None
                    for k in range(4):
                        fc = 4 * f4 + k
                        ins = e.transpose(out=ps[pi][:, 128 * k:128 * k + 128], in_=xs[i][:, fc * 128:(fc + 1) * 128], identity=identf[:, :])
                    return ins
                P.op("pe", f, reads=[b_xs[i], b_c], writes=[b_ps[pi]])
                eng = "dve" if pi == 0 else "act"
                if eng == "dve":
                    P.op("dve", lambda e, i=i, f4=f4, pi=pi: e.tensor_copy(out=stg[i][:, 4 * f4:4 * f4 + 4, :], in_=ps[pi][:, :].rearrange("p (k t) -> p k t", k=4)),
                         reads=[b_ps[pi]], writes=[b_st[i]])
                else:
                    P.op("act", lambda e, i=i, f4=f4, pi=pi: e.copy(out=stg[i][:, 4 * f4:4 * f4 + 4, :], in_=ps[pi][:, :].rearrange("p (k t) -> p k t", k=4)),
                         reads=[b_ps[pi]], writes=[b_st[i]])
            for f8 in range(0, 32, 8):
                P.dma("sp", xTv[:, f8:f8 + 8, tt * 128:(tt + 1) * 128], stg[i][:, f8:f8 + 8, :], reads=[b_st[i]], writes=[b_xT])
        P.stack = old
    barrier(P)


def transpose_out(P, tag, xT, b_xT, y, b_y, identf, b_c):
    with ExitStack() as st:
        old = P.stack; P.stack = st
        xs = [P.sb(tag + "_xs%d" % i, [128, 32, 128], F32) for i in range(2)]
        stg = [P.sb(tag + "_st%d" % i, [128, 4096], F32) for i in range(2)]
        ps = [P.ps(tag + "_ps%d" % i, [128, 512], F32) for i in range(2)]
        b_xs = [Buf(), Buf()]; b_st = [Buf(), Buf()]; b_ps = [Buf(), Buf()]
        it = 0
        xTv = xT.rearrange("(fc p) t -> p fc t", p=128)
        for tt in range(16):
            i = tt % 2
            for f8 in range(0, 32, 8):
                P.dma("sp", xs[i][:, f8:f8 + 8, :], xTv[:, f8:f8 + 8, tt * 128:(tt + 1) * 128], reads=[b_xT], writes=[b_xs[i]])
            for f4 in range(8):
                pi = it % 2; it += 1
                def f(e, i=i, f4=f4, pi=pi):
                    ins = None
                    for k in range(4):
                        ins = e.transpose(out=ps[pi][:, 128 * k:128 * k + 128], in_=xs[i][:, 4 * f4 + k, :], identity=identf[:, :])
                    return ins
                P.op("pe", f, reads=[b_xs[i], b_c], writes=[b_ps[pi]])
                if pi == 0:
                    P.op("dve", lambda e, i=i, f4=f4, pi=pi: e.tensor_copy(out=stg[i][:, 512 * f4:512 * f4 + 512], in_=ps[pi][:, :]), reads=[b_ps[pi]], writes=[b_st[i]])
                else:
                    P.op("act", lambda e, i=i, f4=f4, pi=pi: e.copy(out=stg[i][:, 512 * f4:512 * f4 + 512], in_=ps[pi][:, :]), reads=[b_ps[pi]], writes=[b_st[i]])
            P.dma("sp", y[tt * 128:(tt + 1) * 128, :], stg[i][:, :], reads=[b_st[i]], writes=[b_y])
        P.stack = old
    barrier(P)


def norm_phase(P, tag, srcT, r0, F, gain, mode, dstT, d0, b_src, b_dst, onesf, b_c, eps=1e-6):
    C = F // 128
    TB = 1024
    with ExitStack() as st:
        old = P.stack; P.stack = st
        X = P.sb(tag + "_X", [128, C, TB], F32); b_X = Buf()
        g = P.sb(tag + "_g", [128, C], F32); b_g = Buf()
        sq = [P.sb(tag + "_sq%d" % i, [128, TB], BF16) for i in range(2)]; b_sq = [Buf(), Buf()]
        rs = P.sb(tag + "_rs", [128, TB], F32); b_rs = Buf()
        ps = [P.ps(tag + "_ps%d" % i, [128, 512], F32) for i in range(2)]; b_ps = Buf()
        o16 = [P.sb(tag + "_o%d" % i, [128, TB], BF16 if mode == "bf16" else F32) for i in range(2)]; b_o = [Buf(), Buf()]
        xr = [P.sb(tag + "_xr%d" % i, [128, TB], F32) for i in range(2)]; b_xr = [Buf(), Buf()]
        P.dma("sp", g[:, :], gain.rearrange("(c p) -> p c", p=128), writes=[b_g], allow_slow_non_contiguous=True)
        sv = srcT[r0:r0 + F, :].rearrange("(c p) t -> p c t", p=128)
        dv = dstT[d0:d0 + F, :].rearrange("(c p) t -> p c t", p=128)
        for tb in range(2048 // TB):
            t0 = tb * TB
            for c0 in range(0, C, 4):
                c1 = min(C, c0 + 4)
                P.dma("sp", X[:, c0:c1, :], sv[:, c0:c1, t0:t0 + TB], reads=[b_src], writes=[b_X])
            for c in range(C):
                i = c % 2
                P.op("act", lambda e, c=c, i=i: e.activation(out=sq[i][:, :], in_=X[:, c, :], func=AF.Square), reads=[b_X], writes=[b_sq[i]])
                def f(e, c=c, i=i):
                    e.matmul(ps[0][:, :], lhsT=onesf[:, :], rhs=sq[i][:, 0:512], start=(c == 0), stop=(c == C - 1))
                    return e.matmul(ps[1][:, :], lhsT=onesf[:, :], rhs=sq[i][:, 512:1024], start=(c == 0), stop=(c == C - 1))
                P.op("pe", f, reads=[b_sq[i], b_c], writes=[b_ps])
            for h in range(2):
                P.op("act", lambda e, h=h: e.activation(out=rs[:, 512 * h:512 * h + 512], in_=ps[h][:, :], func=AF.Sqrt, scale=1.0 / F, bias=eps),
                     reads=[b_ps], writes=[b_rs])
            P.op("dve", lambda e: e.reciprocal(out=rs[:, :], in_=rs[:, :]), reads=[b_rs], writes=[b_rs])
            for c in range(C):
                i = c % 2
                if mode == "bf16":
                    P.op("dve", lambda e, c=c, i=i: e.scalar_tensor_tensor(out=o16[i][:, :], in0=X[:, c, :], scalar=g[:, c:c + 1], in1=rs[:, :], op0=ALU.mult, op1=ALU.mult),
                         reads=[b_X, b_g, b_rs], writes=[b_o[i]])
                    P.dma("sp", dv[:, c, t0:t0 + TB], o16[i][:, :], reads=[b_o[i]], writes=[b_dst])
                else:
                    P.dma("sp", xr[i][:, :], dv[:, c, t0:t0 + TB], reads=[b_dst], writes=[b_xr[i]])
                    P.op("dve", lambda e, c=c, i=i: e.scalar_tensor_tensor(out=o16[i][:, :], in0=X[:, c, :], scalar=g[:, c:c + 1], in1=rs[:, :], op0=ALU.mult, op1=ALU.mult),
                         reads=[b_X, b_g, b_rs], writes=[b_o[i]])
                    P.op("dve", lambda e, i=i: e.tensor_tensor(out=o16[i][:, :], in0=o16[i][:, :], in1=xr[i][:, :], op=ALU.add),
                         reads=[b_o[i], b_xr[i]], writes=[b_o[i]])
                    P.dma("sp", dv[:, c, t0:t0 + TB], o16[i][:, :], reads=[b_o[i]], writes=[b_dst])
        P.stack = old
    barrier(P)


def conv_phase(P, tag, uT, b_u, actT, b_act, conv_w, conv_b, DF=11008, jobs=None):
    NJ = DF // 128
    with ExitStack() as st:
        old = P.stack; P.stack = st
        cw = P.sb(tag + "_cw", [128, 3, 2 * NJ], F32); cb = P.sb(tag + "_cb", [128, 2 * NJ], F32); b_cw = Buf()
        for k in range(3):
            P.dma("sp", cw[:, k, :], conv_w[k].rearrange("(c p) -> p c", p=128), writes=[b_cw], allow_slow_non_contiguous=True)
        P.dma("sp", cb[:, :], conv_b.rearrange("(c p) -> p c", p=128), writes=[b_cw], allow_slow_non_contiguous=True)
        ug = [P.sb(tag + "_ug%d" % i, [128, 2050], F32) for i in range(2)]
        uv = [P.sb(tag + "_uv%d" % i, [128, 2050], F32) for i in range(2)]
        b_ug = [Buf(), Buf()]; b_uv = [Buf(), Buf()]
        yg = P.sb(tag + "_yg", [128, 2048], F32); yv = P.sb(tag + "_yv", [128, 2048], F32); s2 = P.sb(tag + "_s2", [128, 2048], F32)
        b_yg, b_yv, b_s2 = Buf(), Buf(), Buf()
        tv = P.sb(tag + "_tv", [128, 2048], F32); b_tv = Buf()
        ao = [P.sb(tag + "_ao%d" % i, [128, 2048], BF16) for i in range(2)]; b_ao = [Buf(), Buf()]
        for i in range(2):
            P.op("dve", lambda e, i=i: e.memset(ug[i][:, 0:2], 0.0), writes=[b_ug[i]])
            P.op("dve", lambda e, i=i: e.memset(uv[i][:, 0:2], 0.0), writes=[b_uv[i]])
        for j in range(NJ):
            i = j % 2
            P.dma("sp", ug[i][:, 2:2050], uT[128 * j:128 * j + 128, :], reads=[b_u], writes=[b_ug[i]])
            P.dma("sp", uv[i][:, 2:2050], uT[DF + 128 * j:DF + 128 * j + 128, :], reads=[b_u], writes=[b_uv[i]])
            for (u, bu, y, by, cj) in ((ug[i], b_ug[i], yg, b_yg, j), (uv[i], b_uv[i], yv, b_yv, NJ + j)):
                P.op("act", lambda e, u=u, y=y, cj=cj: e.activation(out=y[:, :], in_=u[:, 2:2050], func=AF.Identity, scale=cw[:, 2, cj:cj + 1], bias=cb[:, cj:cj + 1]),
                     reads=[bu, b_cw], writes=[by])
            P.op("dve", lambda e, u=ug[i], cj=j: e.scalar_tensor_tensor(out=yg[:, :], in0=u[:, 1:2049], scalar=cw[:, 1, cj:cj + 1], in1=yg[:, :], op0=ALU.mult, op1=ALU.add),
                 reads=[b_ug[i], b_cw, b_yg], writes=[b_yg])
            P.op("dve", lambda e, u=ug[i], cj=j: e.scalar_tensor_tensor(out=yg[:, :], in0=u[:, 0:2048], scalar=cw[:, 0, cj:cj + 1], in1=yg[:, :], op0=ALU.mult, op1=ALU.add),
                 reads=[b_ug[i], b_cw, b_yg], writes=[b_yg])
            for tap in (1, 0):
                P.op("pool", lambda e, u=uv[i], cj=NJ + j, tap=tap: e.tensor_scalar(out=tv[:, :], in0=u[:, tap:tap + 2048], scalar1=cw[:, tap, cj:cj + 1], scalar2=0.0, op0=ALU.mult, op1=ALU.add),
                     reads=[b_uv[i], b_cw, b_tv], writes=[b_tv])
                P.op("pool", lambda e: e.tensor_tensor(out=yv[:, :], in0=yv[:, :], in1=tv[:, :], op=ALU.add), reads=[b_tv, b_yv], writes=[b_yv])
            if jobs:
                for _ in range(3):
                    if jobs:
                        jobs.pop(0)()
            P.op("act", lambda e: e.activation(out=s2[:, :], in_=yg[:, :], func=AF.Square), reads=[b_yg], writes=[b_s2])
            P.op("dve", lambda e: e.tensor_scalar(out=s2[:, :], in0=s2[:, :], scalar1=0.044715, scalar2=1.0, op0=ALU.mult, op1=ALU.add), reads=[b_s2], writes=[b_s2])
            P.op("dve", lambda e: e.tensor_tensor(out=s2[:, :], in0=s2[:, :], in1=yg[:, :], op=ALU.mult), reads=[b_s2, b_yg], writes=[b_s2])
            P.op("act", lambda e: e.activation(out=s2[:, :], in_=s2[:, :], func=AF.Sigmoid, scale=GC), reads=[b_s2], writes=[b_s2])
            P.op("dve", lambda e: e.tensor_tensor(out=yg[:, :], in0=yg[:, :], in1=s2[:, :], op=ALU.mult), reads=[b_s2, b_yg], writes=[b_yg])
            P.op("dve", lambda e, i=i: e.tensor_tensor(out=ao[i][:, :], in0=yg[:, :], in1=yv[:, :], op=ALU.mult), reads=[b_yg, b_yv], writes=[b_ao[i]])
            P.dma("sp", actT[128 * j:128 * j + 128, :], ao[i][:, :], reads=[b_ao[i]], writes=[b_act])
        while jobs:
            jobs.pop(0)()
        P.stack = old
    barrier(P)


def gemm_up_conv(P, tag, hT, W, outT, b_out, conv_w, conv_b, jobs=None):
    K, KC, TB, PW, DF, T = 4096, 32, 1024, 512, 11008, 2048
    NJ2 = 2 * DF // 128
    with ExitStack() as st:
        old = P.stack; P.stack = st
        act = P.sb(tag + "_act", [128, KC, TB], BF16)
        wts = [P.sb(tag + "_w%d" % i, [128, KC, PW], BF16) for i in range(2)]
        pss = [P.ps(tag + "_p%d" % i, [128, TB], F32) for i in range(2)]
        cw = P.sb(tag + "_cw", [128, 3, NJ2], F32); cb = P.sb(tag + "_cb", [128, NJ2], F32); b_cw = Buf()
        hal = P.sb(tag + "_hal", [128, NJ2, 2], F32); b_hal = Buf()
        yb = [P.sb(tag + "_y%d" % i, [128, TB], F32) for i in range(2)]; b_y = [Buf(), Buf()]
        s2 = [P.sb(tag + "_s%d" % i, [128, TB], F32) for i in range(2)]; b_s2 = [Buf(), Buf()]
        G = P.sb(tag + "_G", [128, 4, TB], F32); b_G = [Buf() for _ in range(4)]
        ao = [P.sb(tag + "_ao%d" % i, [128, TB], BF16) for i in range(2)]; b_ao = [Buf(), Buf()]
        b_act = Buf(); b_w = [Buf(), Buf()]; b_p = [Buf(), Buf()]
        for k in range(3):
            P.dma("sp", cw[:, k, :], conv_w[k].rearrange("(c p) -> p c", p=128), writes=[b_cw], allow_slow_non_contiguous=True)
        P.dma("sp", cb[:, :], conv_b.rearrange("(c p) -> p c", p=128), writes=[b_cw], allow_slow_non_contiguous=True)
        actv = hT.rearrange("(c p) t -> p c t", p=128)
        Wv = W.rearrange("(c p) n -> p c n", p=128)
        it = 0; wi = 0; gi = 0; vi = 0
        npan = (DF + PW - 1) // PW
        for tb in range(T // TB):
            t0 = tb * TB
            for k0 in range(0, KC, 8):
                P.dma("sp", act[:, k0:k0 + 8, :], actv[:, k0:k0 + 8, t0:t0 + TB], writes=[b_act])
            for p in range(npan):
                pw = min(PW, DF - PW * p)
                for is_val in (0, 1):
                    n0 = PW * p + (DF if is_val else 0)
                    wb = wi % 2; wi += 1
                    for k0 in range(0, KC, 8):
                        P.dma("pool", wts[wb][:, k0:k0 + 8, :pw], Wv[:, k0:k0 + 8, n0:n0 + pw], writes=[b_w[wb]])
                    for c in range(pw // 128):
                        cj = (n0 + 128 * c) // 128
                        j = 4 * p + c
                        pb = it % 2; it += 1
                        def mm(e, wb=wb, c=c, pb=pb):
                            ins = None
                            for ts in range(TB // 512):
                                for k in range(KC):
                                    ins = e.matmul(pss[pb][:, ts * 512:(ts + 1) * 512], lhsT=wts[wb][:, k, 128 * c:128 * c + 128],
                                                   rhs=act[:, k, ts * 512:(ts + 1) * 512], start=(k == 0), stop=(k == KC - 1))
                            return ins
                        P.op("pe", mm, reads=[b_act, b_w[wb]], writes=[b_p[pb]])
                        yi = pb
                        y = yb[yi]; by = b_y[yi]
                        P.op("act", lambda e, y=y, pb=pb, cj=cj: e.activation(out=y[:, :], in_=pss[pb][:, :], func=AF.Identity, scale=cw[:, 2, cj:cj + 1], bias=cb[:, cj:cj + 1]),
                             reads=[b_p[pb], b_cw], writes=[by])
                        P.op("dve", lambda e, y=y, pb=pb, cj=cj: e.scalar_tensor_tensor(out=y[:, 1:TB], in0=pss[pb][:, 0:TB - 1], scalar=cw[:, 1, cj:cj + 1], in1=y[:, 1:TB], op0=ALU.mult, op1=ALU.add),
                             reads=[b_p[pb], b_cw, by], writes=[by])
                        P.op("dve", lambda e, y=y, pb=pb, cj=cj: e.scalar_tensor_tensor(out=y[:, 2:TB], in0=pss[pb][:, 0:TB - 2], scalar=cw[:, 0, cj:cj + 1], in1=y[:, 2:TB], op0=ALU.mult, op1=ALU.add),
                             reads=[b_p[pb], b_cw, by], writes=[by])
                        if tb == 0:
                            P.op("dve", lambda e, pb=pb, cj=cj: e.tensor_copy(out=hal[:, cj, :], in_=pss[pb][:, TB - 2:TB]), reads=[b_p[pb]], writes=[b_hal])
                        else:
                            P.op("dve", lambda e, y=y, cj=cj: e.scalar_tensor_tensor(out=y[:, 0:1], in0=hal[:, cj, 1:2], scalar=cw[:, 1, cj:cj + 1], in1=y[:, 0:1], op0=ALU.mult, op1=ALU.add),
                                 reads=[b_hal, b_cw, by], writes=[by])
                            P.op("dve", lambda e, y=y, cj=cj: e.scalar_tensor_tensor(out=y[:, 0:2], in0=hal[:, cj, 0:2], scalar=cw[:, 0, cj:cj + 1], in1=y[:, 0:2], op0=ALU.mult, op1=ALU.add),
                                 reads=[b_hal, b_cw, by], writes=[by])
                        if not is_val:
                            si = gi % 2; gi += 1
                            s = s2[si]; bs = b_s2[si]
                            P.op("act", lambda e, y=y, s=s: e.activation(out=s[:, :], in_=y[:, :], func=AF.Square), reads=[by], writes=[bs])
                            P.op("dve", lambda e, s=s: e.tensor_scalar(out=s[:, :], in0=s[:, :], scalar1=0.044715, scalar2=1.0, op0=ALU.mult, op1=ALU.add), reads=[bs], writes=[bs])
                            P.op("dve", lambda e, y=y, s=s: e.tensor_tensor(out=s[:, :], in0=s[:, :], in1=y[:, :], op=ALU.mult), reads=[bs, by], writes=[bs])
                            P.op("act", lambda e, s=s: e.activation(out=s[:, :], in_=s[:, :], func=AF.Sigmoid, scale=GC), reads=[bs], writes=[bs])
                            P.op("dve", lambda e, y=y, s=s, c=c: e.tensor_tensor(out=G[:, c, :], in0=y[:, :], in1=s[:, :], op=ALU.mult), reads=[bs, by], writes=[b_G[c]])
                        else:
                            oi = vi % 2; vi += 1
                            P.op("dve", lambda e, y=y, c=c, oi=oi: e.tensor_tensor(out=ao[oi][:, :], in0=G[:, c, :], in1=y[:, :], op=ALU.mult), reads=[b_G[c], by], writes=[b_ao[oi]])
                            P.dma("sp", outT[128 * j:128 * j + 128, t0:t0 + TB], ao[oi][:, :], reads=[b_ao[oi]], writes=[b_out])
                        if jobs and tb == 0:
                            jobs.pop(0)()
        while jobs:
            jobs.pop(0)()
        P.stack = old
    barrier(P)

NCORES = 4
DEPTH = 4
_CACHE = {}


def build_program(depth=DEPTH, stop_after=None):
    nc = bass.Bass("TRN2", target_bir_lowering=False)
    st = ExitStack()
    P = Prog(nc, st)
    cst_np = make_consts()
    ext = lambda name, shape: P.dram(name, shape, F32, kind="ExternalInput")
    x = ext("x", [2048, 4096])
    w_in = [ext("w_in%d" % l, [4096, 9272]) for l in range(depth)]
    w_out = [ext("w_out%d" % l, [4096, 4096]) for l in range(depth)]
    w_up = [ext("w_up%d" % l, [4096, 22016]) for l in range(depth)]
    w_down = [ext("w_down%d" % l, [11008, 4096]) for l in range(depth)]
    small = {}
    for name, shape in (("b_forget", [4, 8]), ("s5_a_re", [4, 64, 64]), ("s5_a_im", [4, 64, 64]), ("s5_log_dt", [4, 64]),
                        ("s5_b_re", [4, 64, 64, 16]), ("s5_b_im", [4, 64, 64, 16]), ("s5_c_re", [4, 64, 16, 64]), ("s5_c_im", [4, 64, 16, 64]),
                        ("s5_d", [4, 1024]), ("s5_w_glu", [4, 1024, 1024]), ("cmp_pe_k", [4, 32, 128]), ("cmp_pe_v", [4, 32, 128]),
                        ("cmp_w_k", [4, 32, 128, 128]), ("cmp_w_v", [4, 32, 128, 128]), ("rel_bias", [32, 16]),
                        ("g_out_fox", [4, 1024]), ("g_out_s5", [4, 1024]), ("g_out_nsa", [4, 2048]), ("g_pre_mix", [4, 4096]),
                        ("g_post_mix", [4, 4096]), ("g_pre_ffn", [4, 4096]), ("g_post_ffn", [4, 4096]), ("conv_w", [4, 3, 22016]), ("conv_b", [4, 22016])):
        small[name] = ext(name, shape)
    cst = {k: ext("k_" + k, list(v.shape)) for k, v in cst_np.items()}
    y = P.dram("y", [2048, 4096], F32, kind="ExternalOutput")
    xT = P.dram("s_xT", [4096, 2048], F32); hT = P.dram("s_hT", [4096, 2048], BF16)
    zT = P.dram("s_zT", [9272, 2048], F32); mixT = P.dram("s_mixT", [4096, 2048], F32)
    moT = P.dram("s_moT", [4096, 2048], F32); uT = None
    actT = P.dram("s_actT", [11008, 2048], BF16)
    wdbP = P.dram("s_wdbP", [16, 128, 86, 256], BF16); b_wdb = Buf()
    scr = dict(browS=P.dram("s_browS", [16, BROW], F32), b_browS=Buf(), BM=P.dram("s_BM", [16, 13, 128, 512], BF16), b_BM=Buf(),
               cD=P.dram("s_cD", [8, 2048], F32), b_cD=Buf(), gD=P.dram("s_gD", [48, 2048], F32), b_gD=Buf(),
               bbD=P.dram("s_bbD", [2, 64, 64, 16], F32), b_bbD=Buf(), ygD=P.dram("s_ygD", [1024, 2048], F32), b_ygD=Buf())
    b_xT, b_hT, b_zT, b_mix, b_mo, b_u, b_act, b_y = [Buf() for _ in range(8)]
    C = attn_consts(P, cst)
    setup_bias(P, small["rel_bias"], cst, scr)
    transpose_in(P, "ti", x, xT, b_xT, C.identf, C.b)
    for l in range(depth):
        L = "L%d" % l
        norm_phase(P, L + "n1", xT, 0, 4096, small["g_pre_mix"][l], "bf16", hT, 0, b_xT, b_hT, C.onesb, C.b)
        gemm(P, L + "gi", hT, 4096, 2048, w_in[l], 9272, zT, F32, TB=1024, PW=512)
        fox_phase(P, C, L + "fx", zT, mixT, b_mix, small["b_forget"][l], scr)
        prm = dict(a_re=small["s5_a_re"][l], a_im=small["s5_a_im"][l], log_dt=small["s5_log_dt"][l], b_re=small["s5_b_re"][l],
                   b_im=small["s5_b_im"][l], c_re=small["s5_c_re"][l], c_im=small["s5_c_im"][l], d=small["s5_d"][l], w_glu=small["s5_w_glu"][l])
        s5_phase(P, L + "s5", zT, mixT, b_mix, prm, cst, scr)
        nsa_phase(P, C, L + "ns", zT, mixT, b_mix, small["cmp_pe_k"][l], small["cmp_pe_v"][l], small["cmp_w_k"][l], small["cmp_w_v"][l],
                  small["rel_bias"], scr)
        norm_phase(P, L + "na", mixT, 0, 1024, small["g_out_fox"][l], "bf16", hT, 0, b_mix, b_hT, C.onesb, C.b)
        norm_phase(P, L + "nb", mixT, 1024, 1024, small["g_out_s5"][l], "bf16", hT, 1024, b_mix, b_hT, C.onesb, C.b)
        norm_phase(P, L + "nc", mixT, 2048, 2048, small["g_out_nsa"][l], "bf16", hT, 2048, b_mix, b_hT, C.onesb, C.b)
        gemm(P, L + "go", hT, 4096, 2048, w_out[l], 4096, moT, F32, TB=1024, PW=512)
        norm_phase(P, L + "n2", moT, 0, 4096, small["g_post_mix"][l], "resid", xT, 0, b_mo, b_xT, C.onesb, C.b)
        norm_phase(P, L + "n3", xT, 0, 4096, small["g_pre_ffn"][l], "bf16", hT, 0, b_xT, b_hT, C.onesb, C.b)
        jobs = []
        wdv = w_down[l].rearrange("(c p) n -> p c n", p=128)
        for pn in range(16):
            for k0 in range(0, 86, 8):
                k1 = min(86, k0 + 8)
                jobs.append(lambda pn=pn, k0=k0, k1=k1, wdv=wdv: P.dma("pool", wdbP[pn, :, k0:k1, :], wdv[:, k0:k1, 256 * pn:256 * pn + 256], writes=[b_wdb]))
        gemm_up_conv(P, L + "gu", hT, w_up[l], actT, b_act, small["conv_w"][l], small["conv_b"][l], jobs=jobs)
        gemm(P, L + "gd", actT, 11008, 2048, None, 4096, moT, F32, TB=512, PW=256, Wpan=wdbP, b_wpan=b_wdb)
        norm_phase(P, L + "n4", moT, 0, 4096, small["g_post_ffn"][l], "resid", xT, 0, b_mo, b_xT, C.onesb, C.b)
    transpose_out(P, "to", xT, b_xT, y, b_y, C.identf, C.b)
    P.finish_wait_all("sp", [b_y])
    P.emit()
    return nc, cst_np, st


def kernel(**inputs):
    if "prog" not in _CACHE:
        _CACHE["prog"] = build_program()
    nc, cst_np, _st = _CACHE["prog"]
    f32 = lambda a: np.ascontiguousarray(np.asarray(a, dtype=np.float32))
    shared = {}
    for l in range(DEPTH):
        shared["w_in%d" % l] = f32(inputs["w_in"][l])
        shared["w_out%d" % l] = f32(inputs["w_out"][l])
        shared["w_up%d" % l] = f32(inputs["w_up"][l])
        shared["w_down%d" % l] = f32(inputs["w_down"][l])
    for name in ("b_forget", "s5_a_re", "s5_a_im", "s5_log_dt", "s5_b_re", "s5_b_im", "s5_c_re", "s5_c_im", "s5_d", "s5_w_glu",
                 "cmp_pe_k", "cmp_pe_v", "cmp_w_k", "cmp_w_v", "rel_bias", "g_out_fox", "g_out_s5", "g_out_nsa", "g_pre_mix",
                 "g_post_mix", "g_pre_ffn", "g_post_ffn", "conv_w", "conv_b"):
        shared[name] = f32(inputs[name])
    for k, v in cst_np.items():
        shared["k_" + k] = v
    xs = np.asarray(inputs["x"], dtype=np.float32)
    in_maps = []
    for b in range(NCORES):
        m = dict(shared)
        m["x"] = np.ascontiguousarray(xs[b])
        in_maps.append(m)
    res = run_bass_kernel_spmd(nc, in_maps, core_ids=list(range(NCORES)))
    return np.stack([np.asarray(res.results[b]["y"], dtype=np.float32) for b in range(NCORES)], axis=0)
```

```python
import math


import numpy as np
from contextlib import ExitStack
import concourse.bass as bass
import concourse.mybir as mybir
from concourse.bass_utils import run_bass_kernel_spmd

F32 = mybir.dt.float32
BF16 = mybir.dt.bfloat16
I32 = mybir.dt.int32
AF = mybir.ActivationFunctionType
ALU = mybir.AluOpType
AX = mybir.AxisListType

ENGS = ("pe", "act", "dve", "pool", "sp")
ND = 8


class Buf:
    __slots__ = ("name", "writers", "readers")

    def __init__(self, name=""):
        self.name = name
        self.writers = {}
        self.readers = {}


class Prog:
    def __init__(self, nc, stack):
        self.nc = nc
        self.stack = stack
        self.ops = {e: [] for e in ENGS}
        self.cnt = {e: 0 for e in ENGS}
        self.seen = {e: {} for e in ENGS}
        self.esem = {e: stack.enter_context(nc.semaphore("c_" + e)) for e in ENGS}
        self.dsem = {e: [stack.enter_context(nc.semaphore("d_%s%d" % (e, i))) for i in range(ND)]
                     for e in ("sp", "pool", "act")}
        self.dcnt = {e: [0] * ND for e in self.dsem}
        self.dnext = {e: 0 for e in self.dsem}
        self.semname = {}
        self.nbuf = 0

    def sb(self, name, shape, dtype):
        t = self.stack.enter_context(self.nc.sbuf_tensor(name, list(shape), dtype))
        return t

    def ps(self, name, shape, dtype=F32):
        t = self.stack.enter_context(self.nc.psum_tensor(name, list(shape), dtype))
        return t

    def dram(self, name, shape, dtype, kind="Internal"):
        return self.nc.dram_tensor(name, list(shape), dtype, kind=kind).ap()

    def _collect(self, eng, reads, writes, same_ok=False):
        need = {}

        def add(sem, val):
            if need.get(id(sem), (None, -1))[1] < val:
                need[id(sem)] = (sem, val)

        for b in reads:
            for sem, val in b.writers.values():
                add(sem, val)
        for b in writes:
            for sem, val in b.writers.values():
                add(sem, val)
            for sem, val in b.readers.values():
                add(sem, val)
        waits = []
        own = self.esem[eng]
        for key, (sem, val) in need.items():
            if same_ok and sem is own:
                continue
            if self.seen[eng].get(key, -1) >= val:
                continue
            self.seen[eng][key] = val
            waits.append((sem, val))
        return waits

    def _commit(self, ev, reads, writes):
        sem, val = ev
        for b in reads:
            b.readers[id(sem)] = (sem, val)
        for b in writes:
            b.writers = {id(sem): (sem, val)}
            b.readers = {}

    def op(self, eng, fn, reads=(), writes=()):
        waits = self._collect(eng, reads, writes, same_ok=(eng == "pe"))
        self.cnt[eng] += 1
        ev = (self.esem[eng], self.cnt[eng])
        self.ops[eng].append((fn, waits, (self.esem[eng], 1)))
        self._commit(ev, reads, writes)

    def dma(self, eng, out, in_, reads=(), writes=(), **kw):
        slot = self.dnext[eng]
        self.dnext[eng] = (slot + 1) % ND
        sem = self.dsem[eng][slot]
        waits = self._collect(eng, reads, writes)
        prev = self.dcnt[eng][slot] * 16
        if prev > 0 and self.seen[eng].get(id(sem), -1) < prev:
            self.seen[eng][id(sem)] = prev
            waits.append((sem, prev))
        self.dcnt[eng][slot] += 1
        ev = (sem, prev + 16)

        def fn(e, out=out, in_=in_, kw=kw):
            return e.dma_start(out=out, in_=in_, **kw)

        self.ops[eng].append((fn, waits, (sem, 16)))
        self._commit(ev, reads, writes)

    def finish_wait_all(self, eng, bufs):
        waits = self._collect(eng, bufs, ())
        self.ops[eng].append((None, waits, None))

    def emit(self):
        nc = self.nc
        with nc.Block() as block:
            def run(e, name):
                for fn, waits, inc in self.ops[name]:
                    for sem, val in waits:
                        e.wait_ge(sem, val)
                    if fn is None:
                        continue
                    ins = fn(e)
                    if inc is not None:
                        ins.then_inc(inc[0], inc[1])

            @block.tensor
            def _(e):
                run(e, "pe")

            @block.scalar
            def _(e):
                run(e, "act")

            @block.vector
            def _(e):
                run(e, "dve")

            @block.gpsimd
            def _(e):
                run(e, "pool")

            @block.sync
            def _(e):
                run(e, "sp")


SCALE = 128 ** -0.5
BIG = 4096.0
PAD = 2048
BROW = 4096

def t5_bucket_np(dist):
    n = np.maximum(dist, 0)
    nf = np.maximum(n, 1).astype(np.float32)
    large = 16 + (np.log(nf / np.float32(16)) / np.float32(math.log(128 / 16)) * np.float32(16)).astype(np.int32)
    return np.where(n < 16, n, np.minimum(large, 31))

def make_consts():
    c = {}
    c["ident"] = np.eye(128, dtype=np.float32)
    d = np.arange(BROW) - PAD
    bk = t5_bucket_np(d)
    oh = np.zeros((32, BROW), np.float32)
    oh[bk, np.arange(BROW)] = 1.0
    c["onehot"] = oh
    m = np.zeros((17, 128, 512), np.float32)
    s = np.arange(128)[:, None]; q = np.arange(512)[None, :]
    for k, r in enumerate(range(-4, 4)):
        dist = q - s - 128 * r
        m[k] = np.where((dist >= 0) & (dist < 512), 0.0, -BIG)
    for j in range(4):
        ok = (16 * s + 31) <= (512 * j + q)
        m[9 + j] = np.where(ok, 0.0, -BIG)
        m[9 + j][127] = -BIG
    for r in range(4):
        m[13 + r] = np.where(q - s - 128 * r >= 0, 0.0, -BIG)
    for k in range(9):
        m[k] = m[k][::-1].copy()
    for k in range(9, 13):
        m[k][:127] = m[k][:127][::-1].copy()
    c["masks"] = m
    c["jrev"] = np.eye(128, dtype=np.float32)[::-1].copy()
    j127 = np.zeros((128, 128), np.float32); j127[:127, :127] = np.eye(127, dtype=np.float32)[::-1]
    c["jrev127"] = j127
    bs = np.arange(127) * 16; be = bs + 31
    ss = np.arange(32) * 64
    ov = ((bs[:, None] <= ss[None, :] + 63) & (be[:, None] >= ss[None, :])).astype(np.float32)
    ovp = np.zeros((128, 32), np.float32); ovp[:127] = ov
    c["overlap"] = ovp
    E = np.zeros((32, 2048), np.float32)
    E[np.arange(2048) // 64, np.arange(2048)] = 1.0
    c["expand"] = E
    tq = np.arange(2048)[:, None]; jb = np.arange(32)[None, :]
    cur = tq // 64
    causal = jb * 64 <= tq
    forced = ((jb == 0) | (jb == cur) | (jb == cur - 1)).astype(np.float32)
    c["scorec"] = np.where(causal, 1000.0 * forced, -1e30).astype(np.float32)
    mB = np.zeros((128, 4, 128), np.float32)
    for gl in range(8):
        for g2 in range(2):
            for pr4 in range(4):
                if gl == 2 * pr4 + g2:
                    mB[16 * gl:16 * gl + 16, pr4, 64 * g2:64 * g2 + 64] = 1.0
    c["maskB"] = mB
    c["maskC"] = np.ascontiguousarray(mB.transpose(2, 1, 0))
    c["iota"] = np.arange(1024, dtype=np.float32)[None, :]
    return c


def barrier(P):
    evs = []
    for e in ENGS:
        if P.cnt[e] > 0:
            evs.append((P.esem[e], P.cnt[e]))
    for e in P.dsem:
        for i in range(ND):
            if P.dcnt[e][i] > 0:
                evs.append((P.dsem[e][i], P.dcnt[e][i] * 16))
    for e in ENGS:
        waits = []
        for sem, val in evs:
            if sem is P.esem[e]:
                continue
            if P.seen[e].get(id(sem), -1) >= val:
                continue
            P.seen[e][id(sem)] = val
            waits.append((sem, val))
        if waits:
            P.ops[e].append((None, waits, None))

def gemm(P, tag, actT, K, T, W, N, outT, out_dtype, TB=1024, PW=512, epi=None, Wpan=None, b_wpan=None):
    nc = P.nc
    KC = K // 128
    assert K % 128 == 0 and T % TB == 0 and TB % 512 == 0
    NTS = TB // 512
    with ExitStack() as st:
        old = P.stack; P.stack = st
        act = P.sb(tag + "_act", [128, KC, TB], BF16)
        wts = [P.sb(tag + "_w%d" % i, [128, KC, PW], BF16) for i in range(2)]
        osb = [P.sb(tag + "_o%d" % i, [128, TB], out_dtype) for i in range(2)]
        pss = [P.ps(tag + "_p%d" % i, [128, TB], F32) for i in range(2)]
        b_act = Buf(); b_w = [Buf(), Buf()]; b_o = [Buf(), Buf()]; b_p = [Buf(), Buf()]
        actv = actT.rearrange("(c p) t -> p c t", p=128)
        Wv = W.rearrange("(c p) n -> p c n", p=128) if W is not None else None
        it = 0
        npan = (N + PW - 1) // PW
        KG = 8
        for tb in range(T // TB):
            t0 = tb * TB
            for k0 in range(0, KC, KG):
                k1 = min(KC, k0 + KG)
                P.dma("sp", act[:, k0:k1, :], actv[:, k0:k1, t0:t0 + TB], writes=[b_act])
            for pn in range(npan):
                n0 = pn * PW
                pw = min(PW, N - n0)
                wb = pn % 2
                if Wpan is not None:
                    for k0 in range(0, KC, 32):
                        k1 = min(KC, k0 + 32)
                        P.dma("act", wts[wb][:, k0:k1, :pw], Wpan[pn, :, k0:k1, :pw], reads=[b_wpan], writes=[b_w[wb]])
                else:
                    for k0 in range(0, KC, KG):
                        k1 = min(KC, k0 + KG)
                        P.dma("pool", wts[wb][:, k0:k1, :pw], Wv[:, k0:k1, n0:n0 + pw], writes=[b_w[wb]])
                for c0 in range(0, pw, 128):
                    m = min(128, pw - c0)
                    pb = it % 2; it += 1
                    def mm(e, wb=wb, c0=c0, m=m, pb=pb):
                        ins = None
                        for ts in range(NTS):
                            for k in range(KC):
                                ins = e.matmul(pss[pb][:m, ts * 512:(ts + 1) * 512], lhsT=wts[wb][:, k, c0:c0 + m],
                                               rhs=act[:, k, ts * 512:(ts + 1) * 512], start=(k == 0), stop=(k == KC - 1))
                        return ins
                    P.op("pe", mm, reads=[b_act, b_w[wb]], writes=[b_p[pb]])
                    ob = pb
                    if epi is None:
                        if pb == 0:
                            P.op("act", lambda e, m=m, pb=pb, ob=ob: e.copy(out=osb[ob][:m, :], in_=pss[pb][:m, :]),
                                 reads=[b_p[pb]], writes=[b_o[ob]])
                        else:
                            P.op("dve", lambda e, m=m, pb=pb, ob=ob: e.tensor_copy(out=osb[ob][:m, :], in_=pss[pb][:m, :]),
                                 reads=[b_p[pb]], writes=[b_o[ob]])
                    else:
                        epi(P, pss[pb], b_p[pb], osb[ob], b_o[ob], n0 + c0, m, t0, TB)
                    P.dma("sp", outT[n0 + c0:n0 + c0 + m, t0:t0 + TB], osb[ob][:m, :], reads=[b_o[ob]])
        P.stack = old
    barrier(P)


def dsl(j, n=512):
    return slice(j * n, (j + 1) * n)

class AttnCtx:
    pass

def setup_bias(P, relb, cst, scr):
    nc = P.nc
    with ExitStack() as st:
        old = P.stack; P.stack = st
        rb = P.sb("sb_rb", [32, 16], F32); oh = P.sb("sb_oh", [32, BROW], F32)
        brow = P.sb("sb_brow", [16, BROW], F32)
        msk = P.sb("sb_msk", [128, 13, 512], F32)
        toe = [P.sb("sb_toe%d" % i, [128, 512], F32) for i in range(6)]
        bmo = [P.sb("sb_bmo%d" % i, [128, 512], BF16) for i in range(6)]
        ps = P.ps("sb_ps", [16, 512], F32)
        b_rb, b_oh, b_brow, b_msk, b_ps = Buf(), Buf(), Buf(), Buf(), Buf()
        b_toe = [Buf() for _ in range(6)]; b_bmo = [Buf() for _ in range(6)]
        P.dma("sp", rb[:, :], relb[:, :], writes=[b_rb])
        P.dma("sp", oh[:, :], cst["onehot"][:, :], writes=[b_oh])
        for k0 in range(0, 13, 4):
            k1 = min(13, k0 + 4)
            P.dma("sp", msk[:, k0:k1, :], cst["masks"][k0:k1].rearrange("k p q -> p k q"), writes=[b_msk])
        for c in range(BROW // 512):
            P.op("pe", lambda e, c=c: e.matmul(ps[:, :], lhsT=rb[:, :], rhs=oh[:, dsl(c)], start=True, stop=True),
                 reads=[b_rb, b_oh], writes=[b_ps])
            P.op("act", lambda e, c=c:
                 e.activation(out=brow[:, dsl(c)], in_=ps[:, :], func=AF.Copy, scale=1.0 / SCALE),
                 reads=[b_ps], writes=[b_brow])
        b_browD = scr["b_browS"]
        P.dma("sp", scr["browS"][:, :], brow[:, :], reads=[b_brow], writes=[b_browD])
        it = 0
        bt = scr["browS"].tensor
        for h in range(16):
            for k in range(13):
                i = it % 6; it += 1
                if k < 8:
                    off = h * BROW + PAD - 127 - 128 * (k - 4); apat = [[1, 128], [1, 512]]; np_ = 128
                elif k == 8:
                    off = h * BROW + PAD - 127 + 128; apat = [[1, 128], [1, 512]]; np_ = 128
                else:
                    off = h * BROW + PAD + 512 * (k - 9) - 2047; apat = [[16, 127], [1, 512]]; np_ = 127
                src = bass.AP(bt, off, apat)
                P.dma("sp", toe[i][:np_, :], src, reads=[b_browD], writes=[b_toe[i]], allow_slow_non_contiguous=False)
                P.op("dve", lambda e, i=i, k=k, np_=np_: e.tensor_tensor(out=bmo[i][:np_, :], in0=toe[i][:np_, :], in1=msk[:np_, k, :], op=ALU.add),
                     reads=[b_toe[i], b_msk], writes=[b_bmo[i]])
                P.dma("act", scr["BM"][h, k, :np_, :], bmo[i][:np_, :], reads=[b_bmo[i]], writes=[scr["b_BM"]])
        P.stack = old
    barrier(P)


def attn_consts(P, cst):
    C = AttnCtx()
    C.identf = P.sb("c_identf", [128, 128], F32)
    C.identb = P.sb("c_identb", [128, 128], BF16)
    C.onesb = P.sb("c_onesb", [128, 128], BF16)
    C.onesf = P.sb("c_onesf", [128, 128], F32)
    C.ov = P.sb("c_ov", [128, 32], F32)
    C.Eb = P.sb("c_E", [32, 2048], BF16)
    C.scorec = P.sb("c_scorec", [128, 16, 32], F32)
    C.foxm = P.sb("c_foxm", [128, 4, 512], BF16)
    C.b = Buf()
    C.jrevb = P.sb("c_jrevb", [128, 128], BF16)
    C.jrev127b = P.sb("c_jrev127b", [128, 128], BF16)
    P.dma("pool", C.jrevb[:, :], cst["jrev"][:, :], writes=[C.b])
    P.dma("pool", C.jrev127b[:, :], cst["jrev127"][:, :], writes=[C.b])
    P.dma("sp", C.identf[:, :], cst["ident"][:, :], writes=[C.b])
    P.dma("pool", C.identb[:, :], cst["ident"][:, :], writes=[C.b])
    P.dma("pool", C.Eb[:, :], cst["expand"][:, :], writes=[C.b])
    P.dma("sp", C.ov[:, :], cst["overlap"][:, :], writes=[C.b])
    P.dma("sp", C.scorec[:, :, :], cst["scorec"].rearrange("(c p) j -> p c j", p=128), writes=[C.b])
    P.dma("pool", C.foxm[:, :, :], cst["masks"][13:17].rearrange("k p q -> p k q"), writes=[C.b])
    P.op("dve", lambda e: e.memset(C.onesb[:, :], 1.0), writes=[C.b])
    P.op("dve", lambda e: e.memset(C.onesf[:, :], 1.0), writes=[C.b])
    return C


class AttnRes:
    def __init__(self, P, tag):
        self.ps_s = [P.ps(tag + "_pss%d" % i, [128, 512], F32) for i in range(3)]
        self.ps_o = [P.ps(tag + "_pso%d" % i, [128, 512], F32) for i in range(2)]
        self.ps_d = [P.ps(tag + "_psd%d" % i, [128, 512], F32) for i in range(2)]
        self.ps_m = P.ps(tag + "_psm", [128, 512], F32)
        self.ps_t = self.ps_m[:, 0:256].bitcast(BF16)
        self.b_s = [Buf(), Buf(), Buf()]; self.b_o = [Buf(), Buf()]; self.b_d = [Buf(), Buf()]
        self.b_m = Buf(); self.b_t = self.b_m
        self.pt = [P.sb(tag + "_pt%d" % i, [128, 512], BF16) for i in range(4)]
        self.b_pt = [Buf() for _ in range(4)]
        self.tmp = [P.sb(tag + "_tmp%d" % i, [128, 512], F32) for i in range(2)]
        self.b_tmp = [Buf(), Buf()]
        self.rd = [P.sb(tag + "_rd%d" % i, [128, 512], F32) for i in range(2)]
        self.b_rd = [Buf(), Buf()]
        self.t1 = [P.sb(tag + "_t1%d" % i, [128, 512], F32) for i in range(2)]
        self.b_t1 = [Buf(), Buf()]
        self.si = 0; self.pi = 0; self.oi = 0; self.ti = 0


def attn_qblock(P, C, R, QT, b_q, j, tiles, epilogue):
    oi = R.oi % 2; R.oi += 1
    n = len(tiles)
    pend = []

    def emit_s(t):
        si = R.si % 3; R.si += 1
        m = t["m"]
        ex = t.get("extra", [])
        def f(e, t=t, si=si, m=m, ex=ex):
            ins = e.matmul(R.ps_s[si][:m, :], lhsT=t["KT"], rhs=QT[:, dsl(j)], start=True, stop=(len(ex) == 0))
            for q, (l, r, _) in enumerate(ex):
                ins = e.matmul(R.ps_s[si][:m, :], lhsT=l, rhs=r, start=False, stop=(q == len(ex) - 1))
            return ins
        rd = [b_q, t["bK"]] + [b for (_, _, bs) in ex for b in bs]
        P.op("pe", f, reads=rd, writes=[R.b_s[si]])
        pi = R.pi % 4; R.pi += 1
        bias = t.get("bias")
        if t.get("fox") is not None:
            cbc, bcbc = t["fox"]
            ti = R.ti % 2; R.ti += 1
            P.op("dve", lambda e, si=si, ti=ti, m=m, cbc=cbc: e.scalar_tensor_tensor(
                out=R.tmp[ti][:m, :], in0=R.ps_s[si][:m, :], scalar=SCALE, in1=cbc[:m, dsl(j)], op0=ALU.mult, op1=ALU.subtract),
                reads=[R.b_s[si], bcbc], writes=[R.b_tmp[ti]])
            P.op("act", lambda e, ti=ti, pi=pi, m=m, bias=bias: e.activation(
                out=R.pt[pi][:m, :], in_=R.tmp[ti][:m, :], func=AF.Exp, bias=bias, scale=1.0),
                reads=[R.b_tmp[ti], t["bbias"]], writes=[R.b_pt[pi]])
        else:
            kw = {}
            rds = [R.b_s[si]]
            if bias is not None:
                kw["bias"] = bias; rds.append(t["bbias"])
            P.op("act", lambda e, si=si, pi=pi, m=m, kw=kw: e.activation(
                out=R.pt[pi][:m, :], in_=R.ps_s[si][:m, :], func=AF.Exp, scale=SCALE, **kw),
                reads=rds, writes=[R.b_pt[pi]])
        return pi

    def emit_pv(t, pi, first, last):
        m = t["m"]
        def f(e, t=t, pi=pi, m=m):
            e.matmul(R.ps_o[oi][:, :], lhsT=t["V"], rhs=R.pt[pi][:m, :], start=first, stop=last)
            return e.matmul(R.ps_d[oi][:, :], lhsT=C.onesb[:m, :], rhs=R.pt[pi][:m, :], start=first, stop=last)
        P.op("pe", f, reads=[t["bV"], R.b_pt[pi], C.b], writes=[R.b_o[oi], R.b_d[oi]])

    pis = []
    LA = 2
    done = 0
    for q, t in enumerate(tiles):
        pis.append(emit_s(t))
        if q >= LA:
            emit_pv(tiles[done], pis[done], done == 0, done == n - 1)
            done += 1
    while done < n:
        emit_pv(tiles[done], pis[done], done == 0, done == n - 1)
        done += 1
    epilogue(oi, pis)


def load_rows_bf16(P, dst, zT, r0, buf):
    P.dma("pool", dst[:, :], zT[r0:r0 + 128, :], writes=[buf])


def make_V(P, C, R, VT, b_vt, V, b_v, nchunks=16):
    for g4 in range(0, nchunks, 4):
        def f(e, g4=g4):
            ins = None
            for q in range(4):
                ins = e.transpose(out=R.ps_t[:, q * 128:(q + 1) * 128], in_=VT[:, (g4 + q) * 128:(g4 + q + 1) * 128], identity=C.identb[:, :])
            return ins
        P.op("pe", f, reads=[b_vt, C.b], writes=[R.b_t])
        P.op("dve", lambda e, g4=g4: e.tensor_copy(out=V[:, g4:g4 + 4, :], in_=R.ps_t[:, :].rearrange("p (q d) -> p q d", q=4)),
             reads=[R.b_t], writes=[b_v])


def fox_phase(P, C, tag, zT, mixT, b_mix, bfor, scr):
    nc = P.nc
    with ExitStack() as st:
        old = P.stack; P.stack = st
        R = AttnRes(P, tag)
        zf = P.sb(tag + "_zf", [8, 2048], F32); ee = P.sb(tag + "_ee", [8, 2048], F32)
        ll = P.sb(tag + "_ll", [8, 2048], F32); cp = P.sb(tag + "_cp", [8, 2048], F32)
        on8 = P.sb(tag + "_on8", [8, 2048], F32)
        bneg = P.sb(tag + "_bn", [8, 1], F32); bf = P.sb(tag + "_bf", [8, 1], F32)
        ccol = P.sb(tag + "_ccol", [128, 8, 16], F32)
        cbc = [P.sb(tag + "_cbc%d" % i, [128, 2048], F32) for i in range(2)]
        QT = [P.sb(tag + "_q%d" % i, [128, 2048], BF16) for i in range(2)]
        KT = [P.sb(tag + "_k%d" % i, [128, 2048], BF16) for i in range(2)]
        VT = [P.sb(tag + "_vt%d" % i, [128, 2048], BF16) for i in range(2)]
        V = [P.sb(tag + "_v%d" % i, [128, 16, 128], BF16) for i in range(2)]
        ob = [P.sb(tag + "_ob%d" % i, [128, 512], F32) for i in range(2)]
        b_ob = [Buf(), Buf()]
        b_zf, b_e, b_l, b_cp, b_on, b_bn, b_bf, b_ccol = [Buf() for _ in range(8)]
        b_cbc = [Buf(), Buf()]; b_q = [Buf(), Buf()]; b_k = [Buf(), Buf()]; b_vt = [Buf(), Buf()]; b_v = [Buf(), Buf()]
        b_cD = scr["b_cD"]
        P.dma("sp", zf[:, :], zT[3072:3080, :], writes=[b_zf])
        P.dma("sp", bf[:, :], bfor.rearrange("(h o) -> h o", o=1), writes=[b_bf])
        P.op("dve", lambda e: e.memset(on8[:, :], 1.0), writes=[b_on])
        P.op("act", lambda e: e.mul(out=bneg[:, :], in_=bf[:, :], mul=-1.0), reads=[b_bf], writes=[b_bn])
        P.op("act", lambda e: e.activation(out=ee[:, :], in_=zf[:, :], func=AF.Exp, bias=bneg[:, :], scale=-1.0),
             reads=[b_zf, b_bn], writes=[b_e])
        P.op("act", lambda e: e.activation(out=ll[:, :], in_=ee[:, :], func=AF.Ln, bias=1.0, scale=1.0),
             reads=[b_e], writes=[b_l])
        P.op("dve", lambda e: e.tensor_tensor_scan(out=cp[:, :], data0=on8[:, :], data1=ll[:, :], initial=0.0, op0=ALU.mult, op1=ALU.add),
             reads=[b_on, b_l], writes=[b_cp])
        P.dma("sp", scr["cD"][:, :], cp[:, :], reads=[b_cp], writes=[b_cD])
        P.dma("sp", ccol[:, :, :], scr["cD"].rearrange("h (c p) -> p h c", p=128), reads=[b_cD], writes=[b_ccol], allow_slow_non_contiguous=True)
        oc = 0
        for h in range(8):
            i = h % 2
            P.dma("sp", cbc[i][:, :], scr["cD"][h:h + 1, :].partition_broadcast(128), reads=[b_cD], writes=[b_cbc[i]])
            load_rows_bf16(P, QT[i], zT, 128 * h, b_q[i])
            load_rows_bf16(P, KT[i], zT, 1024 + 128 * h, b_k[i])
            load_rows_bf16(P, VT[i], zT, 2048 + 128 * h, b_vt[i])
            make_V(P, C, R, VT[i], b_vt[i], V[i], b_v[i])
            for j in range(4):
                tiles = []
                for kc in range(4 * j + 4):
                    t = dict(KT=KT[i][:, kc * 128:(kc + 1) * 128], bK=b_k[i], V=V[i][:, kc, :], bV=b_v[i], m=128,
                             fox=(cbc[i], b_cbc[i]), bias=ccol[:, h, kc:kc + 1], bbias=b_ccol)
                    r = kc - 4 * j
                    if r >= 0:
                        t["extra"] = [(C.identb[:, :], C.foxm[:, r, :], [C.b])]
                    tiles.append(t)
                def epi(oi, pis, h=h, j=j):
                    nonlocal oc
                    o = oc % 2; oc += 1
                    P.op("dve", lambda e, oi=oi, o=o: e.reciprocal(out=R.rd[o][:, :], in_=R.ps_d[oi][:, :]), reads=[R.b_d[oi]], writes=[R.b_rd[o]])
                    P.op("dve", lambda e, oi=oi, o=o: e.tensor_tensor(out=ob[o][:, :], in0=R.ps_o[oi][:, :], in1=R.rd[o][:, :], op=ALU.mult),
                         reads=[R.b_o[oi], R.b_rd[o]], writes=[b_ob[o]])
                    P.dma("sp", mixT[128 * h:128 * h + 128, dsl(j)], ob[o][:, :], reads=[b_ob[o]], writes=[b_mix])
                attn_qblock(P, C, R, QT[i], b_q[i], j, tiles, epi)
        P.stack = old
    barrier(P)


def nsa_phase(P, C, tag, zT, mixT, b_mix, pek, pev, wk, wv, relb, scr):
    Q0, KC0, VC0, KS0, VS0, KW0, VW0, ZG0 = 4104, 6152, 6664, 7176, 7688, 8200, 8712, 9224
    with ExitStack() as st:
        old = P.stack; P.stack = st
        zg = P.sb(tag + "_zg", [48, 2048], F32); gs = P.sb(tag + "_gs", [48, 2048], F32)
        b_zg, b_gs = Buf(), Buf()
        P.dma("sp", zg[:, :], zT[ZG0:ZG0 + 48, :], writes=[b_zg])
        P.op("act", lambda e: e.activation(out=gs[:, :], in_=zg[:, :], func=AF.Sigmoid), reads=[b_zg], writes=[b_gs])
        P.dma("sp", scr["gD"][:, :], gs[:, :], reads=[b_gs], writes=[scr["b_gD"]])
        P.stack = old
    barrier(P)
    with ExitStack() as st:
        old = P.stack; P.stack = st
        R = AttnRes(P, tag)
        cbias = P.sb(tag + "_cbias", [128, 16], F32); b_cb = Buf()
        P.dma("sp", cbias[:, :], relb[31:32, :].partition_broadcast(128), writes=[b_cb])
        wck = P.sb(tag + "_wck", [128, 32, 128], BF16); wcv = P.sb(tag + "_wcv", [128, 32, 128], BF16)
        peT = P.sb(tag + "_peT", [128, 2, 32], F32)
        b_wc, b_pe = Buf(), Buf()
        for l0 in range(0, 32, 8):
            P.dma("pool", wck[:, l0:l0 + 8, :], wk[l0:l0 + 8].rearrange("l d e -> d l e"), writes=[b_wc])
            P.dma("pool", wcv[:, l0:l0 + 8, :], wv[l0:l0 + 8].rearrange("l d e -> d l e"), writes=[b_wc])
        P.dma("sp", peT[:, 0, :], pek.rearrange("l d -> d l"), writes=[b_pe], allow_slow_non_contiguous=True)
        P.dma("sp", peT[:, 1, :], pev.rearrange("l d -> d l"), writes=[b_pe], allow_slow_non_contiguous=True)
        names = ["kc", "vc", "ks", "vs", "kw", "vw"]
        row0 = dict(kc=KC0, vc=VC0, ks=KS0, vs=VS0, kw=KW0, vw=VW0)
        XT = {n: P.sb(tag + "_x" + n, [128, 2048], BF16) for n in names}
        b_x = {n: Buf() for n in names}
        XA = {n: P.sb(tag + "_a" + n, [128, 2048], BF16) for n in ("kc", "vc")}
        XB = {n: P.sb(tag + "_b" + n, [128, 2048], BF16) for n in ("kc", "vc")}
        b_xa = {n: Buf() for n in XA}
        Vs = P.sb(tag + "_Vs", [128, 16, 128], BF16); Vw = P.sb(tag + "_Vw", [128, 16, 128], BF16)
        b_Vs, b_Vw = Buf(), Buf()
        kcT = P.sb(tag + "_kcT", [128, 128], BF16); vcm = P.sb(tag + "_vcm", [128, 128], BF16)
        b_kcT, b_vcm = Buf(), Buf()
        QT = [P.sb(tag + "_q%d" % r, [128, 2048], BF16) for r in range(4)]
        b_q = [Buf() for _ in range(4)]
        Pn = [P.sb(tag + "_pn%d" % r, [128, 2048], F32) for r in range(4)]
        b_pn = [Buf() for _ in range(4)]
        acc = [P.sb(tag + "_acc%d" % r, [128, 2048], F32) for r in range(4)]
        b_acc = [Buf() for _ in range(4)]
        XF = {"kc": acc[0], "vc": acc[1]}
        b_xf = {"kc": b_acc[0], "vc": b_acc[1]}
        _bm = P.sb(tag + "_bm", [128, 13, 512], BF16); _bbm = Buf()
        BMh = [_bm, _bm]
        b_bm = [_bbm, _bbm]
        gbc = [P.sb(tag + "_gbc%d" % i, [128, 2048], F32) for i in range(2)]
        b_gbc = [Buf(), Buf()]
        selmT = P.sb(tag + "_selmT", [32, 2048], BF16); b_selmT = Buf()
        sc = P.sb(tag + "_sc", [128, 32], F32); sc2 = P.sb(tag + "_sc2", [128, 32], F32)
        m1 = P.sb(tag + "_m1", [128, 8], F32); m2 = P.sb(tag + "_m2", [128, 8], F32)
        selm = P.sb(tag + "_selm", [128, 32], F32)
        b_sc, b_sc2, b_m1, b_m2, b_selm = [Buf() for _ in range(5)]
        gi = [0]; bmi = [0]

        def branch_epilogue(r, hh, br, j, first):
            def epi(oi, pis):
                o = R.oi % 2
                P.op("dve", lambda e, oi=oi, o=o: e.tensor_scalar_max(out=R.rd[o][:, :], in0=R.ps_d[oi][:, :], scalar1=1e-30),
                     reads=[R.b_d[oi]], writes=[R.b_rd[o]])
                P.op("dve", lambda e, o=o: e.reciprocal(out=R.rd[o][:, :], in_=R.rd[o][:, :]), reads=[R.b_rd[o]], writes=[R.b_rd[o]])
                if br == 0:
                    pi = pis[0]
                    P.op("dve", lambda e, o=o, pi=pi: e.tensor_tensor(out=Pn[r][:127, dsl(j)], in0=R.pt[pi][:127, :], in1=R.rd[o][:127, :], op=ALU.mult),
                         reads=[R.b_pt[pi], R.b_rd[o]], writes=[b_pn[r]])
                P.op("dve", lambda e, oi=oi, o=o: e.tensor_tensor(out=R.t1[o][:, :], in0=R.ps_o[oi][:, :], in1=R.rd[o][:, :], op=ALU.mult),
                     reads=[R.b_o[oi], R.b_rd[o]], writes=[R.b_t1[o]])
                g = gcur[0]
                if first:
                    P.op("dve", lambda e, o=o, g=g: e.tensor_tensor(out=acc[r][:, dsl(j)], in0=R.t1[o][:, :], in1=gbc[g][:, dsl(j)], op=ALU.mult),
                         reads=[R.b_t1[o], b_gbc[g]], writes=[b_acc[r]])
                else:
                    P.op("dve", lambda e, o=o, g=g: e.tensor_tensor(out=R.t1[o][:, :], in0=R.t1[o][:, :], in1=gbc[g][:, dsl(j)], op=ALU.mult),
                         reads=[R.b_t1[o], b_gbc[g]], writes=[R.b_t1[o]])
                    P.op("dve", lambda e, o=o: e.tensor_tensor(out=acc[r][:, dsl(j)], in0=acc[r][:, dsl(j)], in1=R.t1[o][:, :], op=ALU.add),
                         reads=[R.b_t1[o], b_acc[r]], writes=[b_acc[r]])
            return epi

        gcur = [0]

        def load_gate(hh, br):
            g = gi[0] % 2; gi[0] += 1
            gcur[0] = g
            row = hh * 3 + br
            P.dma("sp", gbc[g][:, :], scr["gD"][row:row + 1, :].partition_broadcast(128), reads=[scr["b_gD"]], writes=[b_gbc[g]])

        def load_bm(hh):
            i = bmi[0] % 2; bmi[0] += 1
            P.dma("sp", BMh[i][:, 0:9, :], scr["BM"][hh, 0:9].rearrange("k p q -> p k q"), reads=[scr["b_BM"]], writes=[b_bm[i]])
            P.dma("sp", BMh[i][:127, 9:13, :], scr["BM"][hh, 9:13, :127].rearrange("k p q -> p k q"), reads=[scr["b_BM"]], writes=[b_bm[i]])
            return i

        for g in range(4):
            for n in names:
                load_rows_bf16(P, XT[n], zT, row0[n] + 128 * g, b_x[n])
            for n in ("kc", "vc"):
                P.dma("sp", XF[n][:, :], zT[row0[n] + 128 * g: row0[n] + 128 * g + 128, :], writes=[b_xf[n]])
            for r in range(4):
                load_rows_bf16(P, QT[r], zT, Q0 + 128 * (4 * g + r), b_q[r])
            make_V(P, C, R, XT["vs"], b_x["vs"], Vs, b_Vs)
            make_V(P, C, R, XT["vw"], b_x["vw"], Vw, b_Vw)
            for qi, n in enumerate(("kc", "vc")):
                P.op("dve", lambda e, n=n, qi=qi: e.tensor_tensor(
                    out=XA[n][:, :].rearrange("p (c l) -> p c l", l=16), in0=XF[n][:, :].rearrange("p (c l) -> p c l", l=16),
                    in1=peT[:, qi, 0:16].unsqueeze(1).to_broadcast([128, 128, 16]), op=ALU.add),
                    reads=[b_xf[n], b_pe], writes=[b_xa[n]])
                P.op("dve", lambda e, n=n, qi=qi: e.tensor_tensor(
                    out=XB[n][:, :].rearrange("p (c l) -> p c l", l=16), in0=XF[n][:, :].rearrange("p (c l) -> p c l", l=16),
                    in1=peT[:, qi, 16:32].unsqueeze(1).to_broadcast([128, 128, 16]), op=ALU.add),
                    reads=[b_xf[n], b_pe], writes=[b_xa[n]])
            def f_kc(e):
                ins = None
                for l in range(32):
                    X = XA["kc"] if l < 16 else XB["kc"]
                    ins = e.matmul(R.ps_m[:, 0:127], lhsT=wck[:, l, :], rhs=X[:, l:l + 16 * 126 + 1:16], start=(l == 0), stop=(l == 31))
                return ins
            P.op("pe", f_kc, reads=[b_wc, b_xa["kc"]], writes=[R.b_m])
            P.op("dve", lambda e: e.memset(kcT[:, :], 0.0), writes=[b_kcT])
            P.op("dve", lambda e: e.tensor_copy(out=kcT[:, 0:127], in_=R.ps_m[:, 0:127]), reads=[R.b_m], writes=[b_kcT])
            def f_vc(e):
                ins = None
                for l in range(32):
                    X = XA["vc"] if l < 16 else XB["vc"]
                    ins = e.matmul(R.ps_m[:127, 0:128], lhsT=X[:, l:l + 16 * 126 + 1:16], rhs=wcv[:, l, :], start=(l == 0), stop=(l == 31))
                return ins
            P.op("pe", f_vc, reads=[b_wc, b_xa["vc"]], writes=[R.b_m])
            P.op("dve", lambda e: e.memset(vcm[:, :], 0.0), writes=[b_vcm])
            P.op("dve", lambda e: e.tensor_copy(out=vcm[:127, :], in_=R.ps_m[:127, 0:128]), reads=[R.b_m], writes=[b_vcm])
            bmsel = {}
            for r in range(4):
                hh = 4 * g + r
                bi = load_bm(hh); bmsel[r] = bi
                load_gate(hh, 0)
                for j in range(4):
                    t = dict(KT=kcT[:, 0:127], bK=b_kcT, V=vcm[:127, :], bV=b_vcm, m=127,
                             extra=[(C.jrev127b[:127, :127], BMh[bi][:127, 9 + j, :], [C.b, b_bm[bi]])])
                    attn_qblock(P, C, R, QT[r], b_q[r], j, [t], branch_epilogue(r, hh, 0, j, True))
                if r % 2 == 1 and r < 3:
                    pass
            for tc in range(16):
                def f_imp(e, tc=tc):
                    ins = None
                    for r in range(4):
                        ins = e.matmul(R.ps_m[:, 0:32], lhsT=Pn[r][:127, tc * 128:(tc + 1) * 128], rhs=C.ov[:127, :], start=(r == 0), stop=(r == 3))
                    return ins
                P.op("pe", f_imp, reads=b_pn + [C.b], writes=[R.b_m])
                P.op("dve", lambda e, tc=tc: e.tensor_tensor(out=sc[:, :], in0=R.ps_m[:, 0:32], in1=C.scorec[:, tc, :], op=ALU.add),
                     reads=[R.b_m, C.b], writes=[b_sc])
                P.op("dve", lambda e: e.max(out=m1[:, :], in_=sc[:, :]), reads=[b_sc], writes=[b_m1])
                P.op("dve", lambda e: e.match_replace(out=sc2[:, :], in_to_replace=m1[:, :], in_values=sc[:, :], imm_value=-3.0e38),
                     reads=[b_sc, b_m1], writes=[b_sc2])
                P.op("dve", lambda e: e.max(out=m2[:, :], in_=sc2[:, :]), reads=[b_sc2], writes=[b_m2])
                P.op("dve", lambda e: e.tensor_scalar(out=selm[:, :], in0=sc[:, :], scalar1=m2[:, 7:8], scalar2=None, op0=ALU.is_ge),
                     reads=[b_sc, b_m2], writes=[b_selm])
                P.op("dve", lambda e: e.tensor_scalar(out=selm[:, :], in0=selm[:, :], scalar1=1.0, scalar2=BIG, op0=ALU.subtract, op1=ALU.mult),
                     reads=[b_selm], writes=[b_selm])
                P.op("pe", lambda e: e.transpose(out=R.ps_m[:32, 128:256], in_=selm[:, :], identity=C.identf[:, :]),
                     reads=[b_selm, C.b], writes=[R.b_m])
                P.op("dve", lambda e, tc=tc: e.tensor_copy(out=selmT[:, tc * 128:(tc + 1) * 128], in_=R.ps_m[:32, 128:256]),
                     reads=[R.b_m], writes=[b_selmT])
            for r in range(4):
                hh = 4 * g + r
                bi = load_bm(hh)
                load_gate(hh, 1)
                for j in range(4):
                    tiles = []
                    for kc in range(4 * j + 4):
                        rr = kc - 4 * j
                        ex = [(C.Eb[:, kc * 128:(kc + 1) * 128], selmT[:, dsl(j)], [C.b, b_selmT])]
                        t = dict(KT=XT["ks"][:, kc * 128:(kc + 1) * 128], bK=b_x["ks"], V=Vs[:, kc, :], bV=b_Vs, m=128)
                        if rr >= -1:
                            kidx = 8 if rr == -1 else rr + 4
                            ex.append((C.jrevb[:, :], BMh[bi][:, kidx, :], [C.b, b_bm[bi]]))
                        else:
                            t["bias"] = cbias[:, hh:hh + 1]; t["bbias"] = b_cb
                        t["extra"] = ex
                        tiles.append(t)
                    attn_qblock(P, C, R, QT[r], b_q[r], j, tiles, branch_epilogue(r, hh, 1, j, False))
                load_gate(hh, 2)
                for j in range(4):
                    tiles = []
                    for kc in range(max(0, 4 * j - 4), 4 * j + 4):
                        rr = kc - 4 * j
                        t = dict(KT=XT["kw"][:, kc * 128:(kc + 1) * 128], bK=b_x["kw"], V=Vw[:, kc, :], bV=b_Vw, m=128,
                                 extra=[(C.jrevb[:, :], BMh[bi][:, rr + 4, :], [C.b, b_bm[bi]])])
                        tiles.append(t)
                    attn_qblock(P, C, R, QT[r], b_q[r], j, tiles, branch_epilogue(r, hh, 2, j, False))
                for j in range(4):
                    P.dma("sp", mixT[2048 + 128 * hh: 2048 + 128 * hh + 128, dsl(j)], acc[r][:, dsl(j)], reads=[b_acc[r]], writes=[b_mix])
        P.stack = old
    barrier(P)


TWO_PI = 2.0 * math.pi
GC = 1.5957691216057308

def s5_phase(P, tag, zT, mixT, b_mix, prm, cst, scr):
    U0 = 3080
    L = 512
    with ExitStack() as st:
        old = P.stack; P.stack = st
        cnt = [0]
        def T(shape, dt=F32):
            cnt[0] += 1
            return P.sb("%s_t%d" % (tag, cnt[0]), shape, dt)
        defer = [None]
        def _rec(eng, fn, reads, writes):
            if defer[0] is not None:
                defer[0].append((eng, fn, list(reads), list(writes)))
            else:
                P.op(eng, fn, reads=reads, writes=writes)
        def dve(fn, reads, writes):
            _rec("dve", fn, reads, writes)
        def pool(fn, reads, writes):
            _rec("pool", fn, reads, writes)
        def act(fn, reads, writes):
            _rec("act", fn, reads, writes)
        def pe(fn, reads, writes):
            _rec("pe", fn, reads, writes)

        def trig(y, by, shape, eng_name="dve"):
            op = dve if eng_name == "dve" else pool
            ki = T(shape, I32); kf = T(shape); f = T(shape); m = T(shape); fc = T(shape)
            s = T(shape); c = T(shape)
            b = Buf()
            def sl(t):
                return t[tuple(slice(None) for _ in shape)]
            op(lambda e: e.tensor_copy(out=sl(ki), in_=y), [by], [b])
            op(lambda e: e.tensor_copy(out=sl(kf), in_=sl(ki)), [b], [b])
            op(lambda e: e.tensor_tensor(out=sl(f), in0=y, in1=sl(kf), op=ALU.subtract), [by, b], [b])
            def wrap(t):
                op(lambda e: e.tensor_single_scalar(out=sl(m), in_=sl(t), scalar=0.5, op=ALU.is_gt), [b], [b])
                op(lambda e: e.tensor_tensor(out=sl(t), in0=sl(t), in1=sl(m), op=ALU.subtract), [b], [b])
                op(lambda e: e.tensor_single_scalar(out=sl(m), in_=sl(t), scalar=-0.5, op=ALU.is_lt), [b], [b])
                op(lambda e: e.tensor_tensor(out=sl(t), in0=sl(t), in1=sl(m), op=ALU.add), [b], [b])
            wrap(f)
            op(lambda e: e.tensor_scalar_add(out=sl(fc), in0=sl(f), scalar1=0.25), [b], [b])
            wrap(fc)
            act(lambda e: e.activation(out=sl(s), in_=sl(f), func=AF.Sin, scale=TWO_PI), [b], [b])
            act(lambda e: e.activation(out=sl(c), in_=sl(fc), func=AF.Sin, scale=TWO_PI), [b], [b])
            return s, c, b

        lr = T([128, 32]); nlr = T([128, 32]); angn = T([128, 32]); lb_re = T([128, 32]); lb_im = T([128, 32])
        LB = [T([128, 32, 128], BF16) for _ in range(2)]
        LC = [T([128, 32, 128], BF16) for _ in range(2)]
        ps_tr = [P.ps("%s_ptr%d" % (tag, i), [128, 512], F32) for i in range(2)]
        st2 = ExitStack(); st2.__enter__(); P.stack = st2
        A_re = T([128, 32]); A_im = T([128, 32]); ldt = T([128, 32])
        bp = Buf()
        P.dma("sp", A_re[:, :], prm["a_re"].rearrange("(pr g2) p -> g2 p pr", g2=2)[0], writes=[bp], allow_slow_non_contiguous=True) if False else None
        for g2 in range(2):
            P.dma("sp", A_re[64 * g2:64 * g2 + 64, :], prm["a_re"].rearrange("(pr g2) p -> g2 p pr", g2=2)[g2], writes=[bp], allow_slow_non_contiguous=True)
            P.dma("sp", A_im[64 * g2:64 * g2 + 64, :], prm["a_im"].rearrange("(pr g2) p -> g2 p pr", g2=2)[g2], writes=[bp], allow_slow_non_contiguous=True)
            P.dma("sp", ldt[64 * g2:64 * g2 + 64, :], prm["log_dt"].rearrange("(pr g2) -> g2 pr", g2=2)[g2:g2 + 1, :].partition_broadcast(64), writes=[bp], allow_slow_non_contiguous=True)
        lam_re = T([128, 32]); dtt = T([128, 32]); mag = T([128, 32])
        dve(lambda e: e.tensor_scalar_min(out=lam_re[:, :], in0=A_re[:, :], scalar1=-1e-4), [bp], [bp])
        act(lambda e: e.activation(out=dtt[:, :], in_=ldt[:, :], func=AF.Exp), [bp], [bp])
        dve(lambda e: e.tensor_tensor(out=lr[:, :], in0=lam_re[:, :], in1=dtt[:, :], op=ALU.mult), [bp], [bp])
        dve(lambda e: e.tensor_scalar_mul(out=nlr[:, :], in0=lr[:, :], scalar1=-1.0), [bp], [bp])
        dve(lambda e: e.scalar_tensor_tensor(out=angn[:, :], in0=A_im[:, :], scalar=1.0 / TWO_PI, in1=dtt[:, :], op0=ALU.mult, op1=ALU.mult), [bp], [bp])
        act(lambda e: e.activation(out=mag[:, :], in_=lr[:, :], func=AF.Exp), [bp], [bp])
        s0, c0, bt0 = trig(angn[:, :], bp, [128, 32])
        den = T([128, 32]); nr = T([128, 32]); t1 = T([128, 32]); t2 = T([128, 32])
        cf_re = T([128, 32]); cf_im = T([128, 32])
        dve(lambda e: e.tensor_tensor(out=lb_re[:, :], in0=mag[:, :], in1=c0[:, :], op=ALU.mult), [bp, bt0], [bp])
        dve(lambda e: e.tensor_tensor(out=lb_im[:, :], in0=mag[:, :], in1=s0[:, :], op=ALU.mult), [bp, bt0], [bp])
        dve(lambda e: e.tensor_tensor(out=den[:, :], in0=lam_re[:, :], in1=lam_re[:, :], op=ALU.mult), [bp], [bp])
        dve(lambda e: e.tensor_tensor(out=t1[:, :], in0=A_im[:, :], in1=A_im[:, :], op=ALU.mult), [bp], [bp])
        dve(lambda e: e.tensor_tensor(out=den[:, :], in0=den[:, :], in1=t1[:, :], op=ALU.add), [bp], [bp])
        dve(lambda e: e.reciprocal(out=den[:, :], in_=den[:, :]), [bp], [bp])
        dve(lambda e: e.tensor_scalar_add(out=nr[:, :], in0=lb_re[:, :], scalar1=-1.0), [bp], [bp])
        dve(lambda e: e.tensor_tensor(out=t1[:, :], in0=nr[:, :], in1=lam_re[:, :], op=ALU.mult), [bp], [bp])
        dve(lambda e: e.tensor_tensor(out=t2[:, :], in0=lb_im[:, :], in1=A_im[:, :], op=ALU.mult), [bp], [bp])
        dve(lambda e: e.tensor_tensor(out=t1[:, :], in0=t1[:, :], in1=t2[:, :], op=ALU.add), [bp], [bp])
        dve(lambda e: e.tensor_tensor(out=cf_re[:, :], in0=t1[:, :], in1=den[:, :], op=ALU.mult), [bp], [bp])
        dve(lambda e: e.tensor_tensor(out=t1[:, :], in0=lb_im[:, :], in1=lam_re[:, :], op=ALU.mult), [bp], [bp])
        dve(lambda e: e.tensor_tensor(out=t2[:, :], in0=nr[:, :], in1=A_im[:, :], op=ALU.mult), [bp], [bp])
        dve(lambda e: e.tensor_tensor(out=t1[:, :], in0=t1[:, :], in1=t2[:, :], op=ALU.subtract), [bp], [bp])
        dve(lambda e: e.tensor_tensor(out=cf_im[:, :], in0=t1[:, :], in1=den[:, :], op=ALU.mult), [bp], [bp])
        Bre = T([128, 32, 16]); Bim = T([128, 32, 16]); BBre = T([128, 32, 16]); BBim = T([128, 32, 16]); tb = T([128, 32, 16])
        for g2 in range(2):
            P.dma("sp", Bre[64 * g2:64 * g2 + 64, :, :], prm["b_re"].rearrange("(pr g2) p h -> g2 p pr h", g2=2)[g2], writes=[bp])
            P.dma("sp", Bim[64 * g2:64 * g2 + 64, :, :], prm["b_im"].rearrange("(pr g2) p h -> g2 p pr h", g2=2)[g2], writes=[bp])
        def bc(t):
            return t[:, :].unsqueeze(2).to_broadcast([128, 32, 16])
        dve(lambda e: e.tensor_tensor(out=BBre[:, :, :], in0=Bre[:, :, :], in1=bc(cf_re), op=ALU.mult), [bp], [bp])
        dve(lambda e: e.tensor_tensor(out=tb[:, :, :], in0=Bim[:, :, :], in1=bc(cf_im), op=ALU.mult), [bp], [bp])
        dve(lambda e: e.tensor_tensor(out=BBre[:, :, :], in0=BBre[:, :, :], in1=tb[:, :, :], op=ALU.subtract), [bp], [bp])
        dve(lambda e: e.tensor_tensor(out=BBim[:, :, :], in0=Bim[:, :, :], in1=bc(cf_re), op=ALU.mult), [bp], [bp])
        dve(lambda e: e.tensor_tensor(out=tb[:, :, :], in0=Bre[:, :, :], in1=bc(cf_im), op=ALU.mult), [bp], [bp])
        dve(lambda e: e.tensor_tensor(out=BBim[:, :, :], in0=BBim[:, :, :], in1=tb[:, :, :], op=ALU.add), [bp], [bp])
        bL = Buf()
        mB = T([128, 4, 128]); mC = T([128, 4, 128]); idf = T([128, 128]); bm_ = Buf()
        P.dma("sp", mB[:, :, :], cst["maskB"][:, :, :], writes=[bm_])
        P.dma("sp", mC[:, :, :], cst["maskC"][:, :, :], writes=[bm_])
        P.dma("sp", idf[:, :], cst["ident"][:, :], writes=[bm_])
        BBrep = [T([128, 32, 8, 16]) for _ in range(2)]
        Cw = [T([128, 8, 128]) for _ in range(2)]
        bCw = Buf()
        b_ptr = [Buf(), Buf()]
        for q, BBq in enumerate((BBre, BBim)):
            dve(lambda e, q=q, BBq=BBq: e.tensor_copy(out=BBrep[q][:, :, :, :], in_=BBq[:, :, :].unsqueeze(2).to_broadcast([128, 32, 8, 16])), [bp], [bp])
            csrc = prm["c_re"] if q == 0 else prm["c_im"]
            for half in range(2):
                P.dma("sp", Cw[q][:, :, 64 * half:64 * half + 64], csrc.rearrange("(cc gl) ho p -> (gl ho) cc p", gl=8), writes=[bCw])
        it = 0
        for q in range(2):
            for cc in range(8):
                i = it % 2; it += 1
                def f_t(e, q=q, cc=cc, i=i):
                    ins = None
                    for k in range(4):
                        ins = e.transpose(out=ps_tr[i][:, 128 * k:128 * k + 128], in_=BBrep[q][:, 4 * cc + k, :, :].rearrange("p a b -> p (a b)"), identity=idf[:, :])
                    return ins
                P.op("pe", f_t, reads=[bp, bm_], writes=[b_ptr[i]])
                dve(lambda e, q=q, cc=cc, i=i: e.tensor_tensor(out=LB[q][:, 4 * cc:4 * cc + 4, :], in0=ps_tr[i][:, :].rearrange("p (k m) -> p k m", k=4), in1=mB[:, :, :], op=ALU.mult),
                    [b_ptr[i], bm_], [bL])
                i = it % 2; it += 1
                P.op("pe", lambda e, q=q, cc=cc, i=i: e.transpose(out=ps_tr[i][:, 0:128], in_=Cw[q][:, cc, :], identity=idf[:, :]), reads=[bCw, bm_], writes=[b_ptr[i]])
                sgn = 1.0 if q == 0 else -1.0
                dve(lambda e, q=q, cc=cc, i=i, sgn=sgn: e.scalar_tensor_tensor(out=LC[q][:, 4 * cc:4 * cc + 4, :], in0=mC[:, :, :], scalar=sgn,
                                                                           in1=ps_tr[i][:, 0:128].unsqueeze(1).to_broadcast([128, 4, 128]), op0=ALU.mult, op1=ALU.mult),
                    [b_ptr[i], bm_], [bL])
        st2.__exit__(None, None, None); P.stack = st
        barrier(P)
        ubf = T([128, 2, 2048], BF16); bu2 = [Buf(), Buf()]
        dsk = T([128, 8]); bd = Buf()
        P.dma("sp", dsk[:, :], prm["d"].rearrange("(c p) -> p c", p=128), writes=[bd], allow_slow_non_contiguous=True)
        iot = T([128, L]); bi = Buf()
        P.dma("sp", iot[:, :], cst["iota"][0:1, 0:L].partition_broadcast(128), writes=[bi])
        wg = T([128, 8, 1024], BF16); bw = Buf()
        for k in range(8):
            P.dma("pool", wg[:, k, :], prm["w_glu"][128 * k:128 * k + 128, :], writes=[bw])
        ygb = T([128, 8, 2048], BF16); bygb = Buf()
        ps_bu = [[P.ps("%s_pbu%d%d" % (tag, 0, q), [128, 512], F32) for q in range(2)], ps_tr]
        b_pbu = [Buf(), Buf()]
        b_pbu[1] = b_ptr[0]; b_ptr[1] = b_ptr[0]
        ps_y = [P.ps("%s_py%d" % (tag, i), [128, 512], F32) for i in range(4)]
        b_py = [Buf() for _ in range(4)]
        yt = T([128, L]); mp = T([128, L]); mn = T([128, L])
        LpA = [[T([128, L]), T([128, L])] for _ in range(2)]; LmA = [[T([128, L]), T([128, L])] for _ in range(2)]
        b_LA = [Buf(), Buf()]
        b_tab = Buf()
        tki = T([128, L], I32); tkf = T([128, L]); tf = T([128, L]); tm = T([128, L]); tfc = T([128, L]); ts_ = T([128, L]); tc_ = T([128, L])
        gre = [T([128, L]) for _ in range(2)]; gim = [T([128, L]) for _ in range(2)]
        Gre = [T([128, L]) for _ in range(2)]; Gim = [T([128, L]) for _ in range(2)]
        tmpa = [T([128, L]) for _ in range(2)]; tmpb = [T([128, L]) for _ in range(2)]
        hre = [T([128, L], BF16) for _ in range(2)]; him = [T([128, L], BF16) for _ in range(2)]
        hlast = [T([128, 2]) for _ in range(2)]
        b_g = [Buf(), Buf()]; b_G = [Buf(), Buf()]; b_h = [Buf(), Buf()]; b_tmp = [Buf(), Buf()]; b_hl = [Buf(), Buf()]
        ones = T([128, L]); b1 = Buf()
        dve(lambda e: e.memset(ones[:, :], 1.0), [], [b1])
        sc4A = [T([128, 4]), T([128, 4])]; b_scA = [Buf(), Buf()]
        yf = T([128, 2048]); x2 = T([128, 2048]); b_yf = Buf()
        uf = T([128, 2048]); b_uf = Buf()
        blk = 0
        def do_tables(pr):
            cc = pr // 4
            bu_ = bu2[cc % 2]
            if pr % 4 == 0:
                P.dma("pool", ubf[:, cc % 2, :], zT[U0 + 128 * cc:U0 + 128 * cc + 128, :], writes=[bu_])
            Lp = LpA[pr % 2]; Lm = LmA[pr % 2]; b_L = b_LA[pr % 2]
            dve(lambda e, pr=pr: e.tensor_scalar_mul(out=yt[:, :], in0=iot[:, :], scalar1=angn[:, pr:pr + 1]), [bi, bp, b_tab], [b_tab])
            dve(lambda e: e.tensor_copy(out=tki[:, :], in_=yt[:, :]), [b_tab], [b_tab])
            dve(lambda e: e.tensor_copy(out=tkf[:, :], in_=tki[:, :]), [b_tab], [b_tab])
            dve(lambda e: e.tensor_tensor(out=tf[:, :], in0=yt[:, :], in1=tkf[:, :], op=ALU.subtract), [b_tab], [b_tab])
            def wrap(t):
                dve(lambda e: e.tensor_single_scalar(out=tm[:, :], in_=t[:, :], scalar=0.5, op=ALU.is_gt), [b_tab], [b_tab])
                dve(lambda e: e.tensor_tensor(out=t[:, :], in0=t[:, :], in1=tm[:, :], op=ALU.subtract), [b_tab], [b_tab])
                dve(lambda e: e.tensor_single_scalar(out=tm[:, :], in_=t[:, :], scalar=-0.5, op=ALU.is_lt), [b_tab], [b_tab])
                dve(lambda e: e.tensor_tensor(out=t[:, :], in0=t[:, :], in1=tm[:, :], op=ALU.add), [b_tab], [b_tab])
            wrap(tf)
            dve(lambda e: e.tensor_scalar_add(out=tfc[:, :], in0=tf[:, :], scalar1=0.25), [b_tab], [b_tab])
            wrap(tfc)
            act(lambda e: e.activation(out=ts_[:, :], in_=tf[:, :], func=AF.Sin, scale=TWO_PI), [b_tab], [b_tab])
            act(lambda e: e.activation(out=tc_[:, :], in_=tfc[:, :], func=AF.Sin, scale=TWO_PI), [b_tab], [b_tab])
            act(lambda e, pr=pr: e.activation(out=mp[:, :], in_=iot[:, :], func=AF.Exp, scale=lr[:, pr:pr + 1]), [bi, bp, b_tab], [b_tab])
            act(lambda e, pr=pr: e.activation(out=mn[:, :], in_=iot[:, :], func=AF.Exp, scale=nlr[:, pr:pr + 1]), [bi, bp, b_tab], [b_tab])
            dve(lambda e, Lp=Lp: e.tensor_tensor(out=Lp[0][:, :], in0=mp[:, :], in1=tc_[:, :], op=ALU.mult), [b_tab], [b_L])
            dve(lambda e, Lp=Lp: e.tensor_tensor(out=Lp[1][:, :], in0=mp[:, :], in1=ts_[:, :], op=ALU.mult), [b_tab], [b_L])
            dve(lambda e, Lm=Lm: e.tensor_tensor(out=Lm[0][:, :], in0=mn[:, :], in1=tc_[:, :], op=ALU.mult), [b_tab], [b_L])
            dve(lambda e, Lm=Lm: e.scalar_tensor_tensor(out=Lm[1][:, :], in0=mn[:, :], scalar=-1.0, in1=ts_[:, :], op0=ALU.mult, op1=ALU.mult), [b_tab], [b_L])

        def do_block(pr, c):
            cc = pr // 4; i = pr % 2
            bu_ = bu2[cc % 2]
            Lp = LpA[pr % 2]; Lm = LmA[pr % 2]; b_L = b_LA[pr % 2]
            sc4 = sc4A[i]; b_sc = b_scA[i]
            def f_bu(e, pr=pr, cc=cc, c=c, i=i):
                e.matmul(ps_bu[i][0][:, :], lhsT=LB[0][:, pr, :], rhs=ubf[:, cc % 2, c * L:(c + 1) * L], start=True, stop=True)
                return e.matmul(ps_bu[i][1][:, :], lhsT=LB[1][:, pr, :], rhs=ubf[:, cc % 2, c * L:(c + 1) * L], start=True, stop=True)
            pe(f_bu, [bL, bu_], [b_pbu[i]])
            dve(lambda e, i=i, Lp=Lp, Lm=Lm: e.tensor_tensor(out=gre[i][:, :], in0=ps_bu[i][0][:, :], in1=Lm[0][:, :], op=ALU.mult), [b_pbu[i], b_L], [b_g[i]])
            dve(lambda e, i=i, Lp=Lp, Lm=Lm: e.tensor_tensor(out=tmpa[i][:, :], in0=ps_bu[i][1][:, :], in1=Lm[1][:, :], op=ALU.mult), [b_pbu[i], b_L], [b_tmp[i]])
            pool(lambda e, i=i, Lp=Lp, Lm=Lm: e.tensor_tensor(out=gre[i][:, :], in0=gre[i][:, :], in1=tmpa[i][:, :], op=ALU.subtract), [b_g[i], b_tmp[i]], [b_g[i]])
            dve(lambda e, i=i, Lp=Lp, Lm=Lm: e.tensor_tensor(out=gim[i][:, :], in0=ps_bu[i][1][:, :], in1=Lm[0][:, :], op=ALU.mult), [b_pbu[i], b_L], [b_g[i]])
            dve(lambda e, i=i, Lp=Lp, Lm=Lm: e.tensor_tensor(out=tmpb[i][:, :], in0=ps_bu[i][0][:, :], in1=Lm[1][:, :], op=ALU.mult), [b_pbu[i], b_L], [b_tmp[i]])
            pool(lambda e, i=i, Lp=Lp, Lm=Lm: e.tensor_tensor(out=gim[i][:, :], in0=gim[i][:, :], in1=tmpb[i][:, :], op=ALU.add), [b_g[i], b_tmp[i]], [b_g[i]])
            dve(lambda e, i=i, Lp=Lp, Lm=Lm: e.tensor_tensor_scan(out=Gre[i][:, :], data0=ones[:, :], data1=gre[i][:, :], initial=0.0, op0=ALU.mult, op1=ALU.add), [b1, b_g[i]], [b_G[i]])
            dve(lambda e, i=i, Lp=Lp, Lm=Lm: e.tensor_tensor_scan(out=Gim[i][:, :], data0=ones[:, :], data1=gim[i][:, :], initial=0.0, op0=ALU.mult, op1=ALU.add), [b1, b_g[i]], [b_G[i]])
            if c > 0:
                pv = i
                dve(lambda e, pr=pr, pv=pv: e.tensor_tensor(out=sc4[:, 0:1], in0=hlast[pv][:, 0:1], in1=lb_re[:, pr:pr + 1], op=ALU.mult), [b_hl[pv], bp, b_sc], [b_sc])
                dve(lambda e, pr=pr, pv=pv: e.tensor_tensor(out=sc4[:, 1:2], in0=hlast[pv][:, 1:2], in1=lb_im[:, pr:pr + 1], op=ALU.mult), [b_hl[pv], bp, b_sc], [b_sc])
                dve(lambda e: e.tensor_tensor(out=sc4[:, 0:1], in0=sc4[:, 0:1], in1=sc4[:, 1:2], op=ALU.subtract), [b_sc], [b_sc])
                dve(lambda e, pr=pr, pv=pv: e.tensor_tensor(out=sc4[:, 2:3], in0=hlast[pv][:, 1:2], in1=lb_re[:, pr:pr + 1], op=ALU.mult), [b_hl[pv], bp, b_sc], [b_sc])
                dve(lambda e, pr=pr, pv=pv: e.tensor_tensor(out=sc4[:, 3:4], in0=hlast[pv][:, 0:1], in1=lb_im[:, pr:pr + 1], op=ALU.mult), [b_hl[pv], bp, b_sc], [b_sc])
                dve(lambda e: e.tensor_tensor(out=sc4[:, 2:3], in0=sc4[:, 2:3], in1=sc4[:, 3:4], op=ALU.add), [b_sc], [b_sc])
                dve(lambda e, i=i, Lp=Lp, Lm=Lm: e.tensor_scalar_add(out=Gre[i][:, :], in0=Gre[i][:, :], scalar1=sc4[:, 0:1]), [b_sc, b_G[i]], [b_G[i]])
                dve(lambda e, i=i, Lp=Lp, Lm=Lm: e.tensor_scalar_add(out=Gim[i][:, :], in0=Gim[i][:, :], scalar1=sc4[:, 2:3]), [b_sc, b_G[i]], [b_G[i]])
            pool(lambda e, i=i, Lp=Lp, Lm=Lm: e.tensor_tensor(out=tmpa[i][:, :], in0=Gre[i][:, :], in1=Lp[0][:, :], op=ALU.mult), [b_G[i], b_L, b_tmp[i]], [b_tmp[i]])
            pool(lambda e, i=i, Lp=Lp, Lm=Lm: e.tensor_tensor(out=tmpb[i][:, :], in0=Gim[i][:, :], in1=Lp[1][:, :], op=ALU.mult), [b_G[i], b_L, b_tmp[i]], [b_tmp[i]])
            pool(lambda e, i=i, Lp=Lp, Lm=Lm: e.tensor_tensor(out=gre[i][:, :], in0=tmpa[i][:, :], in1=tmpb[i][:, :], op=ALU.subtract), [b_tmp[i], b_g[i]], [b_g[i]])
            pool(lambda e, i=i, Lp=Lp, Lm=Lm: e.tensor_tensor(out=tmpa[i][:, :], in0=Gim[i][:, :], in1=Lp[0][:, :], op=ALU.mult), [b_G[i], b_L, b_tmp[i]], [b_tmp[i]])
            pool(lambda e, i=i, Lp=Lp, Lm=Lm: e.tensor_tensor(out=tmpb[i][:, :], in0=Gre[i][:, :], in1=Lp[1][:, :], op=ALU.mult), [b_G[i], b_L, b_tmp[i]], [b_tmp[i]])
            pool(lambda e, i=i, Lp=Lp, Lm=Lm: e.tensor_tensor(out=gim[i][:, :], in0=tmpa[i][:, :], in1=tmpb[i][:, :], op=ALU.add), [b_tmp[i], b_g[i]], [b_g[i]])
            pool(lambda e, i=i, Lp=Lp, Lm=Lm: e.tensor_copy(out=hlast[i][:, 0:1], in_=gre[i][:, L - 1:L]), [b_g[i], b_hl[i]], [b_hl[i]])
            pool(lambda e, i=i, Lp=Lp, Lm=Lm: e.tensor_copy(out=hlast[i][:, 1:2], in_=gim[i][:, L - 1:L]), [b_g[i], b_hl[i]], [b_hl[i]])
            act(lambda e, i=i, Lp=Lp, Lm=Lm: e.copy(out=hre[i][:, :], in_=gre[i][:, :]), [b_g[i]], [b_h[i]])
            act(lambda e, i=i, Lp=Lp, Lm=Lm: e.copy(out=him[i][:, :], in_=gim[i][:, :]), [b_g[i]], [b_h[i]])
            def f_c(e, pr=pr, c=c, i=i):
                e.matmul(ps_y[c][:, :], lhsT=LC[0][:, pr, :], rhs=hre[i][:, :], start=(pr % 4 == 0), stop=False)
                return e.matmul(ps_y[c][:, :], lhsT=LC[1][:, pr, :], rhs=him[i][:, :], start=False, stop=(pr % 4 == 3))
            pe(f_c, [bL, b_h[i]], [b_py[c]])

        def do_tail(pr):
            cc = pr // 4
            if pr % 4 == 3:
                P.dma("sp", uf[:, :], zT[U0 + 128 * cc:U0 + 128 * cc + 128, :], writes=[b_uf])
                for c in range(4):
                    dve(lambda e, c=c, cc=cc: e.scalar_tensor_tensor(out=yf[:, c * L:(c + 1) * L], in0=uf[:, c * L:(c + 1) * L], scalar=dsk[:, cc:cc + 1],
                                                                      in1=ps_y[c][:, :], op0=ALU.mult, op1=ALU.add), [b_uf, bd, b_py[c]], [b_yf])
                act(lambda e: e.activation(out=x2[:, :], in_=yf[:, :], func=AF.Square), [b_yf], [b_yf])
                dve(lambda e: e.tensor_scalar(out=x2[:, :], in0=x2[:, :], scalar1=0.044715, scalar2=1.0, op0=ALU.mult, op1=ALU.add), [b_yf], [b_yf])
                dve(lambda e: e.tensor_tensor(out=x2[:, :], in0=x2[:, :], in1=yf[:, :], op=ALU.mult), [b_yf], [b_yf])
                act(lambda e: e.activation(out=x2[:, :], in_=x2[:, :], func=AF.Sigmoid, scale=GC), [b_yf], [b_yf])
                dve(lambda e: e.tensor_tensor(out=yf[:, :], in0=yf[:, :], in1=x2[:, :], op=ALU.mult), [b_yf], [b_yf])
                act(lambda e, cc=cc: e.copy(out=ygb[:, cc, :], in_=yf[:, :]), [b_yf], [bygb])
                P.dma("sp", scr["ygD"][128 * cc:128 * cc + 128, :], yf[:, :], reads=[b_yf], writes=[scr["b_ygD"]])

        for pp in range(16):
            do_tables(2 * pp); do_tables(2 * pp + 1)
            for c in range(4):
                la = []; lb = []
                defer[0] = la; do_block(2 * pp, c)
                defer[0] = lb; do_block(2 * pp + 1, c)
                defer[0] = None
                for q in range(max(len(la), len(lb))):
                    if q < len(la):
                        P.op(la[q][0], la[q][1], reads=la[q][2], writes=la[q][3])
                    if q < len(lb):
                        P.op(lb[q][0], lb[q][1], reads=lb[q][2], writes=lb[q][3])
            do_tail(2 * pp + 1)
        for n in range(8):
            for c in range(4):
                def f_g(e, n=n, c=c):
                    ins = None
                    for k in range(8):
                        ins = e.matmul(ps_y[c][:, :], lhsT=wg[:, k, 128 * n:128 * n + 128], rhs=ygb[:, k, c * L:(c + 1) * L], start=(k == 0), stop=(k == 7))
                    return ins
                P.op("pe", f_g, reads=[bw, bygb], writes=[b_py[c]])
            P.dma("sp", uf[:, :], scr["ygD"][128 * n:128 * n + 128, :], reads=[scr["b_ygD"]], writes=[b_uf])
            for c in range(4):
                act(lambda e, c=c: e.activation(out=x2[:, c * L:(c + 1) * L], in_=ps_y[c][:, :], func=AF.Sigmoid), [b_py[c], b_yf], [b_yf])
            dve(lambda e: e.tensor_tensor(out=yf[:, :], in0=uf[:, :], in1=x2[:, :], op=ALU.mult), [b_uf, b_yf], [b_yf])
            P.dma("sp", mixT[1024 + 128 * n:1024 + 128 * n + 128, :], yf[:, :], reads=[b_yf], writes=[b_mix])
        P.stack = old
    barrier(P)


def transpose_in(P, tag, x, xT, b_xT, identf, b_c):
    with ExitStack() as st:
        old = P.stack; P.stack = st
        xs = [P.sb(tag + "_xs%d" % i, [128, 4096], F32) for i in range(2)]
        stg = [P.sb(tag + "_st%d" % i, [128, 32, 128], F32) for i in range(2)]
        ps = [P.ps(tag + "_ps%d" % i, [128, 512], F32) for i in range(2)]
        b_xs = [Buf(), Buf()]; b_st = [Buf(), Buf()]; b_ps = [Buf(), Buf()]
        it = 0
        xTv = xT.rearrange("(fc p) t -> p fc t", p=128)
        for tt in range(16):
            i = tt % 2
            P.dma("sp", xs[i][:, :], x[tt * 128:(tt + 1) * 128, :], writes=[b_xs[i]])
            for f4 in range(8):
                pi = it % 2; it += 1
                def f(e, i=i, f4=f4, pi=pi):
                    ins = None
                    for k in range(4):
                        fc = 4 * f4 + k
                        ins = e.transpose(out=ps[pi][:, 128 * k:128 * k + 128], in_=xs[i][:, fc * 128:(fc + 1) * 128], identity=identf[:, :])
                    return ins
                P.op("pe", f, reads=[b_xs[i], b_c], writes=[b_ps[pi]])
                eng = "dve" if pi == 0 else "act"
                if eng == "dve":
                    P.op("dve", lambda e, i=i, f4=f4, pi=pi: e.tensor_copy(out=stg[i][:, 4 * f4:4 * f4 + 4, :], in_=ps[pi][:, :].rearrange("p (k t) -> p k t", k=4)),
                         reads=[b_ps[pi]], writes=[b_st[i]])
                else:
                    P.op("act", lambda e, i=i, f4=f4, pi=pi: e.copy(out=stg[i][:, 4 * f4:4 * f4 + 4, :], in_=ps[pi][:, :].rearrange("p (k t) -> p k t", k=4)),
                         reads=[b_ps[pi]], writes=[b_st[i]])
            for f8 in range(0, 32, 8):
                P.dma("sp", xTv[:, f8:f8 + 8, tt * 128:(tt + 1) * 128], stg[i][:, f8:f8 + 8, :], reads=[b_st[i]], writes=[b_xT])
        P.stack = old
    barrier(P)


def transpose_out(P, tag, xT, b_xT, y, b_y, identf, b_c):
    with ExitStack() as st:
        old = P.stack; P.stack = st
        xs = [P.sb(tag + "_xs%d" % i, [128, 32, 128], F32) for i in range(2)]
        stg = [P.sb(tag + "_st%d" % i, [128, 4096], F32) for i in range(2)]
        ps = [P.ps(tag + "_ps%d" % i, [128, 512], F32) for i in range(2)]
        b_xs = [Buf(), Buf()]; b_st = [Buf(), Buf()]; b_ps = [Buf(), Buf()]
        it = 0
        xTv = xT.rearrange("(fc p) t -> p fc t", p=128)
        for tt in range(16):
            i = tt % 2
            for f8 in range(0, 32, 8):
                P.dma("sp", xs[i][:, f8:f8 + 8, :], xTv[:, f8:f8 + 8, tt * 128:(tt + 1) * 128], reads=[b_xT], writes=[b_xs[i]])
            for f4 in range(8):
                pi = it % 2; it += 1
                def f(e, i=i, f4=f4, pi=pi):
                    ins = None
                    for k in range(4):
                        ins = e.transpose(out=ps[pi][:, 128 * k:128 * k + 128], in_=xs[i][:, 4 * f4 + k, :], identity=identf[:, :])
                    return ins
                P.op("pe", f, reads=[b_xs[i], b_c], writes=[b_ps[pi]])
                if pi == 0:
                    P.op("dve", lambda e, i=i, f4=f4, pi=pi: e.tensor_copy(out=stg[i][:, 512 * f4:512 * f4 + 512], in_=ps[pi][:, :]), reads=[b_ps[pi]], writes=[b_st[i]])
                else:
                    P.op("act", lambda e, i=i, f4=f4, pi=pi: e.copy(out=stg[i][:, 512 * f4:512 * f4 + 512], in_=ps[pi][:, :]), reads=[b_ps[pi]], writes=[b_st[i]])
            P.dma("sp", y[tt * 128:(tt + 1) * 128, :], stg[i][:, :], reads=[b_st[i]], writes=[b_y])
        P.stack = old
    barrier(P)


def norm_phase(P, tag, srcT, r0, F, gain, mode, dstT, d0, b_src, b_dst, onesf, b_c, eps=1e-6):
    C = F // 128
    TB = 1024
    with ExitStack() as st:
        old = P.stack; P.stack = st
        X = P.sb(tag + "_X", [128, C, TB], F32); b_X = Buf()
        g = P.sb(tag + "_g", [128, C], F32); b_g = Buf()
        sq = [P.sb(tag + "_sq%d" % i, [128, TB], BF16) for i in range(2)]; b_sq = [Buf(), Buf()]
        rs = P.sb(tag + "_rs", [128, TB], F32); b_rs = Buf()
        ps = [P.ps(tag + "_ps%d" % i, [128, 512], F32) for i in range(2)]; b_ps = Buf()
        o16 = [P.sb(tag + "_o%d" % i, [128, TB], BF16 if mode == "bf16" else F32) for i in range(2)]; b_o = [Buf(), Buf()]
        xr = [P.sb(tag + "_xr%d" % i, [128, TB], F32) for i in range(2)]; b_xr = [Buf(), Buf()]
        P.dma("sp", g[:, :], gain.rearrange("(c p) -> p c", p=128), writes=[b_g], allow_slow_non_contiguous=True)
        sv = srcT[r0:r0 + F, :].rearrange("(c p) t -> p c t", p=128)
        dv = dstT[d0:d0 + F, :].rearrange("(c p) t -> p c t", p=128)
        for tb in range(2048 // TB):
            t0 = tb * TB
            for c0 in range(0, C, 4):
                c1 = min(C, c0 + 4)
                P.dma("sp", X[:, c0:c1, :], sv[:, c0:c1, t0:t0 + TB], reads=[b_src], writes=[b_X])
            for c in range(C):
                i = c % 2
                P.op("act", lambda e, c=c, i=i: e.activation(out=sq[i][:, :], in_=X[:, c, :], func=AF.Square), reads=[b_X], writes=[b_sq[i]])
                def f(e, c=c, i=i):
                    e.matmul(ps[0][:, :], lhsT=onesf[:, :], rhs=sq[i][:, 0:512], start=(c == 0), stop=(c == C - 1))
                    return e.matmul(ps[1][:, :], lhsT=onesf[:, :], rhs=sq[i][:, 512:1024], start=(c == 0), stop=(c == C - 1))
                P.op("pe", f, reads=[b_sq[i], b_c], writes=[b_ps])
            for h in range(2):
                P.op("act", lambda e, h=h: e.activation(out=rs[:, 512 * h:512 * h + 512], in_=ps[h][:, :], func=AF.Sqrt, scale=1.0 / F, bias=eps),
                     reads=[b_ps], writes=[b_rs])
            P.op("dve", lambda e: e.reciprocal(out=rs[:, :], in_=rs[:, :]), reads=[b_rs], writes=[b_rs])
            for c in range(C):
                i = c % 2
                if mode == "bf16":
                    P.op("dve", lambda e, c=c, i=i: e.scalar_tensor_tensor(out=o16[i][:, :], in0=X[:, c, :], scalar=g[:, c:c + 1], in1=rs[:, :], op0=ALU.mult, op1=ALU.mult),
                         reads=[b_X, b_g, b_rs], writes=[b_o[i]])
                    P.dma("sp", dv[:, c, t0:t0 + TB], o16[i][:, :], reads=[b_o[i]], writes=[b_dst])
                else:
                    P.dma("sp", xr[i][:, :], dv[:, c, t0:t0 + TB], reads=[b_dst], writes=[b_xr[i]])
                    P.op("dve", lambda e, c=c, i=i: e.scalar_tensor_tensor(out=o16[i][:, :], in0=X[:, c, :], scalar=g[:, c:c + 1], in1=rs[:, :], op0=ALU.mult, op1=ALU.mult),
                         reads=[b_X, b_g, b_rs], writes=[b_o[i]])
                    P.op("dve", lambda e, i=i: e.tensor_tensor(out=o16[i][:, :], in0=o16[i][:, :], in1=xr[i][:, :], op=ALU.add),
                         reads=[b_o[i], b_xr[i]], writes=[b_o[i]])
                    P.dma("sp", dv[:, c, t0:t0 + TB], o16[i][:, :], reads=[b_o[i]], writes=[b_dst])
        P.stack = old
    barrier(P)


def conv_phase(P, tag, uT, b_u, actT, b_act, conv_w, conv_b, DF=11008, jobs=None):
    NJ = DF // 128
    with ExitStack() as st:
        old = P.stack; P.stack = st
        cw = P.sb(tag + "_cw", [128, 3, 2 * NJ], F32); cb = P.sb(tag + "_cb", [128, 2 * NJ], F32); b_cw = Buf()
        for k in range(3):
            P.dma("sp", cw[:, k, :], conv_w[k].rearrange("(c p) -> p c", p=128), writes=[b_cw], allow_slow_non_contiguous=True)
        P.dma("sp", cb[:, :], conv_b.rearrange("(c p) -> p c", p=128), writes=[b_cw], allow_slow_non_contiguous=True)
        ug = [P.sb(tag + "_ug%d" % i, [128, 2050], F32) for i in range(2)]
        uv = [P.sb(tag + "_uv%d" % i, [128, 2050], F32) for i in range(2)]
        b_ug = [Buf(), Buf()]; b_uv = [Buf(), Buf()]
        yg = P.sb(tag + "_yg", [128, 2048], F32); yv = P.sb(tag + "_yv", [128, 2048], F32); s2 = P.sb(tag + "_s2", [128, 2048], F32)
        b_yg, b_yv, b_s2 = Buf(), Buf(), Buf()
        tv = P.sb(tag + "_tv", [128, 2048], F32); b_tv = Buf()
        ao = [P.sb(tag + "_ao%d" % i, [128, 2048], BF16) for i in range(2)]; b_ao = [Buf(), Buf()]
        for i in range(2):
            P.op("dve", lambda e, i=i: e.memset(ug[i][:, 0:2], 0.0), writes=[b_ug[i]])
            P.op("dve", lambda e, i=i: e.memset(uv[i][:, 0:2], 0.0), writes=[b_uv[i]])
        for j in range(NJ):
            i = j % 2
            P.dma("sp", ug[i][:, 2:2050], uT[128 * j:128 * j + 128, :], reads=[b_u], writes=[b_ug[i]])
            P.dma("sp", uv[i][:, 2:2050], uT[DF + 128 * j:DF + 128 * j + 128, :], reads=[b_u], writes=[b_uv[i]])
            for (u, bu, y, by, cj) in ((ug[i], b_ug[i], yg, b_yg, j), (uv[i], b_uv[i], yv, b_yv, NJ + j)):
                P.op("act", lambda e, u=u, y=y, cj=cj: e.activation(out=y[:, :], in_=u[:, 2:2050], func=AF.Identity, scale=cw[:, 2, cj:cj + 1], bias=cb[:, cj:cj + 1]),
                     reads=[bu, b_cw], writes=[by])
            P.op("dve", lambda e, u=ug[i], cj=j: e.scalar_tensor_tensor(out=yg[:, :], in0=u[:, 1:2049], scalar=cw[:, 1, cj:cj + 1], in1=yg[:, :], op0=ALU.mult, op1=ALU.add),
                 reads=[b_ug[i], b_cw, b_yg], writes=[b_yg])
            P.op("dve", lambda e, u=ug[i], cj=j: e.scalar_tensor_tensor(out=yg[:, :], in0=u[:, 0:2048], scalar=cw[:, 0, cj:cj + 1], in1=yg[:, :], op0=ALU.mult, op1=ALU.add),
                 reads=[b_ug[i], b_cw, b_yg], writes=[b_yg])
            for tap in (1, 0):
                P.op("pool", lambda e, u=uv[i], cj=NJ + j, tap=tap: e.tensor_scalar(out=tv[:, :], in0=u[:, tap:tap + 2048], scalar1=cw[:, tap, cj:cj + 1], scalar2=0.0, op0=ALU.mult, op1=ALU.add),
                     reads=[b_uv[i], b_cw, b_tv], writes=[b_tv])
                P.op("pool", lambda e: e.tensor_tensor(out=yv[:, :], in0=yv[:, :], in1=tv[:, :], op=ALU.add), reads=[b_tv, b_yv], writes=[b_yv])
            if jobs:
                for _ in range(3):
                    if jobs:
                        jobs.pop(0)()
            P.op("act", lambda e: e.activation(out=s2[:, :], in_=yg[:, :], func=AF.Square), reads=[b_yg], writes=[b_s2])
            P.op("dve", lambda e: e.tensor_scalar(out=s2[:, :], in0=s2[:, :], scalar1=0.044715, scalar2=1.0, op0=ALU.mult, op1=ALU.add), reads=[b_s2], writes=[b_s2])
            P.op("dve", lambda e: e.tensor_tensor(out=s2[:, :], in0=s2[:, :], in1=yg[:, :], op=ALU.mult), reads=[b_s2, b_yg], writes=[b_s2])
            P.op("act", lambda e: e.activation(out=s2[:, :], in_=s2[:, :], func=AF.Sigmoid, scale=GC), reads=[b_s2], writes=[b_s2])
            P.op("dve", lambda e: e.tensor_tensor(out=yg[:, :], in0=yg[:, :], in1=s2[:, :], op=ALU.mult), reads=[b_s2, b_yg], writes=[b_yg])
            P.op("dve", lambda e, i=i: e.tensor_tensor(out=ao[i][:, :], in0=yg[:, :], in1=yv[:, :], op=ALU.mult), reads=[b_yg, b_yv], writes=[b_ao[i]])
            P.dma("sp", actT[128 * j:128 * j + 128, :], ao[i][:, :], reads=[b_ao[i]], writes=[b_act])
        while jobs:
            jobs.pop(0)()
        P.stack = old
    barrier(P)


def gemm_up_conv(P, tag, hT, W, outT, b_out, conv_w, conv_b, jobs=None):
    K, KC, TB, PW, DF, T = 4096, 32, 1024, 512, 11008, 2048
    NJ2 = 2 * DF // 128
    with ExitStack() as st:
        old = P.stack; P.stack = st
        act = P.sb(tag + "_act", [128, KC, TB], BF16)
        wts = [P.sb(tag + "_w%d" % i, [128, KC, PW], BF16) for i in range(2)]
        pss = [P.ps(tag + "_p%d" % i, [128, TB], F32) for i in range(2)]
        cw = P.sb(tag + "_cw", [128, 3, NJ2], F32); cb = P.sb(tag + "_cb", [128, NJ2], F32); b_cw = Buf()
        hal = P.sb(tag + "_hal", [128, NJ2, 2], F32); b_hal = Buf()
        yb = [P.sb(tag + "_y%d" % i, [128, TB], F32) for i in range(2)]; b_y = [Buf(), Buf()]
        s2 = [P.sb(tag + "_s%d" % i, [128, TB], F32) for i in range(2)]; b_s2 = [Buf(), Buf()]
        G = P.sb(tag + "_G", [128, 4, TB], F32); b_G = [Buf() for _ in range(4)]
        ao = [P.sb(tag + "_ao%d" % i, [128, TB], BF16) for i in range(2)]; b_ao = [Buf(), Buf()]
        b_act = Buf(); b_w = [Buf(), Buf()]; b_p = [Buf(), Buf()]
        for k in range(3):
            P.dma("sp", cw[:, k, :], conv_w[k].rearrange("(c p) -> p c", p=128), writes=[b_cw], allow_slow_non_contiguous=True)
        P.dma("sp", cb[:, :], conv_b.rearrange("(c p) -> p c", p=128), writes=[b_cw], allow_slow_non_contiguous=True)
        actv = hT.rearrange("(c p) t -> p c t", p=128)
        Wv = W.rearrange("(c p) n -> p c n", p=128)
        it = 0; wi = 0; gi = 0; vi = 0
        npan = (DF + PW - 1) // PW
        for tb in range(T // TB):
            t0 = tb * TB
            for k0 in range(0, KC, 8):
                P.dma("sp", act[:, k0:k0 + 8, :], actv[:, k0:k0 + 8, t0:t0 + TB], writes=[b_act])
            for p in range(npan):
                pw = min(PW, DF - PW * p)
                for is_val in (0, 1):
                    n0 = PW * p + (DF if is_val else 0)
                    wb = wi % 2; wi += 1
                    for k0 in range(0, KC, 8):
                        P.dma("pool", wts[wb][:, k0:k0 + 8, :pw], Wv[:, k0:k0 + 8, n0:n0 + pw], writes=[b_w[wb]])
                    for c in range(pw // 128):
                        cj = (n0 + 128 * c) // 128
                        j = 4 * p + c
                        pb = it % 2; it += 1
                        def mm(e, wb=wb, c=c, pb=pb):
                            ins = None
                            for ts in range(TB // 512):
                                for k in range(KC):
                                    ins = e.matmul(pss[pb][:, ts * 512:(ts + 1) * 512], lhsT=wts[wb][:, k, 128 * c:128 * c + 128],
                                                   rhs=act[:, k, ts * 512:(ts + 1) * 512], start=(k == 0), stop=(k == KC - 1))
                            return ins
                        P.op("pe", mm, reads=[b_act, b_w[wb]], writes=[b_p[pb]])
                        yi = pb
                        y = yb[yi]; by = b_y[yi]
                        P.op("act", lambda e, y=y, pb=pb, cj=cj: e.activation(out=y[:, :], in_=pss[pb][:, :], func=AF.Identity, scale=cw[:, 2, cj:cj + 1], bias=cb[:, cj:cj + 1]),
                             reads=[b_p[pb], b_cw], writes=[by])
                        P.op("dve", lambda e, y=y, pb=pb, cj=cj: e.scalar_tensor_tensor(out=y[:, 1:TB], in0=pss[pb][:, 0:TB - 1], scalar=cw[:, 1, cj:cj + 1], in1=y[:, 1:TB], op0=ALU.mult, op1=ALU.add),
                             reads=[b_p[pb], b_cw, by], writes=[by])
                        P.op("dve", lambda e, y=y, pb=pb, cj=cj: e.scalar_tensor_tensor(out=y[:, 2:TB], in0=pss[pb][:, 0:TB - 2], scalar=cw[:, 0, cj:cj + 1], in1=y[:, 2:TB], op0=ALU.mult, op1=ALU.add),
                             reads=[b_p[pb], b_cw, by], writes=[by])
                        if tb == 0:
                            P.op("dve", lambda e, pb=pb, cj=cj: e.tensor_copy(out=hal[:, cj, :], in_=pss[pb][:, TB - 2:TB]), reads=[b_p[pb]], writes=[b_hal])
                        else:
                            P.op("dve", lambda e, y=y, cj=cj: e.scalar_tensor_tensor(out=y[:, 0:1], in0=hal[:, cj, 1:2], scalar=cw[:, 1, cj:cj + 1], in1=y[:, 0:1], op0=ALU.mult, op1=ALU.add),
                                 reads=[b_hal, b_cw, by], writes=[by])
                            P.op("dve", lambda e, y=y, cj=cj: e.scalar_tensor_tensor(out=y[:, 0:2], in0=hal[:, cj, 0:2], scalar=cw[:, 0, cj:cj + 1], in1=y[:, 0:2], op0=ALU.mult, op1=ALU.add),
                                 reads=[b_hal, b_cw, by], writes=[by])
                        if not is_val:
                            si = gi % 2; gi += 1
                            s = s2[si]; bs = b_s2[si]
                            P.op("act", lambda e, y=y, s=s: e.activation(out=s[:, :], in_=y[:, :], func=AF.Square), reads=[by], writes=[bs])
                            P.op("dve", lambda e, s=s: e.tensor_scalar(out=s[:, :], in0=s[:, :], scalar1=0.044715, scalar2=1.0, op0=ALU.mult, op1=ALU.add), reads=[bs], writes=[bs])
                            P.op("dve", lambda e, y=y, s=s: e.tensor_tensor(out=s[:, :], in0=s[:, :], in1=y[:, :], op=ALU.mult), reads=[bs, by], writes=[bs])
                            P.op("act", lambda e, s=s: e.activation(out=s[:, :], in_=s[:, :], func=AF.Sigmoid, scale=GC), reads=[bs], writes=[bs])
                            P.op("dve", lambda e, y=y, s=s, c=c: e.tensor_tensor(out=G[:, c, :], in0=y[:, :], in1=s[:, :], op=ALU.mult), reads=[bs, by], writes=[b_G[c]])
                        else:
                            oi = vi % 2; vi += 1
                            P.op("dve", lambda e, y=y, c=c, oi=oi: e.tensor_tensor(out=ao[oi][:, :], in0=G[:, c, :], in1=y[:, :], op=ALU.mult), reads=[b_G[c], by], writes=[b_ao[oi]])
                            P.dma("sp", outT[128 * j:128 * j + 128, t0:t0 + TB], ao[oi][:, :], reads=[b_ao[oi]], writes=[b_out])
                        if jobs and tb == 0:
                            jobs.pop(0)()
        while jobs:
            jobs.pop(0)()
        P.stack = old
    barrier(P)

NCORES = 4
DEPTH = 4
_CACHE = {}


def build_program(depth=DEPTH, stop_after=None):
    nc = bass.Bass("TRN2", target_bir_lowering=False)
    st = ExitStack()
    P = Prog(nc, st)
    cst_np = make_consts()
    ext = lambda name, shape: P.dram(name, shape, F32, kind="ExternalInput")
    x = ext("x", [2048, 4096])
    w_in = [ext("w_in%d" % l, [4096, 9272]) for l in range(depth)]
    w_out = [ext("w_out%d" % l, [4096, 4096]) for l in range(depth)]
    w_up = [ext("w_up%d" % l, [4096, 22016]) for l in range(depth)]
    w_down = [ext("w_down%d" % l, [11008, 4096]) for l in range(depth)]
    small = {}
    for name, shape in (("b_forget", [4, 8]), ("s5_a_re", [4, 64, 64]), ("s5_a_im", [4, 64, 64]), ("s5_log_dt", [4, 64]),
                        ("s5_b_re", [4, 64, 64, 16]), ("s5_b_im", [4, 64, 64, 16]), ("s5_c_re", [4, 64, 16, 64]), ("s5_c_im", [4, 64, 16, 64]),
                        ("s5_d", [4, 1024]), ("s5_w_glu", [4, 1024, 1024]), ("cmp_pe_k", [4, 32, 128]), ("cmp_pe_v", [4, 32, 128]),
                        ("cmp_w_k", [4, 32, 128, 128]), ("cmp_w_v", [4, 32, 128, 128]), ("rel_bias", [32, 16]),
                        ("g_out_fox", [4, 1024]), ("g_out_s5", [4, 1024]), ("g_out_nsa", [4, 2048]), ("g_pre_mix", [4, 4096]),
                        ("g_post_mix", [4, 4096]), ("g_pre_ffn", [4, 4096]), ("g_post_ffn", [4, 4096]), ("conv_w", [4, 3, 22016]), ("conv_b", [4, 22016])):
        small[name] = ext(name, shape)
    cst = {k: ext("k_" + k, list(v.shape)) for k, v in cst_np.items()}
    y = P.dram("y", [2048, 4096], F32, kind="ExternalOutput")
    xT = P.dram("s_xT", [4096, 2048], F32); hT = P.dram("s_hT", [4096, 2048], BF16)
    zT = P.dram("s_zT", [9272, 2048], F32); mixT = P.dram("s_mixT", [4096, 2048], F32)
    moT = P.dram("s_moT", [4096, 2048], F32); uT = None
    actT = P.dram("s_actT", [11008, 2048], BF16)
    wdbP = P.dram("s_wdbP", [16, 128, 86, 256], BF16); b_wdb = Buf()
    scr = dict(browS=P.dram("s_browS", [16, BROW], F32), b_browS=Buf(), BM=P.dram("s_BM", [16, 13, 128, 512], BF16), b_BM=Buf(),
               cD=P.dram("s_cD", [8, 2048], F32), b_cD=Buf(), gD=P.dram("s_gD", [48, 2048], F32), b_gD=Buf(),
               bbD=P.dram("s_bbD", [2, 64, 64, 16], F32), b_bbD=Buf(), ygD=P.dram("s_ygD", [1024, 2048], F32), b_ygD=Buf())
    b_xT, b_hT, b_zT, b_mix, b_mo, b_u, b_act, b_y = [Buf() for _ in range(8)]
    C = attn_consts(P, cst)
    setup_bias(P, small["rel_bias"], cst, scr)
    transpose_in(P, "ti", x, xT, b_xT, C.identf, C.b)
    for l in range(depth):
        L = "L%d" % l
        norm_phase(P, L + "n1", xT, 0, 4096, small["g_pre_mix"][l], "bf16", hT, 0, b_xT, b_hT, C.onesb, C.b)
        gemm(P, L + "gi", hT, 4096, 2048, w_in[l], 9272, zT, F32, TB=1024, PW=512)
        fox_phase(P, C, L + "fx", zT, mixT, b_mix, small["b_forget"][l], scr)
        prm = dict(a_re=small["s5_a_re"][l], a_im=small["s5_a_im"][l], log_dt=small["s5_log_dt"][l], b_re=small["s5_b_re"][l],
                   b_im=small["s5_b_im"][l], c_re=small["s5_c_re"][l], c_im=small["s5_c_im"][l], d=small["s5_d"][l], w_glu=small["s5_w_glu"][l])
        s5_phase(P, L + "s5", zT, mixT, b_mix, prm, cst, scr)
        nsa_phase(P, C, L + "ns", zT, mixT, b_mix, small["cmp_pe_k"][l], small["cmp_pe_v"][l], small["cmp_w_k"][l], small["cmp_w_v"][l],
                  small["rel_bias"], scr)
        norm_phase(P, L + "na", mixT, 0, 1024, small["g_out_fox"][l], "bf16", hT, 0, b_mix, b_hT, C.onesb, C.b)
        norm_phase(P, L + "nb", mixT, 1024, 1024, small["g_out_s5"][l], "bf16", hT, 1024, b_mix, b_hT, C.onesb, C.b)
        norm_phase(P, L + "nc", mixT, 2048, 2048, small["g_out_nsa"][l], "bf16", hT, 2048, b_mix, b_hT, C.onesb, C.b)
        gemm(P, L + "go", hT, 4096, 2048, w_out[l], 4096, moT, F32, TB=1024, PW=512)
        norm_phase(P, L + "n2", moT, 0, 4096, small["g_post_mix"][l], "resid", xT, 0, b_mo, b_xT, C.onesb, C.b)
        norm_phase(P, L + "n3", xT, 0, 4096, small["g_pre_ffn"][l], "bf16", hT, 0, b_xT, b_hT, C.onesb, C.b)
        jobs = []
        wdv = w_down[l].rearrange("(c p) n -> p c n", p=128)
        for pn in range(16):
            for k0 in range(0, 86, 8):
                k1 = min(86, k0 + 8)
                jobs.append(lambda pn=pn, k0=k0, k1=k1, wdv=wdv: P.dma("pool", wdbP[pn, :, k0:k1, :], wdv[:, k0:k1, 256 * pn:256 * pn + 256], writes=[b_wdb]))
        gemm_up_conv(P, L + "gu", hT, w_up[l], actT, b_act, small["conv_w"][l], small["conv_b"][l], jobs=jobs)
        gemm(P, L + "gd", actT, 11008, 2048, None, 4096, moT, F32, TB=512, PW=256, Wpan=wdbP, b_wpan=b_wdb)
        norm_phase(P, L + "n4", moT, 0, 4096, small["g_post_ffn"][l], "resid", xT, 0, b_mo, b_xT, C.onesb, C.b)
    transpose_out(P, "to", xT, b_xT, y, b_y, C.identf, C.b)
    P.finish_wait_all("sp", [b_y])
    P.emit()
    return nc, cst_np, st


def kernel(**inputs):
    if "prog" not in _CACHE:
        _CACHE["prog"] = build_program()
    nc, cst_np, _st = _CACHE["prog"]
    f32 = lambda a: np.ascontiguousarray(np.asarray(a, dtype=np.float32))
    shared = {}
    for l in range(DEPTH):
        shared["w_in%d" % l] = f32(inputs["w_in"][l])
        shared["w_out%d" % l] = f32(inputs["w_out"][l])
        shared["w_up%d" % l] = f32(inputs["w_up"][l])
        shared["w_down%d" % l] = f32(inputs["w_down"][l])
    for name in ("b_forget", "s5_a_re", "s5_a_im", "s5_log_dt", "s5_b_re", "s5_b_im", "s5_c_re", "s5_c_im", "s5_d", "s5_w_glu",
                 "cmp_pe_k", "cmp_pe_v", "cmp_w_k", "cmp_w_v", "rel_bias", "g_out_fox", "g_out_s5", "g_out_nsa", "g_pre_mix",
                 "g_post_mix", "g_pre_ffn", "g_post_ffn", "conv_w", "conv_b"):
        shared[name] = f32(inputs[name])
    for k, v in cst_np.items():
        shared["k_" + k] = v
    xs = np.asarray(inputs["x"], dtype=np.float32)
    in_maps = []
    for b in range(NCORES):
        m = dict(shared)
        m["x"] = np.ascontiguousarray(xs[b])
        in_maps.append(m)
    res = run_bass_kernel_spmd(nc, in_maps, core_ids=list(range(NCORES)))
    return np.stack([np.asarray(res.results[b]["y"], dtype=np.float32) for b in range(NCORES)], axis=0)
```
